# Optimizing a Trainium2 kernel written in Bass

```python
import math
import jax
import jax.numpy as jnp
from jax import lax
import numpy as np

D_MODEL = 2048
BATCH = 4
SEQ = 4096
DEPTH = 1

N_META = 16
MIX_WIDTH = D_MODEL
RW_WIDTH = MIX_WIDTH // 2
RW_HEAD = 64
RW_HEADS = RW_WIDTH // RW_HEAD
RW_LORA_W = 64
RW_LORA_A = 64
RW_GN_EPS = 64e-5
RW_SHIFT_COLS = 3 * RW_WIDTH + RW_LORA_W + RW_LORA_A
DN_WIDTH = MIX_WIDTH - RW_WIDTH
DN_HEAD = 128
DN_HEADS = DN_WIDTH // DN_HEAD
CONV_W = 4
CHUNK = 64
NORM_EPS = 1e-6
IN_COLS = RW_SHIFT_COLS + RW_WIDTH + 3 * DN_WIDTH + 2 * DN_HEADS + DN_WIDTH

kernel_name = "hymba_rwkv7_gated_deltanet_layer"


def rms_norm(x, w, eps=NORM_EPS):
    xf = x.astype(jnp.float32)
    y = xf * lax.rsqrt(jnp.mean(xf * xf, axis=-1, keepdims=True) + eps)
    return (y * w.astype(jnp.float32)).astype(x.dtype)


def l2_normalize(x, eps=1e-6):
    return x * lax.rsqrt(jnp.sum(x * x, axis=-1, keepdims=True) + eps)


def token_shift(p, mu):
    prev = jnp.pad(p, ((0, 0), (1, 0), (0, 0)))[:, :-1]
    return p + (prev - p) * mu


def rwkv7_mix(p_shift, gate, w0, w2, a0, a2, k_k, k_a, r_k, gn_w, gn_b):
    B, L, _ = p_shift.shape
    p_shift = p_shift.astype(jnp.float32)
    r, k, v, lw, la = jnp.split(
        p_shift, [RW_WIDTH, 2 * RW_WIDTH, 3 * RW_WIDTH, 3 * RW_WIDTH + RW_LORA_W], axis=-1)
    w_log = -jax.nn.softplus(-(w0 + jnp.tanh(lw) @ w2)) - 0.5
    decay = jnp.exp(-jnp.exp(w_log))
    a = jax.nn.sigmoid(a0 + la @ a2)
    hd = lambda t: t.reshape(B, L, RW_HEADS, RW_HEAD)
    kk = l2_normalize(hd(k * k_k))
    k = k * (1.0 + (a - 1.0) * k_a)
    rh, kh, vh, dh, ah = hd(r), hd(k), hd(v), hd(decay), hd(a)
    tm = lambda t: jnp.moveaxis(t, 1, 0)
    xs = (tm(rh), tm(dh), tm(kh), tm(vh), tm(kk), tm(kk * ah))

    def step(S, inp):
        r_t, w_t, k_t, v_t, kk_t, b_t = inp
        sa = jnp.einsum('bhvk,bhk->bhv', S, kk_t)
        S = (S * w_t[:, :, None, :] - sa[..., None] * b_t[:, :, None, :]
             + v_t[..., None] * k_t[:, :, None, :])
        y = jnp.einsum('bhvk,bhk->bhv', S, r_t)
        return S, y

    S0 = jnp.zeros((B, RW_HEADS, RW_HEAD, RW_HEAD), jnp.float32)
    _, y = lax.scan(step, S0, xs)
    y = jnp.moveaxis(y, 0, 1)
    mean = jnp.mean(y, axis=-1, keepdims=True)
    var = jnp.mean(jnp.square(y - mean), axis=-1, keepdims=True)
    y = ((y - mean) * lax.rsqrt(var + RW_GN_EPS)).reshape(B, L, RW_WIDTH) * gn_w + gn_b
    bonus = jnp.sum(rh * kh * hd(jnp.broadcast_to(r_k, r.shape)), axis=-1, keepdims=True) * vh
    y = y + bonus.reshape(B, L, RW_WIDTH)
    return y * jax.nn.silu(gate.astype(jnp.float32))


def causal_dwconv(u, w):
    C = u.shape[-1]
    return lax.conv_general_dilated(
        u, w[:, None, :], window_strides=(1,), padding=[(CONV_W - 1, 0)],
        dimension_numbers=('NWC', 'WIO', 'NWC'), feature_group_count=C)


def gated_delta_mix(qkv, b, alpha, z, conv_w, A_log, dt_bias, norm_w):
    B, L, _ = qkv.shape
    f32 = jnp.float32
    qkv = jax.nn.silu(causal_dwconv(qkv.astype(f32), conv_w.astype(f32)))
    q, k, v = jnp.split(qkv, [DN_WIDTH, 2 * DN_WIDTH], axis=-1)
    hd = lambda t: t.reshape(B, L, DN_HEADS, DN_HEAD)
    q = l2_normalize(hd(q)) * (DN_HEAD ** -0.5)
    k = l2_normalize(hd(k))
    v = hd(v)
    beta = jax.nn.sigmoid(b.astype(f32))
    g = -jnp.exp(A_log) * jax.nn.softplus(alpha.astype(f32) + dt_bias)

    pad_f = (-N_META) % CHUNK
    pad_b = (-(L + pad_f)) % CHUNK
    Lp = L + pad_f + pad_b
    Nc = Lp // CHUNK
    pad4 = ((0, 0), (pad_f, pad_b), (0, 0), (0, 0))
    pad3 = ((0, 0), (pad_f, pad_b), (0, 0))
    ch4 = lambda t: jnp.transpose(jnp.pad(t, pad4), (0, 2, 1, 3)).reshape(B, DN_HEADS, Nc, CHUNK, DN_HEAD)
    ch3 = lambda t: jnp.transpose(jnp.pad(t, pad3), (0, 2, 1)).reshape(B, DN_HEADS, Nc, CHUNK)
    q, k, v = ch4(q), ch4(k), ch4(v)
    beta, g = ch3(beta), ch3(g)

    g = jnp.cumsum(g, axis=-1)
    k_beta = k * beta[..., None]
    v_beta = v * beta[..., None]
    tri_incl = jnp.tril(jnp.ones((CHUNK, CHUNK), bool))
    tri_strict = jnp.tril(jnp.ones((CHUNK, CHUNK), bool), -1)
    decay_mask = jnp.exp(jnp.where(tri_incl, g[..., :, None] - g[..., None, :], -jnp.inf))
    M = jnp.where(tri_strict, jnp.einsum('bhncd,bhnsd->bhncs', k_beta, k) * decay_mask, 0.0)
    eye = jnp.eye(CHUNK, dtype=f32)
    T = lax.linalg.triangular_solve(M + eye, jnp.broadcast_to(eye, M.shape),
                                    left_side=True, lower=True, unit_diagonal=True)
    u = jnp.einsum('bhncs,bhnsd->bhncd', T, v_beta)
    w = jnp.einsum('bhncs,bhnsd->bhncd', T, k_beta * jnp.exp(g)[..., None])
    attn = jnp.where(tri_incl, jnp.einsum('bhncd,bhnsd->bhncs', q, k) * decay_mask, 0.0)

    def chunk_step(S, inp):
        q_c, k_c, u_c, w_c, g_c, a_c = inp
        v_new = u_c - jnp.einsum('bhck,bhkv->bhcv', w_c, S)
        o = (jnp.einsum('bhck,bhkv->bhcv', q_c * jnp.exp(g_c)[..., None], S)
             + jnp.einsum('bhcs,bhsv->bhcv', a_c, v_new))
        g_last = g_c[..., -1]
        S = (S * jnp.exp(g_last)[..., None, None]
             + jnp.einsum('bhck,bhcv->bhkv', k_c * jnp.exp(g_last[..., None] - g_c)[..., None], v_new))
        return S, o

    cm = lambda t: jnp.moveaxis(t, 2, 0)
    S0 = jnp.zeros((B, DN_HEADS, DN_HEAD, DN_HEAD), f32)
    _, o = lax.scan(chunk_step, S0, (cm(q), cm(k), cm(u), cm(w), cm(g), cm(attn)))
    o = jnp.moveaxis(o, 0, 2).reshape(B, DN_HEADS, Lp, DN_HEAD)
    o = jnp.transpose(o, (0, 2, 1, 3))[:, pad_f:pad_f + L]
    o = o * lax.rsqrt(jnp.mean(o * o, axis=-1, keepdims=True) + NORM_EPS) * norm_w
    o = o * jax.nn.silu(hd(z.astype(f32)))
    return o.reshape(B, L, DN_WIDTH)


def hybrid_layer(h, norm_w, w_in, shift_mu, rw_w0, rw_w2, rw_a0, rw_a2, rw_k_k, rw_k_a,
                 rw_r_k, rw_gn_w, rw_gn_b, dn_conv_w, dn_A_log, dn_dt_bias, dn_norm_w, w_out):
    u = rms_norm(h, norm_w)
    p = u @ w_in
    s1 = RW_SHIFT_COLS
    s2 = s1 + RW_WIDTH
    s3 = s2 + 3 * DN_WIDTH
    s4 = s3 + DN_HEADS
    s5 = s4 + DN_HEADS
    rw_p, rw_gate, dn_qkv, dn_b, dn_a, dn_z = jnp.split(p, [s1, s2, s3, s4, s5], axis=-1)
    y_a = rwkv7_mix(token_shift(rw_p.astype(jnp.float32), shift_mu), rw_gate, rw_w0, rw_w2,
                    rw_a0, rw_a2, rw_k_k, rw_k_a, rw_r_k, rw_gn_w, rw_gn_b)
    y_b = gated_delta_mix(dn_qkv, dn_b, dn_a, dn_z, dn_conv_w, dn_A_log, dn_dt_bias, dn_norm_w)
    y = jnp.concatenate([y_a, y_b], axis=-1).astype(h.dtype)
    return h + y @ w_out


def setup_inputs(seed: int = 0) -> dict:
    key = jax.random.key(seed)
    ks = jax.random.split(key, 20)
    f32 = jnp.float32
    nrm = lambda k, shape, s: s * jax.random.normal(k, shape, f32)
    x = nrm(ks[0], (BATCH, SEQ, D_MODEL), 1.0)
    meta_tokens = nrm(ks[1], (N_META, D_MODEL), 1.0)
    norm_w = 1.0 + nrm(ks[2], (DEPTH, D_MODEL), 0.02)
    w_in = nrm(ks[3], (DEPTH, D_MODEL, IN_COLS), D_MODEL ** -0.5)
    rw_shift_mu = jax.random.uniform(ks[4], (DEPTH, RW_SHIFT_COLS), f32)
    rw_w0 = jax.random.uniform(ks[5], (DEPTH, RW_WIDTH), f32, -6.0, 1.0)
    rw_w2 = nrm(ks[6], (DEPTH, RW_LORA_W, RW_WIDTH), 0.3 * RW_LORA_W ** -0.5)
    rw_a0 = nrm(ks[7], (DEPTH, RW_WIDTH), 0.1)
    rw_a2 = nrm(ks[8], (DEPTH, RW_LORA_A, RW_WIDTH), 0.3 * RW_LORA_A ** -0.5)
    rw_k_k = 0.85 + nrm(ks[9], (DEPTH, RW_WIDTH), 0.05)
    rw_k_a = 1.0 + nrm(ks[10], (DEPTH, RW_WIDTH), 0.05)
    rw_r_k = nrm(ks[11], (DEPTH, RW_WIDTH), 0.1)
    rw_gn_w = 1.0 + nrm(ks[12], (DEPTH, RW_WIDTH), 0.02)
    rw_gn_b = nrm(ks[13], (DEPTH, RW_WIDTH), 0.02)
    dn_conv_w = nrm(ks[14], (DEPTH, CONV_W, 3 * DN_WIDTH), CONV_W ** -0.5)
    dn_A_log = jnp.log(jax.random.uniform(ks[15], (DEPTH, DN_HEADS), f32, 1.0, 16.0))
    dt = jnp.exp(jax.random.uniform(ks[16], (DEPTH, DN_HEADS), f32, math.log(1e-3), math.log(1e-1)))
    dn_dt_bias = dt + jnp.log(-jnp.expm1(-dt))
    dn_norm_w = 1.0 + nrm(ks[17], (DEPTH, DN_HEAD), 0.02)
    w_out = nrm(ks[18], (DEPTH, MIX_WIDTH, D_MODEL), MIX_WIDTH ** -0.5)
    final_norm_w = 1.0 + nrm(ks[19], (D_MODEL,), 0.02)
    return {"x": x, "meta_tokens": meta_tokens, "norm_w": norm_w, "w_in": w_in,
            "rw_shift_mu": rw_shift_mu, "rw_w0": rw_w0, "rw_w2": rw_w2, "rw_a0": rw_a0,
            "rw_a2": rw_a2, "rw_k_k": rw_k_k, "rw_k_a": rw_k_a, "rw_r_k": rw_r_k,
            "rw_gn_w": rw_gn_w, "rw_gn_b": rw_gn_b, "dn_conv_w": dn_conv_w,
            "dn_A_log": dn_A_log, "dn_dt_bias": dn_dt_bias, "dn_norm_w": dn_norm_w,
            "w_out": w_out, "final_norm_w": final_norm_w}


def reference(x, meta_tokens, norm_w, w_in, rw_shift_mu, rw_w0, rw_w2, rw_a0, rw_a2,
              rw_k_k, rw_k_a, rw_r_k, rw_gn_w, rw_gn_b, dn_conv_w, dn_A_log, dn_dt_bias,
              dn_norm_w, w_out, final_norm_w):
    B = x.shape[0]
    meta = jnp.broadcast_to(meta_tokens.astype(x.dtype)[None], (B, N_META, x.shape[-1]))
    h = jnp.concatenate([meta, x], axis=1)
    for l in range(DEPTH):
        h = hybrid_layer(h, norm_w[l], w_in[l], rw_shift_mu[l], rw_w0[l], rw_w2[l], rw_a0[l],
                         rw_a2[l], rw_k_k[l], rw_k_a[l], rw_r_k[l], rw_gn_w[l], rw_gn_b[l],
                         dn_conv_w[l], dn_A_log[l], dn_dt_bias[l], dn_norm_w[l], w_out[l])
    h = rms_norm(h, final_norm_w)
    return h[:, N_META:]
```

```python
import contextlib
import numpy as np
import ml_dtypes
import concourse.bass as bass
import concourse.mybir as mybir
from concourse.bass_utils import run_bass_kernel_spmd

F32 = mybir.dt.float32
BF16 = mybir.dt.bfloat16
AF = mybir.ActivationFunctionType
ALU = mybir.AluOpType
AX = mybir.AxisListType

D = 2048
SEQ = 4096
NMETA = 16
NCH = 34
NCOL = 33 * 128 + 36
LASTM = 36
NPV = 92
CW = 0.6065306597126334
C_ID, C_MSLN4, C_MA, C_MB, C_BD1, C_ONES, C_SEL, C_NMSU4, C_MIU4, C_END = (
    0, 128, 640, 1664, 2688, 2816, 2944, 3968, 4480, 4992)


class Tile:
    def __init__(self, t, ntok=1):
        self.t = t
        self.k = [Tok() for _ in range(ntok)]

    def __getitem__(self, idx):
        return self.t[idx]


class Tok:
    __slots__ = ("w", "rs")

    def __init__(self):
        self.w = {}
        self.rs = {}


class Eng:
    def __init__(self, h, sem, is_pe=False):
        self.h = h
        self.sem = sem
        self.n = 0
        self.known = {}
        self.is_pe = is_pe


def _toks(lst):
    out = []
    for x in lst:
        if isinstance(x, Tile):
            out.extend(x.k)
        elif isinstance(x, Tok):
            out.append(x)
        else:
            t, i = x
            out.append(t.k[i])
    return out


class Sch:
    NS = 16

    def __init__(self, nc, es):
        self.nc = nc
        sem = lambda n: es.enter_context(nc.semaphore(n))
        self.P = Eng(nc.tensor, sem("s_pe"), True)
        self.V = Eng(nc.vector, sem("s_dve"))
        self.A = Eng(nc.scalar, sem("s_act"))
        self.G = Eng(nc.gpsimd, sem("s_pool"))
        self.Q = Eng(nc.sync, sem("s_sp"))
        self.engs = [self.P, self.V, self.A, self.G, self.Q]
        self.dsems = [sem("s_dma%d" % i) for i in range(self.NS)]
        self.dcnt = [0] * self.NS
        self.dn = 0
        self.nops = 0
        self.maxops = 10 ** 9
        self.marks = []
        self.rec = None

    def _wait(self, eng, sem, val):
        k = id(sem)
        if eng.known.get(k, 0) >= val:
            return
        eng.h.wait_ge(sem, val)
        eng.known[k] = val

    def _deps(self, eng, reads, writes, is_dma):
        need = {}

        def add(rec, raw):
            sem, val, src = rec
            if (not is_dma) and src is eng:
                if eng.is_pe or not raw:
                    return
            k = id(sem)
            if k not in need or need[k][1] < val:
                need[k] = (sem, val)

        for t in reads:
            for rec in t.w.values():
                add(rec, True)
        for t in writes:
            for rec in t.w.values():
                add(rec, False)
            for rec in t.rs.values():
                add(rec, False)
        for sem, val in need.values():
            self._wait(eng, sem, val)

    def _record(self, rec, reads, writes):
        k = id(rec[0])
        for t in reads:
            t.rs[k] = rec
        for t in writes:
            t.w[k] = rec
            t.rs = {}

    def mark(self, name):
        self.marks.append((name, self.nops))

    def begin(self):
        self.rec = []
        return self.rec

    def end(self):
        self.rec = None

    def cut(self):
        if self.rec is not None:
            self.rec.append(None)

    def play(self, *lists):
        its = [list(l) for l in lists]
        pos = [0] * len(its)
        live = True
        while live:
            live = False
            for i, l in enumerate(its):
                while pos[i] < len(l):
                    item = l[pos[i]]
                    pos[i] += 1
                    if item is None:
                        break
                    if item[0] == "op":
                        self.op(*item[1:])
                    else:
                        self.dma(*item[1:])
                if pos[i] < len(l):
                    live = True

    def op(self, eng, fn, R=(), W=()):
        if self.rec is not None:
            self.rec.append(("op", eng, fn, R, W))
            return
        self.nops += 1
        if self.nops > self.maxops:
            return
        reads = _toks(R)
        writes = _toks(W)
        self._deps(eng, reads, writes, False)
        inst = fn()
        eng.n += 1
        inst.then_inc(eng.sem, 1)
        self._record((eng.sem, eng.n, eng), reads, writes)

    def dma(self, out, in_, R=(), W=()):
        if self.rec is not None:
            self.rec.append(("dma", out, in_, R, W))
            return
        self.nops += 1
        if self.nops > self.maxops:
            return
        reads = _toks(R)
        writes = _toks(W)
        q = self.Q
        self._deps(q, reads, writes, True)
        i = self.dn % self.NS
        self.dn += 1
        sem = self.dsems[i]
        if self.dcnt[i] > 0:
            self._wait(q, sem, 16 * self.dcnt[i])
        self.dcnt[i] += 1
        q.h.dma_start(out=out, in_=in_).then_inc(sem, 16)
        self._record((sem, 16 * self.dcnt[i], None), reads, writes)

    def barrier(self):
        for e in self.engs:
            for f in self.engs:
                if f is not e and f.n > 0:
                    self._wait(e, f.sem, f.n)
            for i in range(self.NS):
                if self.dcnt[i] > 0:
                    self._wait(e, self.dsems[i], 16 * self.dcnt[i])

    def finish(self):
        q = self.Q
        for f in self.engs:
            if f is not q and f.n > 0:
                self._wait(q, f.sem, f.n)
        for i in range(self.NS):
            if self.dcnt[i] > 0:
                self._wait(q, self.dsems[i], 16 * self.dcnt[i])

    def mm(self, out, lhsT, rhs, start, stop, R, W):
        self.op(self.P, lambda: self.nc.tensor.matmul(out, lhsT=lhsT, rhs=rhs, start=start, stop=stop), R, W)

    def tr(self, out, in_, ident, R, W):
        self.op(self.P, lambda: self.nc.tensor.transpose(out, in_, ident), R, W)

    def act(self, out, in_, func, R, W, bias=0.0, scale=1.0, accum_out=None):
        if accum_out is None:
            self.op(self.A, lambda: self.nc.scalar.activation(out=out, in_=in_, func=func, bias=bias, scale=scale), R, W)
        else:
            self.op(self.A, lambda: self.nc.scalar.activation(out=out, in_=in_, func=func, bias=bias, scale=scale,
                                                              accum_out=accum_out), R, W)

    def tt(self, e, out, in0, in1, op, R, W):
        self.op(e, lambda: e.h.tensor_tensor(out=out, in0=in0, in1=in1, op=op), R, W)

    def ts(self, e, out, in0, s1, op0, R, W, s2=None, op1=None):
        if op1 is None:
            self.op(e, lambda: e.h.tensor_scalar(out=out, in0=in0, scalar1=s1, scalar2=None, op0=op0), R, W)
        else:
            self.op(e, lambda: e.h.tensor_scalar(out=out, in0=in0, scalar1=s1, scalar2=s2, op0=op0, op1=op1), R, W)

    def stt(self, e, out, in0, scalar, in1, op0, op1, R, W):
        self.op(e, lambda: e.h.scalar_tensor_tensor(out=out, in0=in0, scalar=scalar, in1=in1, op0=op0, op1=op1), R, W)

    def cp(self, e, out, in_, R, W):
        if e is self.A:
            self.op(e, lambda: self.nc.scalar.activation(out=out, in_=in_, func=AF.Copy), R, W)
        else:
            self.op(e, lambda: e.h.tensor_copy(out=out, in_=in_), R, W)

    def ms(self, e, ap, val, W):
        self.op(e, lambda: e.h.memset(ap, val), (), W)


def build(NT, NH, stop=9, maxops=10 ** 9, dbg=False, coll=False):
    T = NT * 128
    NX = (NT - 1) * 128
    nc = bass.Bass("TRN2", target_bir_lowering=False)
    dt = lambda name, shape, dtype, kind: nc.dram_tensor(name, shape, dtype, kind=kind).ap()
    x_d = dt("x", [NX, D], F32, "ExternalInput")
    meta_d = dt("meta", [NMETA, D], F32, "ExternalInput")
    nw_d = dt("normw", [128, D], F32, "ExternalInput")
    fnw_d = dt("fnormw", [128, D], F32, "ExternalInput")
    win_d = dt("w_in_l", [NH, D, NCOL], F32, "ExternalInput")
    pv_d = dt("pv", [NH, 128, NPV], F32, "ExternalInput")
    w2a2_d = dt("w2a2", [NH, 128, 2, 512], F32, "ExternalInput")
    NKH = 2 if coll else NH
    wout_d = dt("w_out_l", [NKH * 1024, D], F32, "ExternalInput")
    cst_d = dt("consts", [128, C_END], F32, "ExternalInput")
    out_d = dt("out", [NX, D], F32, "ExternalOutput")
    uT_d = dt("uT_s", [16, 128, T], BF16, "Internal")
    pT_d = dt("pT_s", [NH, NCH * 128, T], F32, "Internal")
    yT_t = nc.dram_tensor("yT_s", [NH * 8 * 128, T], BF16, kind="ExternalOutput" if dbg else "Internal")
    yT_d = yT_t.ap().rearrange("(k p) t -> k p t", p=128)
    if coll:
        yTk = [nc.dram_tensor("yT_k%d" % k, [128, T], BF16, kind="Internal") for k in range(8)]
        yAk = [nc.dram_tensor("yA_k%d" % k, [2 * 128, T], BF16, kind="Internal") for k in range(8)]
    else:
        yA_d = yT_d

    with contextlib.ExitStack() as es:
        S = Sch(nc, es)
        S.maxops = maxops
        P, V, A, G = S.P, S.V, S.A, S.G

        def sb(stack, name, shape, dtype, ntok=1):
            return Tile(stack.enter_context(nc.sbuf_tensor("sb_" + name, shape, dtype)), ntok)

        PB = [Tile(es.enter_context(nc.psum_tensor("pb%d" % i, [128, 512], F32))) for i in range(7)]
        PSB = Tile(es.enter_context(nc.psum_tensor("psb", [128, 1024], BF16)), 2)

        CST = sb(es, "cst", [128, C_END], F32)
        IDB = sb(es, "idb", [128, 128], BF16)
        S.dma(CST[:], cst_d[:, :], (), [CST])
        S.cp(V, IDB[:], CST[:, C_ID:C_ID + 128], [CST], [IDB])
        IDF = CST[:, C_ID:C_ID + 128]
        S.barrier()
        KC = ()

        with contextlib.ExitStack() as ph:
            NWR = sb(ph, "nwr", [128, D], F32)
            S.dma(NWR[:], nw_d[:, :], (), [NWR])
            XT = [sb(ph, "xt%d" % i, [128, D], F32) for i in range(2)]
            UNB = [sb(ph, "unb%d" % i, [128, D], BF16) for i in range(2)]
            UTS = [sb(ph, "uts%d" % i, [128, 16, 128], BF16) for i in range(2)]
            ST1 = [sb(ph, "st1_%d" % i, [128, 4], F32) for i in range(2)]
            for i in range(NT):
                xt, unb, uts, st1 = XT[i % 2], UNB[i % 2], UTS[i % 2], ST1[i % 2]
                if i == 0:
                    S.ms(G, xt[:], 0.0, [xt])
                    S.dma(xt[112:128, :], meta_d[:, :], (), [xt])
                else:
                    S.dma(xt[:], x_d[(i - 1) * 128:i * 128, :], (), [xt])
                S.ms(V, st1[:, 0:1], 0.0, [st1])
                S.act(unb[:], xt[:], AF.Square, [xt], [unb, st1], accum_out=st1[:, 0:1])
                S.act(st1[:, 1:2], st1[:, 0:1], AF.Sqrt, [st1], [st1], bias=1e-6, scale=1.0 / D)
                S.op(V, lambda: nc.vector.reciprocal(out=st1[:, 2:3], in_=st1[:, 1:2]), [st1], [st1])
                S.stt(V, unb[:], xt[:], st1[:, 2:3], NWR[:], ALU.mult, ALU.mult, [xt, st1, NWR], [unb])
                for r in range(2):
                    for j in range(8):
                        kc = r * 8 + j
                        S.tr(PSB[:, j * 128:(j + 1) * 128], unb[:, kc * 128:(kc + 1) * 128], IDB[:], [unb], [PSB])
                    S.cp(A if r == 0 else V, uts[:, r * 8:(r + 1) * 8, :],
                         PSB[:, :].rearrange("p (a b) -> p a b", a=8), [PSB], [uts])
                S.dma(uT_d[:, :, i * 128:(i + 1) * 128].rearrange("k p t -> p k t"), uts[:], [uts], ())
        S.barrier()

        passes = [(0, 17), (17, NCH)] if stop >= 1 else []
        with contextlib.ExitStack() as ph:
            WBFs = [sb(ph, "wbf%d" % i, [128, 16, 17 * 128], BF16) for i in range(2)]
            WST = [sb(ph, "wst%d" % i, [128, 17 * 128], F32) for i in range(1)]
            UTT = [sb(ph, "utt%d" % i, [128, 16, 512], BF16) for i in range(2)]
            OST = [sb(ph, "ost%d" % i, [128, 512], F32) for i in range(4)]
            cnt = 0
            ocnt = 0
            plist = [(hf, ca, cb) for hf in range(NH) for (ca, cb) in passes]

            def load_w(pi):
                hf, ca, cb = plist[pi]
                c0 = ca * 128
                ncols = min(cb * 128, NCOL) - c0
                wbf = WBFs[pi % 2]
                for kc in range(16):
                    wst = WST[0]
                    S.dma(wst[:, 0:ncols], win_d[hf, kc * 128:(kc + 1) * 128, c0:c0 + ncols], (), [wst])
                    S.cp(V if kc % 2 == 0 else A, wbf[:, kc, 0:ncols], wst[:, 0:ncols], [wst], [wbf])

            if plist:
                load_w(0)
            for pi, (hf, ca, cb) in enumerate(plist):
                WBF = WBFs[pi % 2]
                for t0 in range(0, T, 512):
                    n = min(512, T - t0)
                    utt = UTT[cnt % 2]
                    cnt += 1
                    S.dma(utt[:, :, 0:n], uT_d[:, :, t0:t0 + n].rearrange("k p t -> p k t"), (), [utt])
                    if t0 == (512 if T > 512 else 0) and pi + 1 < len(plist):
                        load_w(pi + 1)
                    for cc in range(ca, cb):
                        m = 128 if cc < NCH - 1 else LASTM
                        pb = PB[ocnt % 4]
                        ost = OST[ocnt % 4]
                        for kc in range(16):
                            S.mm(pb[0:m, 0:n], WBF[:, kc, (cc - ca) * 128:(cc - ca) * 128 + m], utt[:, kc, 0:n],
                                 kc == 0, kc == 15, [WBF, utt], [pb])
                        S.cp(A if ocnt % 2 == 0 else V, ost[0:m, 0:n], pb[0:m, 0:n], [pb], [ost])
                        S.dma(pT_d[hf, cc * 128:cc * 128 + m, t0:t0 + n], ost[0:m, 0:n], [ost], ())
                        ocnt += 1
        S.barrier()

        with contextlib.ExitStack() as ph:
            def t_(name, shape, dtype=F32, ntok=1):
                return sb(ph, name, shape, dtype, ntok)

            PV = t_("pv", [128, NPV])
            OMM = t_("omm", [128, 13])
            MUF = t_("muf", [128, 13, 128])
            KKF = t_("kkf", [128, 4, 128])
            KAF = t_("kaf", [128, 4, 128])
            OMKAF = t_("omkaf", [128, 4, 128])
            OMKA = t_("omka", [128, 4])
            NA = t_("na", [64, 1])
            W2b = t_("w2b", [128, 2, 512], BF16)
            ONE4 = t_("one4", [128, 128])
            STG = [t_("stg%d" % i, [128, NCH, 131]) for i in range(2)]
            TMP = t_("tmp", [128, 13, 128], F32, 13)
            W2f = TMP
            XS = t_("xs", [128, 13, 128], F32, 13)
            TL = t_("tl", [128, 128], BF16)
            SG = t_("sg", [128, 4, 128])
            AA = t_("aa", [128, 4, 128])
            CUM = t_("cum", [128, 4, 128])
            CME = t_("cme", [128, 4, 128])
            BMs = [t_("bm%d" % i, [128, 16]) for i in range(2)]
            PIN = t_("pin", [128, 4, 128])
            PEX = t_("pex", [128, 4, 128])
            IP = t_("ip", [128, 4, 128])
            KRAW = t_("kraw", [128, 4, 128])
            SQ = t_("sq", [128, 4, 128])
            RN = t_("rn", [128, 4, 128])
            KK = t_("kk", [128, 4, 128])
            KF = t_("kf", [128, 4, 128])
            KP = t_("kp", [128, 4, 128])
            BB = t_("bb", [128, 4, 128])
            FMQs = [t_("fmq%d" % i, [128, 4, 2, 128], BF16) for i in range(2)]
            KDF = t_("kdf", [128, 4, 128])
            BDF = t_("bdf", [128, 4, 128])
            KDZ = t_("kdz", [128, 4, 2, 128], BF16)
            BDZ = t_("bdz", [128, 4, 2, 128], BF16)
            H2Z = t_("h2z", [128, 4, 2, 64], BF16)
            KEB = t_("keb", [128, 4, 128], BF16)
            BEB_ = t_("beb", [128, 4, 128], BF16)
            VB_ = t_("vb", [128, 4, 128], BF16)
            KETs = [t_("ket%d" % i, [128, 4, 128], BF16) for i in range(2)]
            BETs = [t_("bet%d" % i, [128, 4, 128], BF16) for i in range(2)]
            VTs = [t_("vt%d" % i, [128, 4, 128], BF16) for i in range(2)]
            RKR = KRAW
            BONs = [t_("bon%d" % i, [128, 4, 128]) for i in range(2)]
            SGTs = [t_("sgt%d" % i, [128, 4, 128]) for i in range(2)]
            XB = [t_("xb%d" % i, [128, 4, 128], BF16) for i in range(2)]
            XTB = [t_("xtb%d" % i, [128, 4, 128], BF16) for i in range(2)]
            RT = [t_("rt%d" % i, [128, 4, 128], BF16) for i in range(2)]
            XTARBs = [t_("xtarb%d" % i, [128, 4, 2, 128], BF16) for i in range(2)]
            AKKARKs = [t_("akkark%d" % i, [128, 4, 2, 128], BF16) for i in range(2)]
            XBD = t_("xbd", [128, 4, 128], BF16)
            XTBD = t_("xtbd", [128, 4, 128], BF16)
            RTD = t_("rtd", [128, 4, 128], BF16)
            H2F = t_("h2f", [128, 4, 64])
            ZBs = [t_("zb%d" % i, [128, 4, 64], BF16) for i in range(2)]
            NSAs = [t_("nsa%d" % i, [128, 4, 64], BF16) for i in range(2)]
            YSB = t_("ysb", [128, 8, 64])
            YSQ = t_("ysq", [128, 8, 64])
            GST = t_("gst", [128, 48])
            YN = YSQ
            Y2 = t_("y2", [128, 4, 128])
            YOB = t_("yob", [128, 8, 128], BF16)
            CV = TMP
            TMPG = t_("tmpg", [128, 128])
            QKV = XS
            SQD = t_("sqd", [128, 8, 128])
            RND = SQD
            QTB = t_("qtb", [128, 4, 128], BF16)
            KTB = t_("ktb", [128, 4, 128], BF16)
            VTB = t_("vtb", [128, 4, 128], BF16)
            GB = t_("gb", [128, 128])
            GE = t_("ge", [64, 128])
            TM = t_("tm", [128, 64])
            TMS = t_("tms", [128, 24])
            GCB = SG
            BEBC = AA
            EGB = t_("egb", [128, 4, 128])
            DD = PEX
            D1 = IP
            D2 = KRAW
            EE = D1
            ET = D2
            XE = D1
            XTE = SQ
            AE = D2
            ATB = t_("atb", [128, 4, 128], BF16)
            XH = t_("xh", [128, 4, 128], BF16)
            XL = t_("xl", [128, 4, 128], BF16)
            RRT = t_("rrt", [128, 4, 128], BF16)
            RRN = t_("rrn", [128, 4, 128], BF16)
            XF = t_("xf", [128, 4, 128])
            VBD = t_("vbd", [128, 4, 128], BF16)
            KBG = t_("kbg", [128, 4, 128], BF16)
            KE = t_("ke", [128, 4, 128], BF16)
            QGT = t_("qgt", [128, 4, 128], BF16)
            USB = t_("usb", [128, 4, 128])
            WTB = t_("wtb", [128, 4, 128], BF16)
            VN = t_("vn", [128, 4, 128], BF16)
            SF = t_("sf", [128, 4, 128])
            SBF = t_("sbf", [128, 4, 128], BF16)
            OSQ = t_("osq", [128, 4, 128])
            ON = t_("on", [128, 4, 128])
            SZ = t_("sz", [128, 4, 128])

            MSLN4 = CST[:, C_MSLN4:C_MSLN4 + 512].rearrange("p (a b) -> p a b", a=4)
            MA = CST[:, C_MA:C_MA + 1024].rearrange("p (a b c) -> p a b c", a=4, b=2)
            MB = CST[:, C_MB:C_MB + 1024].rearrange("p (a b c) -> p a b c", a=4, b=2)
            BD1 = CST[:, C_BD1:C_BD1 + 128]
            ONES = CST[:, C_ONES:C_ONES + 128]
            NMSU4 = CST[:, C_NMSU4:C_NMSU4 + 512].rearrange("p (a b) -> p a b", a=4)
            MIU4 = CST[:, C_MIU4:C_MIU4 + 512].rearrange("p (a b) -> p a b", a=4)
            ID4 = t_("id4", [128, 4, 128], BF16)
            for a in range(4):
                S.cp(V, ID4[:, a, :], IDF, KC, [ID4])
            S.ms(V, ONE4[:], 1.0, [ONE4])
            S.ms(G, KDZ[:], 0.0, [KDZ])
            S.ms(G, BDZ[:], 0.0, [BDZ])
            S.ms(G, H2Z[:], 0.0, [H2Z])
            S.ms(G, GB[:], 0.0, [GB])
            for i in range(2):
                S.ms(G, STG[i][:, 33, :], 0.0, [STG[i]])

            def b4(pb):
                return pb[:, :].rearrange("p (a b) -> p a b", a=4)

            def doubling(xb, xtb, rt, bS, bY, bYT, nlev=7):
                S.tt(G, rt[:], xtb[:], ID4[:], ALU.add, [xtb, ID4], [rt])
                for lvl in range(nlev):
                    last = lvl == nlev - 1
                    if lvl >= 1:
                        for h in range(4):
                            S.mm(bS[:, h * 128:(h + 1) * 128], IDB[:], rt[:, h, :], True, False, [rt], [bS])
                            S.mm(bS[:, h * 128:(h + 1) * 128], xb[:, h, :], rt[:, h, :], False, True, [xb, rt], [bS])
                    if not last:
                        for h in range(4):
                            S.mm(bY[:, h * 128:(h + 1) * 128], xtb[:, h, :], xb[:, h, :], True, True, [xtb, xb], [bY])
                        if lvl < nlev - 2:
                            for h in range(4):
                                S.mm(bYT[:, h * 128:(h + 1) * 128], xb[:, h, :], xtb[:, h, :], True, True, [xtb, xb], [bYT])
                    S.cut()
                    if lvl >= 1:
                        S.cp(A if lvl % 2 == 0 else V, rt[:], b4(bS), [bS], [rt])
                    if not last:
                        S.cp(V if lvl % 2 == 0 else A, xb[:], b4(bY), [bY], [xb])
                        if lvl < nlev - 2:
                            S.cp(A, xtb[:], b4(bYT), [bYT], [xtb])
                    S.cut()

            for hf in range(NH if stop >= 2 else 0):
                S.dma(PV[:], pv_d[hf, :, :], (), [PV])
                S.dma(W2f[:, 0:8, :].rearrange("p (a b) c -> p a (b c)", a=2), w2a2_d[hf, :, :, :], (), [W2f])
                S.cp(V, W2b[:], W2f[:, 0:8, :].rearrange("p (a b) c -> p a (b c)", a=2), [W2f], [W2b])
                S.ts(V, OMM[:], PV[:, 0:13], -1.0, ALU.mult, [PV], [OMM], 1.0, ALU.add)
                S.ts(V, OMKA[:], PV[:, 25:29], -1.0, ALU.mult, [PV], [OMKA], 1.0, ALU.add)
                S.act(NA[:], PV[0:64, 90:91], AF.Exp, [PV], [NA])
                S.ts(V, NA[:], NA[:], -1.0, ALU.mult, [NA], [NA])
                for cc in range(13):
                    S.ts(V, MUF[:, cc, :], ONE4[:], PV[:, cc:cc + 1], ALU.mult, [ONE4, PV], [MUF])
                for g in range(4):
                    S.ts(V, KKF[:, g, :], ONE4[:], PV[:, 21 + g:22 + g], ALU.mult, [ONE4, PV], [KKF])
                    S.ts(V, KAF[:, g, :], ONE4[:], PV[:, 25 + g:26 + g], ALU.mult, [ONE4, PV], [KAF])
                    S.ts(V, OMKAF[:, g, :], ONE4[:], OMKA[:, g:g + 1], ALU.mult, [ONE4, OMKA], [OMKAF])
                S.ms(V, H2F[:], 0.0, [H2F])
                S.ms(V, SF[:], 0.0, [SF])
                S.ms(G, SBF[:], 0.0, [SBF])
                MU = lambda cc: PV[:, cc:cc + 1]
                W0 = lambda g: PV[:, 13 + g:14 + g]
                A0 = lambda g: PV[:, 17 + g:18 + g]
                K_K = lambda g: PV[:, 21 + g:22 + g]
                K_A = lambda g: PV[:, 25 + g:26 + g]
                R_K = lambda g: PV[:, 29 + g:30 + g]
                GNW = lambda g: PV[:, 33 + g:34 + g]
                GNB = lambda g: PV[:, 37 + g:38 + g]
                CWT = lambda cj, tap: PV[:, 41 + 4 * cj + tap:42 + 4 * cj + tap]
                DNW = PV[:, 89:90]

                def load_chunk(c):
                    stg = STG[c % 2]
                    t0 = c * 128
                    src = pT_d[hf].rearrange("(cc p) t -> p cc t", p=128)
                    if c == 0:
                        S.ms(G, stg[:, :, 0:3], 0.0, [stg])
                        lo, dst0 = 0, 3
                    else:
                        lo, dst0 = t0 - 3, 0
                    for (a, b) in ((0, 9), (9, 17), (17, 25), (25, 33)):
                        S.dma(stg[:, a:b, dst0:131], src[:, a:b, lo:t0 + 128], (), [stg])
                    S.dma(stg[0:LASTM, 33, dst0:131], pT_d[hf, 33 * 128:33 * 128 + LASTM, lo:t0 + 128], (), [stg])

                def rw_pre(ci):
                    stg = STG[ci % 2]
                    BON = BONs[ci % 2]
                    SGT = SGTs[ci % 2]
                    BM, FMQ, KET, BET, VT = BMs[ci % 2], FMQs[ci % 2], KETs[ci % 2], BETs[ci % 2], VTs[ci % 2]
                    S.tt(G, TMP[:], stg[:, 0:13, 2:130], stg[:, 0:13, 3:131], ALU.subtract, [stg], [TMP])
                    S.tt(G, TMP[:], TMP[:], MUF[:], ALU.mult, [TMP, MUF], [TMP])
                    S.tt(V, XS[:], stg[:, 0:13, 3:131], TMP[:], ALU.add, [stg, TMP], [XS])
                    XR = XS[:, 0:4, :]
                    XK = XS[:, 4:8, :]
                    XV = XS[:, 8:12, :]
                    xr_t = [(XS, i) for i in range(0, 4)]
                    xk_t = [(XS, i) for i in range(4, 8)]
                    xv_t = [(XS, i) for i in range(8, 12)]
                    S.cut()
                    S.act(TL[0:64, :], XS[0:64, 12, :], AF.Tanh, [(XS, 12)], [TL])
                    S.cp(A, TL[64:128, :], XS[64:128, 12, :], [(XS, 12)], [TL])
                    for g in range(4):
                        S.mm(PB[6][:, g * 128:(g + 1) * 128], W2b[:, 0, g * 128:(g + 1) * 128], TL[:, :], True, True,
                             [W2b, TL], [PB[6]])
                    S.cut()
                    for g in range(4):
                        S.act(SG[:, g, :], PB[6][:, g * 128:(g + 1) * 128], AF.Sigmoid, [PB[6], PV], [SG], bias=W0(g))
                    for g in range(4):
                        S.mm(PB[6][:, g * 128:(g + 1) * 128], W2b[:, 1, g * 128:(g + 1) * 128], TL[:, :], True, True,
                             [W2b, TL], [PB[6]])
                    for g in range(4):
                        S.act(AA[:, g, :], PB[6][:, g * 128:(g + 1) * 128], AF.Sigmoid, [PB[6], PV], [AA], bias=A0(g))
                    S.cut()
                    for g in range(4):
                        S.op(V, lambda g=g: nc.vector.tensor_tensor_scan(out=CUM[:, g, :], data0=ONE4[:], data1=SG[:, g, :],
                                                                        initial=0.0, op0=ALU.mult, op1=ALU.add),
                             [SG, ONE4], [CUM])
                    S.ts(V, BM[:, 0:4], CUM[:, :, 63], CW, ALU.mult, [CUM], [BM])
                    S.ts(V, BM[:, 4:8], CUM[:, :, 63], -CW, ALU.mult, [CUM], [BM])
                    S.tt(V, CME[:], CUM[:], SG[:], ALU.subtract, [CUM, SG], [CME])
                    S.cut()
                    for g in range(4):
                        S.act(PIN[:, g, :], CUM[:, g, :], AF.Exp, [CUM, BM], [PIN], bias=BM[:, g:g + 1], scale=-CW)
                    for g in range(4):
                        S.act(IP[:, g, :], CUM[:, g, :], AF.Exp, [CUM, BM], [IP], bias=BM[:, 4 + g:5 + g], scale=CW)
                    S.cut()
                    for g in range(4):
                        S.act(PEX[:, g, :], CME[:, g, :], AF.Exp, [CME, BM], [PEX], bias=BM[:, g:g + 1], scale=-CW)
                    S.act(BM[:, 8:12], BM[:, 0:4], AF.Exp, [BM], [BM], scale=-1.0)
                    S.tt(V, BM[:, 12:16], BM[:, 8:12], PIN[:, :, 127], ALU.mult, [BM, PIN], [BM])
                    S.cut()
                    S.tt(G, KRAW[:], XK, KKF[:], ALU.mult, xk_t + [KKF], [KRAW])
                    S.tt(G, SQ[:], KRAW[:], KRAW[:], ALU.mult, [KRAW], [SQ])
                    S.mm(PB[6][:, :], BD1, SQ[:, :, :].rearrange("p a b -> p (a b)"), True, True, [SQ], [PB[6]])
                    S.act(RN[:], b4(PB[6]), AF.Ln, [PB[6]], [RN], bias=1e-6)
                    S.act(RN[:], RN[:], AF.Exp, [RN], [RN], scale=-0.5)
                    S.cut()
                    S.tt(V, KK[:], KRAW[:], RN[:], ALU.mult, [KRAW, RN], [KK])
                    S.tt(G, KF[:], AA[:], KAF[:], ALU.mult, [AA, KAF], [KF])
                    S.tt(G, KF[:], KF[:], OMKAF[:], ALU.add, [KF, OMKAF], [KF])
                    S.tt(V, KP[:], XK, KF[:], ALU.mult, xk_t + [KF], [KP])
                    S.tt(G, BB[:], KK[:], AA[:], ALU.mult, [KK, AA], [BB])
                    S.cut()
                    S.tt(V, FMQ[:, :, 0, :], KK[:], PEX[:], ALU.mult, [KK, PEX], [FMQ])
                    S.tt(V, FMQ[:, :, 1, :], XR, PIN[:], ALU.mult, xr_t + [PIN], [FMQ])
                    S.tt(V, KDF[:], KP[:], IP[:], ALU.mult, [KP, IP], [KDF])
                    S.tt(G, BDF[:], BB[:], IP[:], ALU.mult, [BB, IP], [BDF])
                    S.cut()
                    for hl in range(2):
                        ps = slice(hl * 64, hl * 64 + 64)
                        S.cp(A, KDZ[ps, :, hl, :], KDF[ps, :, :], [KDF], [KDZ])
                        S.cp(A, BDZ[ps, :, hl, :], BDF[ps, :, :], [BDF], [BDZ])
                    for g in range(4):
                        S.act(KEB[:, g, :], KDF[:, g, :], AF.Copy, [KDF, PIN], [KEB], scale=PIN[:, g, 127:128])
                    for g in range(4):
                        S.act(BEB_[:, g, :], BDF[:, g, :], AF.Copy, [BDF, PIN], [BEB_], scale=PIN[:, g, 127:128])
                    S.cut()
                    S.cp(A, VB_[:], XV, xv_t, [VB_])
                    for (src, dst) in ((KEB, KET), (BEB_, BET), (VB_, VT)):
                        for g in range(4):
                            S.tr(PSB[:, 512 + g * 128:512 + (g + 1) * 128], src[:, g, :], IDB[:], [src], [(PSB, 1)])
                        S.cp(A, dst[:], PSB[:, 512:1024].rearrange("p (a b) -> p a b", a=4), [(PSB, 1)], [dst])
                        S.cut()
                    S.cut()
                    for g in range(4):
                        S.stt(V, RKR[:, g, :], XR[:, g, :], R_K(g), KP[:, g, :], ALU.mult, ALU.mult, xr_t + [PV, KP], [RKR])
                    S.mm(PB[6][:, :], BD1, RKR[:, :, :].rearrange("p a b -> p (a b)"), True, True, [RKR], [PB[6]])
                    S.tt(V, BON[:], b4(PB[6]), XV, ALU.mult, [PB[6]] + xv_t, [BON])
                    S.act(SGT[:], stg[:, 13:17, 3:131], AF.Silu, [stg], [SGT])


                load_chunk(0)
                rw_pre(0)
                for c in range(NT):
                    stg = STG[c % 2]
                    BON = BONs[c % 2]
                    SGT = SGTs[c % 2]
                    BM, FMQ, KET, BET, VT = BMs[c % 2], FMQs[c % 2], KETs[c % 2], BETs[c % 2], VTs[c % 2]
                    if c + 1 < NT:
                        load_chunk(c + 1)
                    cur = lambda cc: stg[:, cc, 3:131]
                    prv = lambda cc: stg[:, cc, 2:130]
                    L_amat, L_dbl, L_chain = [], [], []
                    for gi in range(2):
                        XTARB, AKKARK, ZB, NSA = XTARBs[gi], AKKARKs[gi], ZBs[gi], NSAs[gi]
                        bX, bXT, bAK = (PB[2], PB[3], PB[4]) if gi == 0 else (PB[6], PB[0], PB[1])
                        L_amat.append(S.begin())
                        for pl in range(2):
                            g = gi * 2 + pl
                            for hl in range(2):
                                ps = slice(hl * 64, hl * 64 + 64)
                                hh = pl * 2 + hl
                                S.mm(bX[:, hh * 128:(hh + 1) * 128], FMQ[:, g, 0, :], BDZ[:, g, hl, :], True, True,
                                     [FMQ, BDZ], [bX])
                        for hb in range(2):
                            for q in range(2):
                                hh = hb * 2 + q
                                g = gi * 2 + hh // 2
                                ps = slice((hh % 2) * 64, (hh % 2) * 64 + 64)
                                S.mm(bXT[:, q * 256:(q + 1) * 256], BDZ[:, g, hh % 2, :],
                                     FMQ[:, g, :, :].rearrange("p a b -> p (a b)"), True, True, [FMQ, BDZ], [bXT])
                                S.mm(bAK[:, q * 256:(q + 1) * 256], KDZ[:, g, hh % 2, :],
                                     FMQ[:, g, :, :].rearrange("p a b -> p (a b)"), True, True, [FMQ, KDZ], [bAK])
                            S.tt(V, XTARB[:, hb * 2:hb * 2 + 2, :, :],
                                 bXT[:, :].rearrange("p (a b c) -> p a b c", a=2, b=2), MA[:, 0:2, :, :], ALU.mult,
                                 [bXT], [XTARB])
                            S.tt(V, AKKARK[:, hb * 2:hb * 2 + 2, :, :],
                                 bAK[:, :].rearrange("p (a b c) -> p a b c", a=2, b=2), MB[:, 0:2, :, :], ALU.mult,
                                 [bAK], [AKKARK])
                        S.tt(V, XB[gi][:], b4(bX), MSLN4, ALU.mult, [bX], [XB[gi]])
                        S.cp(G, XTB[gi][:], XTARB[:, :, 0, :], [XTARB], [XTB[gi]])
                        S.cut()
                        L_dbl.append(S.begin())
                        if gi == 0:
                            doubling(XB[gi], XTB[gi], RT[gi], PB[4], PB[2], PB[3])
                        else:
                            doubling(XB[gi], XTB[gi], RT[gi], PB[6], PB[0], PB[1])
                        L_chain.append(S.begin())
                        for pl in range(2):
                            g = gi * 2 + pl
                            for hl in range(2):
                                ps = slice(hl * 64, hl * 64 + 64)
                                S.ts(V, H2Z[ps, g, hl, :], H2F[ps, g, :], BM[ps, 8 + g:9 + g], ALU.mult, [H2F, BM], [H2Z])
                        for hh in range(4):
                            g = gi * 2 + hh // 2
                            hl = hh % 2
                            ps = slice(hl * 64, hl * 64 + 64)
                            S.mm(PB[0][:, hh * 64:(hh + 1) * 64], FMQ[:, g, 0, :], H2Z[:, g, hl, :], True, False,
                                 [FMQ, H2Z], [PB[0]])
                            S.mm(PB[0][:, hh * 64:(hh + 1) * 64], AKKARK[:, hh, 0, :], VT[:, g, hl * 64:(hl + 1) * 64],
                                 False, True, [AKKARK, VT], [PB[0]])
                        S.cut()
                        S.cp(A, ZB[:], PB[0][:, 0:256].rearrange("p (a b) -> p a b", a=4), [PB[0]], [ZB])
                        for hh in range(4):
                            S.mm(PB[0][:, 256 + hh * 64:256 + (hh + 1) * 64], RT[gi][:, hh, :], ZB[:, hh, :], True, True,
                                 [RT[gi], ZB], [PB[0]])
                        S.cut()
                        S.ts(V, NSA[:], PB[0][:, 256:512].rearrange("p (a b) -> p a b", a=4), -1.0, ALU.mult,
                             [PB[0]], [NSA])
                        for hh in range(4):
                            g = gi * 2 + hh // 2
                            hl = hh % 2
                            ps = slice(hl * 64, hl * 64 + 64)
                            oy = PB[5][:, (gi * 4 + hh) * 64:(gi * 4 + hh + 1) * 64]
                            S.mm(oy, FMQ[:, g, 1, :], H2Z[:, g, hl, :], True, False, [FMQ, H2Z], [PB[5]])
                            S.mm(oy, AKKARK[:, hh, 1, :], VT[:, g, hl * 64:(hl + 1) * 64], False, False,
                                 [AKKARK, VT], [PB[5]])
                            S.mm(oy, XTARB[:, hh, 1, :], NSA[:, hh, :], False, True, [XTARB, NSA], [PB[5]])
                        for pl in range(2):
                            g = gi * 2 + pl
                            oh = PB[1][:, pl * 128:(pl + 1) * 128]
                            S.mm(oh, KET[:, g, :], VT[:, g, :], True, False, [KET, VT], [PB[1]])
                            S.mm(oh, BET[:, g, :], NSA[:, pl * 2:pl * 2 + 2, :].rearrange("p a b -> p (a b)"), False, True,
                                 [BET, NSA], [PB[1]])
                        S.cut()
                        for pl in range(2):
                            g = gi * 2 + pl
                            for hl in range(2):
                                ps = slice(hl * 64, hl * 64 + 64)
                                S.stt(V, H2F[ps, g, :], H2F[ps, g, :], BM[ps, 12 + g:13 + g],
                                      PB[1][ps, pl * 128 + hl * 64:pl * 128 + (hl + 1) * 64], ALU.mult, ALU.add,
                                      [H2F, BM, PB[1], H2Z], [H2F])
                        S.cut()
                    L_rwpost = S.begin()
                    S.cp(A, YSB[:], PB[5][:, :].rearrange("p (a b) -> p a b", a=8), [PB[5]], [YSB])
                    S.tt(G, YSQ[:], YSB[:], YSB[:], ALU.mult, [YSB], [YSQ])
                    S.op(V, lambda: nc.vector.tensor_reduce(out=GST[:, 0:8], in_=YSB[:], axis=AX.X, op=ALU.add), [YSB], [GST])
                    S.op(V, lambda: nc.vector.tensor_reduce(out=GST[:, 8:16], in_=YSQ[:], axis=AX.X, op=ALU.add), [YSQ], [GST])
                    S.ts(V, GST[:, 16:24], GST[:, 0:8], 1.0 / 64, ALU.mult, [GST], [GST])
                    S.tt(V, GST[:, 24:32], GST[:, 16:24], GST[:, 16:24], ALU.mult, [GST], [GST])
                    S.stt(V, GST[:, 32:40], GST[:, 8:16], 1.0 / 64, GST[:, 24:32], ALU.mult, ALU.subtract, [GST], [GST])
                    S.cut()
                    S.act(GST[:, 40:48], GST[:, 32:40], AF.Sqrt, [GST], [GST], bias=64e-5)
                    S.op(V, lambda: nc.vector.reciprocal(out=GST[:, 40:48], in_=GST[:, 40:48]), [GST], [GST])
                    for h in range(8):
                        S.ts(V, YN[:, h, :], YSB[:, h, :], GST[:, 16 + h:17 + h], ALU.subtract,
                             [YSB, GST], [YN], GST[:, 40 + h:41 + h], ALU.mult)
                    S.cut()
                    for g in range(4):
                        S.tr(PB[6][:, g * 128:(g + 1) * 128], YN[:, 2 * g:2 * g + 2, :].rearrange("p a b -> p (a b)"), IDF,
                             [YN], [PB[6]])
                    for g in range(4):
                        S.ts(V, Y2[:, g, :], PB[6][:, g * 128:(g + 1) * 128], GNW(g), ALU.mult, [PB[6], PV], [Y2],
                             GNB(g), ALU.add)
                    S.tt(G, Y2[:], Y2[:], BON[:], ALU.add, [Y2, BON], [Y2])
                    S.tt(V, YOB[:, 0:4, :], Y2[:], SGT[:], ALU.mult, [Y2, SGT], [YOB])

                    L_dnpre = S.begin()
                    for cj in range(12):
                        e = G if cj == 11 else V
                        cc = 17 + cj
                        S.ts(e, CV[:, cj, :], stg[:, cc, 0:128], CWT(cj, 0), ALU.mult, [stg, PV], [(CV, cj)])
                        for tap in range(1, 4):
                            if e is V:
                                S.stt(e, CV[:, cj, :], stg[:, cc, tap:tap + 128], CWT(cj, tap), CV[:, cj, :], ALU.mult, ALU.add,
                                      [stg, PV, (CV, cj)], [(CV, cj)])
                            else:
                                S.ts(G, TMPG[:], stg[:, cc, tap:tap + 128], CWT(cj, tap), ALU.mult, [stg, PV], [TMPG])
                                S.tt(G, CV[:, cj, :], CV[:, cj, :], TMPG[:], ALU.add, [TMPG, (CV, cj)], [(CV, cj)])
                        if cj % 2 == 1:
                            S.cut()
                    for j in range(3):
                        S.act(QKV[:, 4 * j:4 * j + 4, :], CV[:, 4 * j:4 * j + 4, :], AF.Silu,
                              [(CV, 4 * j + i) for i in range(4)], [QKV])
                    S.cut()
                    S.tt(G, SQD[:], QKV[:, 0:8, :], QKV[:, 0:8, :], ALU.mult, [QKV], [SQD])
                    S.mm(PB[5][:, :], ONES, SQD[:, 0:4, :].rearrange("p a b -> p (a b)"), True, True, [SQD], [PB[5]])
                    S.act(RND[:, 0:4, :], b4(PB[5]), AF.Ln, [PB[5]], [RND], bias=1e-6)
                    S.cut()
                    S.mm(PB[5][:, :], ONES, SQD[:, 4:8, :].rearrange("p a b -> p (a b)"), True, True, [SQD], [PB[5]])
                    S.act(RND[:, 4:8, :], b4(PB[5]), AF.Ln, [PB[5]], [RND], bias=1e-6)
                    S.cut()
                    S.act(RND[:], RND[:], AF.Exp, [RND], [RND], scale=-0.5)
                    S.cut()
                    S.tt(V, QKV[:, 0:8, :], QKV[:, 0:8, :], RND[:], ALU.mult, [QKV, RND], [QKV])
                    S.cp(A, KTB[:], QKV[:, 4:8, :], [QKV], [KTB])
                    S.cp(A, VTB[:], QKV[:, 8:12, :], [QKV], [VTB])
                    S.cut()
                    S.act(GE[0:32, :], stg[0:32, 33, 3:131], AF.Exp, [stg, PV], [GE], bias=PV[0:32, 91:92])
                    S.act(GE[0:32, :], GE[0:32, :], AF.Ln, [GE], [GE], bias=1.0)
                    S.ts(V, GE[0:32, :], GE[0:32, :], NA[0:32, 0:1], ALU.mult, [GE, NA], [GE])
                    S.op(V, lambda: nc.vector.tensor_tensor_scan(out=GB[0:32, :], data0=ONE4[0:32, :], data1=GE[0:32, :],
                                                                initial=0.0, op0=ALU.mult, op1=ALU.add), [GE, ONE4], [GB])
                    S.act(GB[32:64, :], stg[32:64, 33, 3:131], AF.Sigmoid, [stg], [GB])
                    S.tr(PB[5][:, 0:128], GB[:, :], IDF, [GB], [PB[5]])
                    S.cp(V, TM[:], PB[5][:, 0:64], [PB[5]], [TM])
                    S.cut()
                    for i in range(4):
                        S.mm(PB[5][:, i * 128:(i + 1) * 128], CST[:, C_SEL + i * 128:C_SEL + (i + 1) * 128], GB[:, :],
                             True, True, [GB], [PB[5]])
                    S.cp(A, GCB[:], b4(PB[5]), [PB[5]], [GCB])
                    S.cut()
                    for i in range(4):
                        S.mm(PB[5][:, i * 128:(i + 1) * 128], CST[:, C_SEL + (4 + i) * 128:C_SEL + (5 + i) * 128], GB[:, :],
                             True, True, [GB], [PB[5]])
                    S.cp(A, BEBC[:], b4(PB[5]), [PB[5]], [BEBC])
                    S.cut()
                    S.cut()
                    S.act(EGB[:], GCB[:], AF.Exp, [GCB], [EGB])
                    S.act(TMS[:, 0:4], TM[:, 0:4], AF.Exp, [TM], [TMS])
                    S.tt(V, TMS[:, 4:8], TMS[:, 0:4], TM[:, 32:36], ALU.mult, [TMS, TM], [TMS])
                    for h in range(4):
                        S.act(TMS[:, 8 + h:9 + h], TM[:, h:h + 1], AF.Exp, [TM, GCB], [TMS], bias=GCB[:, h, 127:128], scale=-1.0)
                    for h in range(4):
                        S.ts(V, DD[:, h, :], GCB[:, h, :], TM[:, h:h + 1], ALU.subtract, [GCB, TM], [DD])
                    S.cut()
                    S.ts(V, D1[:], DD[:], 0.0, ALU.max, [DD], [D1])
                    S.ts(V, D2[:], DD[:], 0.0, ALU.min, [DD], [D2])
                    S.act(EE[:], D1[:], AF.Exp, [D1], [EE], scale=-1.0)
                    S.act(ET[:], D2[:], AF.Exp, [D2], [ET])
                    for h in range(4):
                        S.stt(V, XE[:, h, :], EE[:, h, :], TM[:, 32 + h:33 + h], MSLN4[:, h, :], ALU.mult, ALU.mult,
                              [EE, TM], [XE])
                    S.cut()
                    S.tt(G, XTE[:], ET[:], BEBC[:], ALU.mult, [ET, BEBC], [XTE])
                    S.tt(G, XTE[:], XTE[:], NMSU4, ALU.mult, [XTE], [XTE])
                    S.tt(G, AE[:], ET[:], MIU4, ALU.mult, [ET], [AE])
                    S.cut()
                    S.act(QTB[:], QKV[:, 0:4, :], AF.Copy, [QKV], [QTB], scale=128.0 ** -0.5)
                    S.stt(V, QGT[:], QKV[:, 0:4, :], 128.0 ** -0.5, EGB[:], ALU.mult, ALU.mult, [QKV, EGB], [QGT])
                    L_dnamat = S.begin()
                    for h in range(4):
                        S.mm(PB[5][:, h * 128:(h + 1) * 128], KTB[:, h, :], KTB[:, h, :], True, True, [KTB], [PB[5]])
                    for h in range(4):
                        S.tr(PSB[:, h * 128:(h + 1) * 128], KTB[:, h, :], IDB[:], [KTB], [PSB])
                    for h in range(4):
                        S.tr(PSB[:, 512 + h * 128:512 + (h + 1) * 128], VTB[:, h, :], IDB[:], [VTB], [PSB])
                    S.cut()
                    S.tt(V, XF[:], b4(PB[5]), XE[:], ALU.mult, [PB[5], XE], [XF])
                    S.tt(V, XTBD[:], b4(PB[5]), XTE[:], ALU.mult, [PB[5], XTE], [XTBD])
                    S.cut()
                    for h in range(4):
                        S.mm(PB[5][:, h * 128:(h + 1) * 128], KTB[:, h, :], QTB[:, h, :], True, True, [KTB, QTB], [PB[5]])
                    S.cp(A, XH[:], XF[:], [XF], [XH])
                    S.tt(G, XL[:], XF[:], XH[:], ALU.subtract, [XF, XH], [XL])
                    S.cp(A, XBD[:], XH[:], [XH], [XBD])
                    S.cut()
                    S.tt(V, ATB[:], b4(PB[5]), AE[:], ALU.mult, [PB[5], AE], [ATB])
                    for h in range(4):
                        S.ts(V, KBG[:, h, :], PSB[:, h * 128:(h + 1) * 128], TMS[:, 4 + h:5 + h], ALU.mult, [PSB, TMS], [KBG])
                        S.ts(V, KE[:, h, :], PSB[:, h * 128:(h + 1) * 128], TMS[:, 8 + h:9 + h], ALU.mult, [PSB, TMS], [KE])
                        S.ts(V, VBD[:, h, :], PSB[:, 512 + h * 128:512 + (h + 1) * 128], TM[:, 32 + h:33 + h], ALU.mult,
                             [PSB, TM], [VBD])
                        if h % 2 == 1:
                            S.cut()
                    L_dndbl = S.begin()
                    doubling(XBD, XTBD, RTD, PB[4], PB[2], PB[3], nlev=6)
                    for h in range(4):
                        S.mm(PB[2][:, h * 128:(h + 1) * 128], XH[:, h, :], RTD[:, h, :], True, False, [XH, RTD], [PB[2]])
                        S.mm(PB[2][:, h * 128:(h + 1) * 128], XL[:, h, :], RTD[:, h, :], False, True, [XL, RTD], [PB[2]])
                    S.cut()
                    S.tt(V, XF[:], b4(PB[2]), RTD[:], ALU.subtract, [PB[2], RTD], [XF])
                    S.tt(G, RRT[:], XF[:], ID4[:], ALU.add, [XF], [RRT])
                    for h in range(4):
                        S.tr(PSB[:, h * 128:(h + 1) * 128], RRT[:, h, :], IDB[:], [RRT], [(PSB, 0)])
                    S.cut()
                    S.cp(A, RRN[:], PSB[:, 0:512].rearrange("p (a b) -> p a b", a=4), [(PSB, 0)], [RRN])
                    for h in range(4):
                        S.mm(PB[3][:, h * 128:(h + 1) * 128], RRN[:, h, :], RTD[:, h, :], True, True, [RRN, RTD], [PB[3]])
                    S.cut()
                    S.tt(V, RTD[:], b4(PB[3]), RTD[:], ALU.add, [PB[3], RTD], [RTD])
                    L_dntail = S.begin()
                    for h in range(4):
                        S.mm(PB[0][:, h * 128:(h + 1) * 128], RTD[:, h, :], VBD[:, h, :], True, True, [RTD, VBD], [PB[0]])
                    for h in range(4):
                        S.mm(PB[1][:, h * 128:(h + 1) * 128], KBG[:, h, :], RTD[:, h, :], True, True, [RTD, KBG], [PB[1]])
                    S.cut()
                    S.cp(A, USB[:], b4(PB[0]), [PB[0]], [USB])
                    S.cp(V, WTB[:], b4(PB[1]), [PB[1]], [WTB])
                    for h in range(4):
                        S.mm(PB[0][:, h * 128:(h + 1) * 128], WTB[:, h, :], SBF[:, h, :], True, True, [WTB, SBF], [PB[0]])
                    S.cut()
                    S.tt(V, VN[:], USB[:], b4(PB[0]), ALU.subtract, [USB, PB[0]], [VN])
                    for h in range(4):
                        S.mm(PB[1][:, h * 128:(h + 1) * 128], QGT[:, h, :], SBF[:, h, :], True, False, [QGT, SBF], [PB[1]])
                        S.mm(PB[1][:, h * 128:(h + 1) * 128], ATB[:, h, :], VN[:, h, :], False, True, [ATB, VN], [PB[1]])
                    for h in range(4):
                        S.mm(PB[0][:, h * 128:(h + 1) * 128], KE[:, h, :], VN[:, h, :], True, True, [KE, VN], [PB[0]])
                    for h in range(4):
                        S.stt(V, SF[:, h, :], SF[:, h, :], EGB[:, h, 127:128], PB[0][:, h * 128:(h + 1) * 128], ALU.mult, ALU.add,
                              [SF, EGB, PB[0]], [SF])
                    S.cp(A, SBF[:], SF[:], [SF], [SBF])
                    S.cut()
                    S.act(OSQ[:], b4(PB[1]), AF.Square, [PB[1]], [OSQ])
                    S.op(V, lambda: nc.vector.tensor_reduce(out=TMS[:, 12:16], in_=OSQ[:], axis=AX.X, op=ALU.add), [OSQ], [TMS])
                    S.act(TMS[:, 16:20], TMS[:, 12:16], AF.Sqrt, [TMS], [TMS], bias=1e-6, scale=1.0 / 128)
                    S.op(V, lambda: nc.vector.reciprocal(out=TMS[:, 20:24], in_=TMS[:, 16:20]), [TMS], [TMS])
                    for h in range(4):
                        S.act(ON[:, h, :], PB[1][:, h * 128:(h + 1) * 128], AF.Copy, [PB[1], TMS], [ON], scale=TMS[:, 20 + h:21 + h])
                    for h in range(4):
                        S.tr(PB[2][:, h * 128:(h + 1) * 128], ON[:, h, :], IDF, [ON], [PB[2]])
                    S.act(SZ[:], stg[:, 29:33, 3:131], AF.Silu, [stg], [SZ])
                    S.cut()
                    S.stt(V, YOB[:, 4:8, :], b4(PB[2]), DNW, SZ[:], ALU.mult, ALU.mult, [PB[2], PV, SZ], [YOB])
                    if c + 1 < NT:
                        L_pre = S.begin()
                        rw_pre(c + 1)
                    else:
                        L_pre = []
                    S.end()
                    S.play(L_amat[0] + L_dbl[0], L_amat[1] + L_dbl[1], L_dnpre + L_dnamat)
                    S.play(L_chain[0] + L_chain[1], L_dndbl, L_pre)
                    S.play(L_rwpost, L_dntail)
                    if coll:
                        for k in range(8):
                            S.dma(yTk[k].ap()[:, c * 128:(c + 1) * 128], YOB[:, k, :], [YOB], ())
                    else:
                        S.dma(yT_d[hf * 8:(hf + 1) * 8, :, c * 128:(c + 1) * 128].rearrange("k p t -> p k t"), YOB[:], [YOB], ())
        S.barrier()
        CCT = Tok()

        NK = NKH * 8
        with contextlib.ExitStack() as ph:
            WOB = sb(ph, "wob", [128, NK, D], BF16)
            WOS = [sb(ph, "wos%d" % i, [128, D], F32) for i in range(2)]
            FNW = sb(ph, "fnw", [128, D], F32)
            YT = [sb(ph, "yt%d" % i, [128, NK, 512 if coll else 128], BF16) for i in range(2)]
            XR3 = [sb(ph, "xr3_%d" % i, [128, D], F32) for i in range(2)]
            HS = sb(ph, "hs", [128, D], F32)
            JK = sb(ph, "jk", [128, D], BF16)
            OT = [sb(ph, "ot%d" % i, [128, D], F32) for i in range(2)]
            ST3 = [sb(ph, "st3_%d" % i, [128, 4], F32) for i in range(2)]
            S.dma(FNW[:], fnw_d[:, :], (), [FNW])
            for kc in range(NK if stop >= 3 else 0):
                wos = WOS[kc % 2]
                S.dma(wos[:], wout_d[kc * 128:(kc + 1) * 128, :], (), [wos])
                S.cp(V if kc % 2 == 0 else A, WOB[:, kc, :], wos[:], [wos], [WOB])
            if coll and stop >= 3:
                ccsem = es.enter_context(nc.semaphore("s_cc"))
                with nc.Block() as block:
                    @block.gpsimd
                    def _(g):
                        for k in range(8):
                            g.collective_compute("AllGather", ALU.bypass, replica_groups=[[0, 1], [2, 3], [4, 5], [6, 7]],
                                                 ins=[yTk[k].ap().opt()], outs=[yAk[k].ap().opt()]).then_inc(ccsem)
                            g.wait_ge(ccsem, k + 1)
                CCD = sb(ph, "ccd", [128, 8], F32)
                S.ms(G, CCD[:], 0.0, [CCD, CCT])

            for i in range(1, NT if stop >= 3 else 0):
                yt, xr, ot, st3 = YT[i % 2], XR3[i % 2], OT[i % 2], ST3[i % 2]
                if coll:
                    st_i = (i - 1) // 4
                    yt = YT[st_i % 2]
                    if (i - 1) % 4 == 0:
                        n3 = min(512, T - i * 128)
                        for k in range(8):
                            for r in range(2):
                                S.dma(yt[:, r * 8 + k, 0:n3], yAk[k].ap()[r * 128:(r + 1) * 128, i * 128:i * 128 + n3], [CCT], [yt])
                    yoff = ((i - 1) % 4) * 128
                else:
                    yoff = 0
                    S.dma(yt[:], yA_d[:, :, i * 128:(i + 1) * 128].rearrange("k p t -> p k t"), [CCT], [yt])
                S.dma(xr[:], x_d[(i - 1) * 128:i * 128, :], (), [xr])
                for nb in range(4):
                    for kc in range(NK):
                        S.mm(PB[nb][:, :], yt[:, kc, yoff:yoff + 128], WOB[:, kc, nb * 512:(nb + 1) * 512], kc == 0, kc == NK - 1,
                             [yt, WOB], [PB[nb]])
                for nb in range(4):
                    S.tt(V, HS[:, nb * 512:(nb + 1) * 512], PB[nb][:, :], xr[:, nb * 512:(nb + 1) * 512], ALU.add,
                         [PB[nb], xr], [HS])
                S.ms(V, st3[:, 0:1], 0.0, [st3])
                S.act(JK[:], HS[:], AF.Square, [HS], [JK, st3], accum_out=st3[:, 0:1])
                S.act(st3[:, 1:2], st3[:, 0:1], AF.Sqrt, [st3], [st3], bias=1e-6, scale=1.0 / D)
                S.op(V, lambda: nc.vector.reciprocal(out=st3[:, 2:3], in_=st3[:, 1:2]), [st3], [st3])
                S.stt(V, ot[:], HS[:], st3[:, 2:3], FNW[:], ALU.mult, ALU.mult, [HS, st3, FNW], [ot])
                S.dma(out_d[(i - 1) * 128:i * 128, :], ot[:], [ot], ())
        S.finish()
        print('MARKS', S.marks[:40], 'total', S.nops, [e.n for e in S.engs])
    return nc


def _consts():
    c = np.zeros((128, C_END), np.float32)
    p = np.arange(128)[:, None]
    f = np.arange(128)[None, :]
    ident = (p == f).astype(np.float32)
    msl_neg = -(p > f).astype(np.float32)
    msu = (f > p).astype(np.float32)
    miu = (f >= p).astype(np.float32)
    c[:, C_ID:C_ID + 128] = ident
    c[:, C_MSLN4:C_MSLN4 + 512] = np.tile(msl_neg, (1, 4))
    c[:, C_MA:C_MA + 1024] = np.tile(np.concatenate([-msu, miu], 1), (1, 4))
    c[:, C_MB:C_MB + 1024] = np.tile(np.concatenate([msu, miu], 1), (1, 4))
    c[:, C_BD1:C_BD1 + 128] = ((p // 64) == (f // 64)).astype(np.float32)
    c[:, C_ONES:C_ONES + 128] = 1.0
    rows = [0, 1, 2, 3, 32, 33, 34, 35]
    for i, r in enumerate(rows):
        c[r, C_SEL + i * 128:C_SEL + (i + 1) * 128] = 1.0
    c[:, C_NMSU4:C_NMSU4 + 512] = np.tile(-msu, (1, 4))
    c[:, C_MIU4:C_MIU4 + 512] = np.tile(miu, (1, 4))
    return c


def _half_layout(hh, w_in, mu, w0, w2, a0, a2, k_k, k_a, r_k, gn_w, gn_b, conv_w, A_log, dt_bias, dn_norm_w, w_out):
    o = 512 * hh
    ar = np.arange(512)
    cols = np.concatenate([o + ar, 1024 + o + ar, 2048 + o + ar, 3072 + np.arange(128), 3200 + o + ar,
                           4224 + o + ar, 5248 + o + ar, 6272 + o + ar, 7312 + o + ar])
    wl = np.zeros((D, NCOL), np.float32)
    wl[:, :33 * 128] = w_in[:, cols]
    wl[:, 33 * 128:33 * 128 + 4] = w_in[:, 7304 + 4 * hh:7304 + 4 * hh + 4]
    wl[:, 33 * 128 + 32:33 * 128 + 36] = w_in[:, 7296 + 4 * hh:7296 + 4 * hh + 4]
    pv = np.zeros((128, NPV), np.float32)
    pv[:, 0:13] = mu[cols[:13 * 128]].reshape(13, 128).T
    loc = lambda v: v[o:o + 512].reshape(4, 128).T
    pv[:, 13:17] = loc(w0)
    pv[:, 17:21] = loc(a0)
    pv[:, 21:25] = loc(k_k)
    pv[:, 25:29] = loc(k_a)
    pv[:, 29:33] = loc(r_k)
    pv[:, 33:37] = loc(gn_w)
    pv[:, 37:41] = loc(gn_b)
    for cj in range(12):
        base = (cj // 4) * 1024 + o + (cj % 4) * 128
        pv[:, 41 + 4 * cj:45 + 4 * cj] = conv_w[:, base:base + 128].T
    pv[:, 89] = dn_norm_w
    pv[0:4, 90] = A_log[4 * hh:4 * hh + 4]
    pv[0:4, 91] = dt_bias[4 * hh:4 * hh + 4]
    w2a2 = np.zeros((128, 2, 512), np.float32)
    w2a2[0:64, 0] = w2[:, o:o + 512]
    w2a2[64:128, 1] = a2[:, o:o + 512]
    wo = np.concatenate([w_out[o:o + 512], w_out[1024 + o:1024 + o + 512]], 0)
    return wl, pv, np.ascontiguousarray(w2a2), np.ascontiguousarray(wo)


_NC_CACHE = {}


def kernel(x, meta_tokens, norm_w, w_in, rw_shift_mu, rw_w0, rw_w2, rw_a0, rw_a2, rw_k_k, rw_k_a, rw_r_k,
           rw_gn_w, rw_gn_b, dn_conv_w, dn_A_log, dn_dt_bias, dn_norm_w, w_out, final_norm_w, _nt=33, _cores=8, _stop=9, _nh=2, _maxops=10 ** 9, _dbg=False, _b0=0):
    f = lambda a: np.asarray(a, np.float32)
    x = f(x)
    NT = _nt
    coll = (_cores == 8) and not _dbg
    NH = 1 if coll else _nh
    halves = [_half_layout(hh, f(w_in)[0], f(rw_shift_mu)[0], f(rw_w0)[0], f(rw_w2)[0], f(rw_a0)[0], f(rw_a2)[0],
                           f(rw_k_k)[0], f(rw_k_a)[0], f(rw_r_k)[0], f(rw_gn_w)[0], f(rw_gn_b)[0], f(dn_conv_w)[0],
                           f(dn_A_log)[0], f(dn_dt_bias)[0], f(dn_norm_w)[0], f(w_out)[0]) for hh in range(2)]
    shared = {
        "meta": f(meta_tokens),
        "normw": np.ascontiguousarray(np.broadcast_to(f(norm_w)[0][None], (128, D))),
        "fnormw": np.ascontiguousarray(np.broadcast_to(f(final_norm_w)[None], (128, D))),
        "consts": _consts(),
    }
    sel = (lambda hh: [hh]) if coll else (lambda hh: list(range(NH)))
    key = (NT, NH, _stop, _maxops, _dbg, coll)
    if key not in _NC_CACHE:
        _NC_CACHE[key] = build(NT, NH, _stop, _maxops, _dbg, coll)
    nc = _NC_CACHE[key]
    NX = (NT - 1) * 128
    wo_halves = list(range(2)) if coll else list(range(NH))
    w_out_l = np.concatenate([halves[h][3] for h in wo_halves], 0)
    in_maps = []
    for c in range(_cores):
        b = (_b0 + c // 2) % x.shape[0]
        hs = sel(c % 2)
        m = dict(shared)
        m["x"] = np.ascontiguousarray(x[b, :NX])
        m["w_in_l"] = np.stack([halves[h][0] for h in hs])
        m["pv"] = np.stack([halves[h][1] for h in hs])
        m["w2a2"] = np.stack([halves[h][2] for h in hs])
        m["w_out_l"] = w_out_l
        in_maps.append(m)
    res = run_bass_kernel_spmd(nc, in_maps, core_ids=list(range(_cores)))
    if _dbg:
        return res.results[0]["out"][None], res.results[0]["yT_s"]
    outs = [res.results[2 * b]["out"] for b in range(x.shape[0])] if _cores == 8 else [res.results[0]["out"]]
    return np.stack(outs, 0).astype(np.float32)
```

```python
import contextlib
import numpy as np
import ml_dtypes
import concourse.bass as bass
import concourse.mybir as mybir
from concourse.bass_utils import run_bass_kernel_spmd

F32 = mybir.dt.float32
BF16 = mybir.dt.bfloat16
AF = mybir.ActivationFunctionType
ALU = mybir.AluOpType
AX = mybir.AxisListType

D = 2048
SEQ = 4096
NMETA = 16
NCH = 34
NCOL = 33 * 128 + 36
LASTM = 36
NPV = 92
CW = 0.6065306597126334
C_ID, C_MSLN4, C_MA, C_MB, C_BD1, C_ONES, C_SEL, C_NMSU4, C_MIU4, C_END = (
    0, 128, 640, 1664, 2688, 2816, 2944, 3968, 4480, 4992)


class Tile:
    def __init__(self, t, ntok=1):
        self.t = t
        self.k = [Tok() for _ in range(ntok)]

    def __getitem__(self, idx):
        return self.t[idx]


class Tok:
    __slots__ = ("w", "rs")

    def __init__(self):
        self.w = {}
        self.rs = {}


class Eng:
    def __init__(self, h, sem, is_pe=False):
        self.h = h
        self.sem = sem
        self.n = 0
        self.known = {}
        self.is_pe = is_pe


def _toks(lst):
    out = []
    for x in lst:
        if isinstance(x, Tile):
            out.extend(x.k)
        elif isinstance(x, Tok):
            out.append(x)
        else:
            t, i = x
            out.append(t.k[i])
    return out


class Sch:
    NS = 16

    def __init__(self, nc, es):
        self.nc = nc
        sem = lambda n: es.enter_context(nc.semaphore(n))
        self.P = Eng(nc.tensor, sem("s_pe"), True)
        self.V = Eng(nc.vector, sem("s_dve"))
        self.A = Eng(nc.scalar, sem("s_act"))
        self.G = Eng(nc.gpsimd, sem("s_pool"))
        self.Q = Eng(nc.sync, sem("s_sp"))
        self.engs = [self.P, self.V, self.A, self.G, self.Q]
        self.dsems = [sem("s_dma%d" % i) for i in range(self.NS)]
        self.dcnt = [0] * self.NS
        self.dn = 0
        self.nops = 0
        self.maxops = 10 ** 9
        self.marks = []
        self.rec = None

    def _wait(self, eng, sem, val):
        k = id(sem)
        if eng.known.get(k, 0) >= val:
            return
        eng.h.wait_ge(sem, val)
        eng.known[k] = val

    def _deps(self, eng, reads, writes, is_dma):
        need = {}

        def add(rec, raw):
            sem, val, src = rec
            if (not is_dma) and src is eng:
                if eng.is_pe or not raw:
                    return
            k = id(sem)
            if k not in need or need[k][1] < val:
                need[k] = (sem, val)

        for t in reads:
            for rec in t.w.values():
                add(rec, True)
        for t in writes:
            for rec in t.w.values():
                add(rec, False)
            for rec in t.rs.values():
                add(rec, False)
        for sem, val in need.values():
            self._wait(eng, sem, val)

    def _record(self, rec, reads, writes):
        k = id(rec[0])
        for t in reads:
            t.rs[k] = rec
        for t in writes:
            t.w[k] = rec
            t.rs = {}

    def mark(self, name):
        self.marks.append((name, self.nops))

    def begin(self):
        self.rec = []
        return self.rec

    def end(self):
        self.rec = None

    def cut(self):
        if self.rec is not None:
            self.rec.append(None)

    def play(self, *lists):
        its = [list(l) for l in lists]
        pos = [0] * len(its)
        live = True
        while live:
            live = False
            for i, l in enumerate(its):
                while pos[i] < len(l):
                    item = l[pos[i]]
                    pos[i] += 1
                    if item is None:
                        break
                    if item[0] == "op":
                        self.op(*item[1:])
                    else:
                        self.dma(*item[1:])
                if pos[i] < len(l):
                    live = True

    def op(self, eng, fn, R=(), W=()):
        if self.rec is not None:
            self.rec.append(("op", eng, fn, R, W))
            return
        self.nops += 1
        if self.nops > self.maxops:
            return
        reads = _toks(R)
        writes = _toks(W)
        self._deps(eng, reads, writes, False)
        inst = fn()
        eng.n += 1
        inst.then_inc(eng.sem, 1)
        self._record((eng.sem, eng.n, eng), reads, writes)

    def dma(self, out, in_, R=(), W=()):
        if self.rec is not None:
            self.rec.append(("dma", out, in_, R, W))
            return
        self.nops += 1
        if self.nops > self.maxops:
            return
        reads = _toks(R)
        writes = _toks(W)
        q = self.Q
        self._deps(q, reads, writes, True)
        i = self.dn % self.NS
        self.dn += 1
        sem = self.dsems[i]
        if self.dcnt[i] > 0:
            self._wait(q, sem, 16 * self.dcnt[i])
        self.dcnt[i] += 1
        q.h.dma_start(out=out, in_=in_).then_inc(sem, 16)
        self._record((sem, 16 * self.dcnt[i], None), reads, writes)

    def barrier(self):
        for e in self.engs:
            for f in self.engs:
                if f is not e and f.n > 0:
                    self._wait(e, f.sem, f.n)
            for i in range(self.NS):
                if self.dcnt[i] > 0:
                    self._wait(e, self.dsems[i], 16 * self.dcnt[i])

    def finish(self):
        q = self.Q
        for f in self.engs:
            if f is not q and f.n > 0:
                self._wait(q, f.sem, f.n)
        for i in range(self.NS):
            if self.dcnt[i] > 0:
                self._wait(q, self.dsems[i], 16 * self.dcnt[i])

    def mm(self, out, lhsT, rhs, start, stop, R, W):
        self.op(self.P, lambda: self.nc.tensor.matmul(out, lhsT=lhsT, rhs=rhs, start=start, stop=stop), R, W)

    def tr(self, out, in_, ident, R, W):
        self.op(self.P, lambda: self.nc.tensor.transpose(out, in_, ident), R, W)

    def act(self, out, in_, func, R, W, bias=0.0, scale=1.0, accum_out=None):
        if accum_out is None:
            self.op(self.A, lambda: self.nc.scalar.activation(out=out, in_=in_, func=func, bias=bias, scale=scale), R, W)
        else:
            self.op(self.A, lambda: self.nc.scalar.activation(out=out, in_=in_, func=func, bias=bias, scale=scale,
                                                              accum_out=accum_out), R, W)

    def tt(self, e, out, in0, in1, op, R, W):
        self.op(e, lambda: e.h.tensor_tensor(out=out, in0=in0, in1=in1, op=op), R, W)

    def ts(self, e, out, in0, s1, op0, R, W, s2=None, op1=None):
        if op1 is None:
            self.op(e, lambda: e.h.tensor_scalar(out=out, in0=in0, scalar1=s1, scalar2=None, op0=op0), R, W)
        else:
            self.op(e, lambda: e.h.tensor_scalar(out=out, in0=in0, scalar1=s1, scalar2=s2, op0=op0, op1=op1), R, W)

    def stt(self, e, out, in0, scalar, in1, op0, op1, R, W):
        self.op(e, lambda: e.h.scalar_tensor_tensor(out=out, in0=in0, scalar=scalar, in1=in1, op0=op0, op1=op1), R, W)

    def cp(self, e, out, in_, R, W):
        if e is self.A:
            self.op(e, lambda: self.nc.scalar.activation(out=out, in_=in_, func=AF.Copy), R, W)
        else:
            self.op(e, lambda: e.h.tensor_copy(out=out, in_=in_), R, W)

    def ms(self, e, ap, val, W):
        self.op(e, lambda: e.h.memset(ap, val), (), W)


def build(NT, NH, stop=9, maxops=10 ** 9, dbg=False, coll=False):
    T = NT * 128
    NX = (NT - 1) * 128
    nc = bass.Bass("TRN2", target_bir_lowering=False)
    dt = lambda name, shape, dtype, kind: nc.dram_tensor(name, shape, dtype, kind=kind).ap()
    x_d = dt("x", [NX, D], F32, "ExternalInput")
    meta_d = dt("meta", [NMETA, D], F32, "ExternalInput")
    nw_d = dt("normw", [128, D], F32, "ExternalInput")
    fnw_d = dt("fnormw", [128, D], F32, "ExternalInput")
    win_d = dt("w_in_l", [NH, D, NCOL], F32, "ExternalInput")
    pv_d = dt("pv", [NH, 128, NPV], F32, "ExternalInput")
    w2a2_d = dt("w2a2", [NH, 128, 2, 512], F32, "ExternalInput")
    NKH = 2 if coll else NH
    wout_d = dt("w_out_l", [NKH * 1024, D], F32, "ExternalInput")
    cst_d = dt("consts", [128, C_END], F32, "ExternalInput")
    out_d = dt("out", [NX, D], F32, "ExternalOutput")
    uT_d = dt("uT_s", [16, 128, T], BF16, "Internal")
    pT_d = dt("pT_s", [NH, NCH * 128, T], F32, "Internal")
    yT_t = nc.dram_tensor("yT_s", [NH * 8 * 128, T], BF16, kind="ExternalOutput" if dbg else "Internal")
    yT_d = yT_t.ap().rearrange("(k p) t -> k p t", p=128)
    if coll:
        yTk = [nc.dram_tensor("yT_k%d" % k, [128, T], BF16, kind="Internal") for k in range(8)]
        yAk = [nc.dram_tensor("yA_k%d" % k, [2 * 128, T], BF16, kind="Internal") for k in range(8)]
    else:
        yA_d = yT_d

    with contextlib.ExitStack() as es:
        S = Sch(nc, es)
        S.maxops = maxops
        P, V, A, G = S.P, S.V, S.A, S.G

        def sb(stack, name, shape, dtype, ntok=1):
            return Tile(stack.enter_context(nc.sbuf_tensor("sb_" + name, shape, dtype)), ntok)

        PB = [Tile(es.enter_context(nc.psum_tensor("pb%d" % i, [128, 512], F32))) for i in range(7)]
        PSB = Tile(es.enter_context(nc.psum_tensor("psb", [128, 1024], BF16)), 2)

        CST = sb(es, "cst", [128, C_END], F32)
        IDB = sb(es, "idb", [128, 128], BF16)
        S.dma(CST[:], cst_d[:, :], (), [CST])
        S.cp(V, IDB[:], CST[:, C_ID:C_ID + 128], [CST], [IDB])
        IDF = CST[:, C_ID:C_ID + 128]
        S.barrier()
        KC = ()

        with contextlib.ExitStack() as ph:
            NWR = sb(ph, "nwr", [128, D], F32)
            S.dma(NWR[:], nw_d[:, :], (), [NWR])
            XT = [sb(ph, "xt%d" % i, [128, D], F32) for i in range(2)]
            UNB = [sb(ph, "unb%d" % i, [128, D], BF16) for i in range(2)]
            UTS = [sb(ph, "uts%d" % i, [128, 16, 128], BF16) for i in range(2)]
            ST1 = [sb(ph, "st1_%d" % i, [128, 4], F32) for i in range(2)]
            for i in range(NT):
                xt, unb, uts, st1 = XT[i % 2], UNB[i % 2], UTS[i % 2], ST1[i % 2]
                if i == 0:
                    S.ms(G, xt[:], 0.0, [xt])
                    S.dma(xt[112:128, :], meta_d[:, :], (), [xt])
                else:
                    S.dma(xt[:], x_d[(i - 1) * 128:i * 128, :], (), [xt])
                S.ms(V, st1[:, 0:1], 0.0, [st1])
                S.act(unb[:], xt[:], AF.Square, [xt], [unb, st1], accum_out=st1[:, 0:1])
                S.act(st1[:, 1:2], st1[:, 0:1], AF.Sqrt, [st1], [st1], bias=1e-6, scale=1.0 / D)
                S.op(V, lambda: nc.vector.reciprocal(out=st1[:, 2:3], in_=st1[:, 1:2]), [st1], [st1])
                S.stt(V, unb[:], xt[:], st1[:, 2:3], NWR[:], ALU.mult, ALU.mult, [xt, st1, NWR], [unb])
                for r in range(2):
                    for j in range(8):
                        kc = r * 8 + j
                        S.tr(PSB[:, j * 128:(j + 1) * 128], unb[:, kc * 128:(kc + 1) * 128], IDB[:], [unb], [PSB])
                    S.cp(A if r == 0 else V, uts[:, r * 8:(r + 1) * 8, :],
                         PSB[:, :].rearrange("p (a b) -> p a b", a=8), [PSB], [uts])
                S.dma(uT_d[:, :, i * 128:(i + 1) * 128].rearrange("k p t -> p k t"), uts[:], [uts], ())
        S.barrier()

        passes = [(0, 17), (17, NCH)] if stop >= 1 else []
        with contextlib.ExitStack() as ph:
            WBF = sb(ph, "wbf", [128, 16, 17 * 128], BF16)
            WST = [sb(ph, "wst%d" % i, [128, 17 * 128], F32) for i in range(2)]
            UTT = [sb(ph, "utt%d" % i, [128, 16, 512], BF16) for i in range(2)]
            OST = [sb(ph, "ost%d" % i, [128, 512], F32) for i in range(4)]
            cnt = 0
            ocnt = 0
            for hf in range(NH):
                for (ca, cb) in passes:
                    c0 = ca * 128
                    ncols = min(cb * 128, NCOL) - c0
                    for kc in range(16):
                        wst = WST[kc % 2]
                        S.dma(wst[:, 0:ncols], win_d[hf, kc * 128:(kc + 1) * 128, c0:c0 + ncols], (), [wst])
                        S.cp(V if kc % 2 == 0 else A, WBF[:, kc, 0:ncols], wst[:, 0:ncols], [wst], [WBF])
                    for t0 in range(0, T, 512):
                        n = min(512, T - t0)
                        utt = UTT[cnt % 2]
                        cnt += 1
                        S.dma(utt[:, :, 0:n], uT_d[:, :, t0:t0 + n].rearrange("k p t -> p k t"), (), [utt])
                        for cc in range(ca, cb):
                            m = 128 if cc < NCH - 1 else LASTM
                            pb = PB[ocnt % 4]
                            ost = OST[ocnt % 4]
                            for kc in range(16):
                                S.mm(pb[0:m, 0:n], WBF[:, kc, (cc - ca) * 128:(cc - ca) * 128 + m], utt[:, kc, 0:n],
                                     kc == 0, kc == 15, [WBF, utt], [pb])
                            S.cp(A if ocnt % 2 == 0 else V, ost[0:m, 0:n], pb[0:m, 0:n], [pb], [ost])
                            S.dma(pT_d[hf, cc * 128:cc * 128 + m, t0:t0 + n], ost[0:m, 0:n], [ost], ())
                            ocnt += 1
        S.barrier()

        with contextlib.ExitStack() as ph:
            def t_(name, shape, dtype=F32, ntok=1):
                return sb(ph, name, shape, dtype, ntok)

            PV = t_("pv", [128, NPV])
            OMM = t_("omm", [128, 13])
            MUF = t_("muf", [128, 13, 128])
            KKF = t_("kkf", [128, 4, 128])
            KAF = t_("kaf", [128, 4, 128])
            OMKAF = t_("omkaf", [128, 4, 128])
            OMKA = t_("omka", [128, 4])
            NA = t_("na", [64, 1])
            W2b = t_("w2b", [128, 2, 512], BF16)
            ONE4 = t_("one4", [128, 128])
            STG = [t_("stg%d" % i, [128, NCH, 131]) for i in range(2)]
            TMP = t_("tmp", [128, 13, 128], F32, 13)
            W2f = TMP
            XS = t_("xs", [128, 13, 128], F32, 13)
            TL = t_("tl", [128, 128], BF16)
            SG = t_("sg", [128, 4, 128])
            AA = t_("aa", [128, 4, 128])
            CUM = t_("cum", [128, 4, 128])
            CME = t_("cme", [128, 4, 128])
            BMs = [t_("bm%d" % i, [128, 16]) for i in range(2)]
            PIN = t_("pin", [128, 4, 128])
            PEX = t_("pex", [128, 4, 128])
            IP = t_("ip", [128, 4, 128])
            KRAW = t_("kraw", [128, 4, 128])
            SQ = t_("sq", [128, 4, 128])
            RN = t_("rn", [128, 4, 128])
            KK = t_("kk", [128, 4, 128])
            KF = t_("kf", [128, 4, 128])
            KP = t_("kp", [128, 4, 128])
            BB = t_("bb", [128, 4, 128])
            FMQs = [t_("fmq%d" % i, [128, 4, 2, 128], BF16) for i in range(2)]
            KDF = t_("kdf", [128, 4, 128])
            BDF = t_("bdf", [128, 4, 128])
            KDZ = t_("kdz", [128, 4, 2, 128], BF16)
            BDZ = t_("bdz", [128, 4, 2, 128], BF16)
            H2Z = t_("h2z", [128, 4, 2, 64], BF16)
            KEB = t_("keb", [128, 4, 128], BF16)
            BEB_ = t_("beb", [128, 4, 128], BF16)
            VB_ = t_("vb", [128, 4, 128], BF16)
            KETs = [t_("ket%d" % i, [128, 4, 128], BF16) for i in range(2)]
            BETs = [t_("bet%d" % i, [128, 4, 128], BF16) for i in range(2)]
            VTs = [t_("vt%d" % i, [128, 4, 128], BF16) for i in range(2)]
            RKR = KRAW
            BONs = [t_("bon%d" % i, [128, 4, 128]) for i in range(2)]
            SGTs = [t_("sgt%d" % i, [128, 4, 128]) for i in range(2)]
            XB = [t_("xb%d" % i, [128, 4, 128], BF16) for i in range(2)]
            XTB = [t_("xtb%d" % i, [128, 4, 128], BF16) for i in range(2)]
            RT = [t_("rt%d" % i, [128, 4, 128], BF16) for i in range(2)]
            XTARBs = [t_("xtarb%d" % i, [128, 4, 2, 128], BF16) for i in range(2)]
            AKKARKs = [t_("akkark%d" % i, [128, 4, 2, 128], BF16) for i in range(2)]
            XBD = t_("xbd", [128, 4, 128], BF16)
            XTBD = t_("xtbd", [128, 4, 128], BF16)
            RTD = t_("rtd", [128, 4, 128], BF16)
            H2F = t_("h2f", [128, 4, 64])
            ZBs = [t_("zb%d" % i, [128, 4, 64], BF16) for i in range(2)]
            NSAs = [t_("nsa%d" % i, [128, 4, 64], BF16) for i in range(2)]
            YSB = t_("ysb", [128, 8, 64])
            YSQ = t_("ysq", [128, 8, 64])
            GST = t_("gst", [128, 48])
            YN = YSQ
            Y2 = t_("y2", [128, 4, 128])
            YOB = t_("yob", [128, 8, 128], BF16)
            CV = TMP
            TMPG = t_("tmpg", [128, 128])
            QKV = XS
            SQD = t_("sqd", [128, 8, 128])
            RND = SQD
            QTB = t_("qtb", [128, 4, 128], BF16)
            KTB = t_("ktb", [128, 4, 128], BF16)
            VTB = t_("vtb", [128, 4, 128], BF16)
            GB = t_("gb", [128, 128])
            GE = t_("ge", [64, 128])
            TM = t_("tm", [128, 64])
            TMS = t_("tms", [128, 24])
            GCB = SG
            BEBC = AA
            EGB = t_("egb", [128, 4, 128])
            DD = PEX
            D1 = IP
            D2 = KRAW
            EE = D1
            ET = D2
            XE = D1
            XTE = SQ
            AE = D2
            ATB = t_("atb", [128, 4, 128], BF16)
            XH = t_("xh", [128, 4, 128], BF16)
            XL = t_("xl", [128, 4, 128], BF16)
            RRT = t_("rrt", [128, 4, 128], BF16)
            RRN = t_("rrn", [128, 4, 128], BF16)
            XF = t_("xf", [128, 4, 128])
            VBD = t_("vbd", [128, 4, 128], BF16)
            KBG = t_("kbg", [128, 4, 128], BF16)
            KE = t_("ke", [128, 4, 128], BF16)
            QGT = t_("qgt", [128, 4, 128], BF16)
            USB = t_("usb", [128, 4, 128])
            WTB = t_("wtb", [128, 4, 128], BF16)
            VN = t_("vn", [128, 4, 128], BF16)
            SF = t_("sf", [128, 4, 128])
            SBF = t_("sbf", [128, 4, 128], BF16)
            OSQ = t_("osq", [128, 4, 128])
            ON = t_("on", [128, 4, 128])
            SZ = t_("sz", [128, 4, 128])

            MSLN4 = CST[:, C_MSLN4:C_MSLN4 + 512].rearrange("p (a b) -> p a b", a=4)
            MA = CST[:, C_MA:C_MA + 1024].rearrange("p (a b c) -> p a b c", a=4, b=2)
            MB = CST[:, C_MB:C_MB + 1024].rearrange("p (a b c) -> p a b c", a=4, b=2)
            BD1 = CST[:, C_BD1:C_BD1 + 128]
            ONES = CST[:, C_ONES:C_ONES + 128]
            NMSU4 = CST[:, C_NMSU4:C_NMSU4 + 512].rearrange("p (a b) -> p a b", a=4)
            MIU4 = CST[:, C_MIU4:C_MIU4 + 512].rearrange("p (a b) -> p a b", a=4)
            ID4 = t_("id4", [128, 4, 128], BF16)
            for a in range(4):
                S.cp(V, ID4[:, a, :], IDF, KC, [ID4])
            S.ms(V, ONE4[:], 1.0, [ONE4])
            S.ms(G, KDZ[:], 0.0, [KDZ])
            S.ms(G, BDZ[:], 0.0, [BDZ])
            S.ms(G, H2Z[:], 0.0, [H2Z])
            S.ms(G, GB[:], 0.0, [GB])
            for i in range(2):
                S.ms(G, STG[i][:, 33, :], 0.0, [STG[i]])

            def b4(pb):
                return pb[:, :].rearrange("p (a b) -> p a b", a=4)

            def doubling(xb, xtb, rt, bS, bY, bYT, nlev=7):
                S.tt(G, rt[:], xtb[:], ID4[:], ALU.add, [xtb, ID4], [rt])
                for lvl in range(nlev):
                    last = lvl == nlev - 1
                    if lvl >= 1:
                        for h in range(4):
                            S.mm(bS[:, h * 128:(h + 1) * 128], IDB[:], rt[:, h, :], True, False, [rt], [bS])
                            S.mm(bS[:, h * 128:(h + 1) * 128], xb[:, h, :], rt[:, h, :], False, True, [xb, rt], [bS])
                    if not last:
                        for h in range(4):
                            S.mm(bY[:, h * 128:(h + 1) * 128], xtb[:, h, :], xb[:, h, :], True, True, [xtb, xb], [bY])
                        if lvl < nlev - 2:
                            for h in range(4):
                                S.mm(bYT[:, h * 128:(h + 1) * 128], xb[:, h, :], xtb[:, h, :], True, True, [xtb, xb], [bYT])
                    S.cut()
                    if lvl >= 1:
                        S.cp(A if lvl % 2 == 0 else V, rt[:], b4(bS), [bS], [rt])
                    if not last:
                        S.cp(V if lvl % 2 == 0 else A, xb[:], b4(bY), [bY], [xb])
                        if lvl < nlev - 2:
                            S.cp(A, xtb[:], b4(bYT), [bYT], [xtb])
                    S.cut()

            for hf in range(NH if stop >= 2 else 0):
                S.dma(PV[:], pv_d[hf, :, :], (), [PV])
                S.dma(W2f[:, 0:8, :].rearrange("p (a b) c -> p a (b c)", a=2), w2a2_d[hf, :, :, :], (), [W2f])
                S.cp(V, W2b[:], W2f[:, 0:8, :].rearrange("p (a b) c -> p a (b c)", a=2), [W2f], [W2b])
                S.ts(V, OMM[:], PV[:, 0:13], -1.0, ALU.mult, [PV], [OMM], 1.0, ALU.add)
                S.ts(V, OMKA[:], PV[:, 25:29], -1.0, ALU.mult, [PV], [OMKA], 1.0, ALU.add)
                S.act(NA[:], PV[0:64, 90:91], AF.Exp, [PV], [NA])
                S.ts(V, NA[:], NA[:], -1.0, ALU.mult, [NA], [NA])
                for cc in range(13):
                    S.ts(V, MUF[:, cc, :], ONE4[:], PV[:, cc:cc + 1], ALU.mult, [ONE4, PV], [MUF])
                for g in range(4):
                    S.ts(V, KKF[:, g, :], ONE4[:], PV[:, 21 + g:22 + g], ALU.mult, [ONE4, PV], [KKF])
                    S.ts(V, KAF[:, g, :], ONE4[:], PV[:, 25 + g:26 + g], ALU.mult, [ONE4, PV], [KAF])
                    S.ts(V, OMKAF[:, g, :], ONE4[:], OMKA[:, g:g + 1], ALU.mult, [ONE4, OMKA], [OMKAF])
                S.ms(V, H2F[:], 0.0, [H2F])
                S.ms(V, SF[:], 0.0, [SF])
                S.ms(G, SBF[:], 0.0, [SBF])
                MU = lambda cc: PV[:, cc:cc + 1]
                W0 = lambda g: PV[:, 13 + g:14 + g]
                A0 = lambda g: PV[:, 17 + g:18 + g]
                K_K = lambda g: PV[:, 21 + g:22 + g]
                K_A = lambda g: PV[:, 25 + g:26 + g]
                R_K = lambda g: PV[:, 29 + g:30 + g]
                GNW = lambda g: PV[:, 33 + g:34 + g]
                GNB = lambda g: PV[:, 37 + g:38 + g]
                CWT = lambda cj, tap: PV[:, 41 + 4 * cj + tap:42 + 4 * cj + tap]
                DNW = PV[:, 89:90]

                def load_chunk(c):
                    stg = STG[c % 2]
                    t0 = c * 128
                    src = pT_d[hf].rearrange("(cc p) t -> p cc t", p=128)
                    if c == 0:
                        S.ms(G, stg[:, :, 0:3], 0.0, [stg])
                        lo, dst0 = 0, 3
                    else:
                        lo, dst0 = t0 - 3, 0
                    for (a, b) in ((0, 9), (9, 17), (17, 25), (25, 33)):
                        S.dma(stg[:, a:b, dst0:131], src[:, a:b, lo:t0 + 128], (), [stg])
                    S.dma(stg[0:LASTM, 33, dst0:131], pT_d[hf, 33 * 128:33 * 128 + LASTM, lo:t0 + 128], (), [stg])

                def rw_pre(ci):
                    stg = STG[ci % 2]
                    BON = BONs[ci % 2]
                    SGT = SGTs[ci % 2]
                    BM, FMQ, KET, BET, VT = BMs[ci % 2], FMQs[ci % 2], KETs[ci % 2], BETs[ci % 2], VTs[ci % 2]
                    S.tt(G, TMP[:], stg[:, 0:13, 2:130], stg[:, 0:13, 3:131], ALU.subtract, [stg], [TMP])
                    S.tt(G, TMP[:], TMP[:], MUF[:], ALU.mult, [TMP, MUF], [TMP])
                    S.tt(V, XS[:], stg[:, 0:13, 3:131], TMP[:], ALU.add, [stg, TMP], [XS])
                    XR = XS[:, 0:4, :]
                    XK = XS[:, 4:8, :]
                    XV = XS[:, 8:12, :]
                    xr_t = [(XS, i) for i in range(0, 4)]
                    xk_t = [(XS, i) for i in range(4, 8)]
                    xv_t = [(XS, i) for i in range(8, 12)]
                    S.cut()
                    S.act(TL[0:64, :], XS[0:64, 12, :], AF.Tanh, [(XS, 12)], [TL])
                    S.cp(A, TL[64:128, :], XS[64:128, 12, :], [(XS, 12)], [TL])
                    for g in range(4):
                        S.mm(PB[6][:, g * 128:(g + 1) * 128], W2b[:, 0, g * 128:(g + 1) * 128], TL[:, :], True, True,
                             [W2b, TL], [PB[6]])
                    S.cut()
                    for g in range(4):
                        S.act(SG[:, g, :], PB[6][:, g * 128:(g + 1) * 128], AF.Sigmoid, [PB[6], PV], [SG], bias=W0(g))
                    for g in range(4):
                        S.mm(PB[6][:, g * 128:(g + 1) * 128], W2b[:, 1, g * 128:(g + 1) * 128], TL[:, :], True, True,
                             [W2b, TL], [PB[6]])
                    for g in range(4):
                        S.act(AA[:, g, :], PB[6][:, g * 128:(g + 1) * 128], AF.Sigmoid, [PB[6], PV], [AA], bias=A0(g))
                    S.cut()
                    for g in range(4):
                        S.op(V, lambda g=g: nc.vector.tensor_tensor_scan(out=CUM[:, g, :], data0=ONE4[:], data1=SG[:, g, :],
                                                                        initial=0.0, op0=ALU.mult, op1=ALU.add),
                             [SG, ONE4], [CUM])
                    S.ts(V, BM[:, 0:4], CUM[:, :, 63], CW, ALU.mult, [CUM], [BM])
                    S.ts(V, BM[:, 4:8], CUM[:, :, 63], -CW, ALU.mult, [CUM], [BM])
                    S.tt(V, CME[:], CUM[:], SG[:], ALU.subtract, [CUM, SG], [CME])
                    S.cut()
                    for g in range(4):
                        S.act(PIN[:, g, :], CUM[:, g, :], AF.Exp, [CUM, BM], [PIN], bias=BM[:, g:g + 1], scale=-CW)
                    for g in range(4):
                        S.act(IP[:, g, :], CUM[:, g, :], AF.Exp, [CUM, BM], [IP], bias=BM[:, 4 + g:5 + g], scale=CW)
                    S.cut()
                    for g in range(4):
                        S.act(PEX[:, g, :], CME[:, g, :], AF.Exp, [CME, BM], [PEX], bias=BM[:, g:g + 1], scale=-CW)
                    S.act(BM[:, 8:12], BM[:, 0:4], AF.Exp, [BM], [BM], scale=-1.0)
                    S.tt(V, BM[:, 12:16], BM[:, 8:12], PIN[:, :, 127], ALU.mult, [BM, PIN], [BM])
                    S.cut()
                    S.tt(G, KRAW[:], XK, KKF[:], ALU.mult, xk_t + [KKF], [KRAW])
                    S.tt(G, SQ[:], KRAW[:], KRAW[:], ALU.mult, [KRAW], [SQ])
                    S.mm(PB[6][:, :], BD1, SQ[:, :, :].rearrange("p a b -> p (a b)"), True, True, [SQ], [PB[6]])
                    S.act(RN[:], b4(PB[6]), AF.Ln, [PB[6]], [RN], bias=1e-6)
                    S.act(RN[:], RN[:], AF.Exp, [RN], [RN], scale=-0.5)
                    S.cut()
                    S.tt(V, KK[:], KRAW[:], RN[:], ALU.mult, [KRAW, RN], [KK])
                    S.tt(G, KF[:], AA[:], KAF[:], ALU.mult, [AA, KAF], [KF])
                    S.tt(G, KF[:], KF[:], OMKAF[:], ALU.add, [KF, OMKAF], [KF])
                    S.tt(V, KP[:], XK, KF[:], ALU.mult, xk_t + [KF], [KP])
                    S.tt(G, BB[:], KK[:], AA[:], ALU.mult, [KK, AA], [BB])
                    S.cut()
                    S.tt(V, FMQ[:, :, 0, :], KK[:], PEX[:], ALU.mult, [KK, PEX], [FMQ])
                    S.tt(V, FMQ[:, :, 1, :], XR, PIN[:], ALU.mult, xr_t + [PIN], [FMQ])
                    S.tt(V, KDF[:], KP[:], IP[:], ALU.mult, [KP, IP], [KDF])
                    S.tt(G, BDF[:], BB[:], IP[:], ALU.mult, [BB, IP], [BDF])
                    S.cut()
                    for hl in range(2):
                        ps = slice(hl * 64, hl * 64 + 64)
                        S.cp(A, KDZ[ps, :, hl, :], KDF[ps, :, :], [KDF], [KDZ])
                        S.cp(A, BDZ[ps, :, hl, :], BDF[ps, :, :], [BDF], [BDZ])
                    for g in range(4):
                        S.act(KEB[:, g, :], KDF[:, g, :], AF.Copy, [KDF, PIN], [KEB], scale=PIN[:, g, 127:128])
                    for g in range(4):
                        S.act(BEB_[:, g, :], BDF[:, g, :], AF.Copy, [BDF, PIN], [BEB_], scale=PIN[:, g, 127:128])
                    S.cut()
                    S.cp(A, VB_[:], XV, xv_t, [VB_])
                    for (src, dst) in ((KEB, KET), (BEB_, BET), (VB_, VT)):
                        for g in range(4):
                            S.tr(PSB[:, 512 + g * 128:512 + (g + 1) * 128], src[:, g, :], IDB[:], [src], [(PSB, 1)])
                        S.cp(A, dst[:], PSB[:, 512:1024].rearrange("p (a b) -> p a b", a=4), [(PSB, 1)], [dst])
                        S.cut()
                    S.cut()
                    for g in range(4):
                        S.stt(V, RKR[:, g, :], XR[:, g, :], R_K(g), KP[:, g, :], ALU.mult, ALU.mult, xr_t + [PV, KP], [RKR])
                    S.mm(PB[6][:, :], BD1, RKR[:, :, :].rearrange("p a b -> p (a b)"), True, True, [RKR], [PB[6]])
                    S.tt(V, BON[:], b4(PB[6]), XV, ALU.mult, [PB[6]] + xv_t, [BON])
                    S.act(SGT[:], stg[:, 13:17, 3:131], AF.Silu, [stg], [SGT])


                load_chunk(0)
                rw_pre(0)
                for c in range(NT):
                    stg = STG[c % 2]
                    BON = BONs[c % 2]
                    SGT = SGTs[c % 2]
                    BM, FMQ, KET, BET, VT = BMs[c % 2], FMQs[c % 2], KETs[c % 2], BETs[c % 2], VTs[c % 2]
                    if c + 1 < NT:
                        load_chunk(c + 1)
                    cur = lambda cc: stg[:, cc, 3:131]
                    prv = lambda cc: stg[:, cc, 2:130]
                    L_amat, L_dbl, L_chain = [], [], []
                    for gi in range(2):
                        XTARB, AKKARK, ZB, NSA = XTARBs[gi], AKKARKs[gi], ZBs[gi], NSAs[gi]
                        bX, bXT, bAK = (PB[2], PB[3], PB[4]) if gi == 0 else (PB[6], PB[0], PB[1])
                        L_amat.append(S.begin())
                        for pl in range(2):
                            g = gi * 2 + pl
                            for hl in range(2):
                                ps = slice(hl * 64, hl * 64 + 64)
                                hh = pl * 2 + hl
                                S.mm(bX[:, hh * 128:(hh + 1) * 128], FMQ[:, g, 0, :], BDZ[:, g, hl, :], True, True,
                                     [FMQ, BDZ], [bX])
                        for hb in range(2):
                            for q in range(2):
                                hh = hb * 2 + q
                                g = gi * 2 + hh // 2
                                ps = slice((hh % 2) * 64, (hh % 2) * 64 + 64)
                                S.mm(bXT[:, q * 256:(q + 1) * 256], BDZ[:, g, hh % 2, :],
                                     FMQ[:, g, :, :].rearrange("p a b -> p (a b)"), True, True, [FMQ, BDZ], [bXT])
                                S.mm(bAK[:, q * 256:(q + 1) * 256], KDZ[:, g, hh % 2, :],
                                     FMQ[:, g, :, :].rearrange("p a b -> p (a b)"), True, True, [FMQ, KDZ], [bAK])
                            S.tt(V, XTARB[:, hb * 2:hb * 2 + 2, :, :],
                                 bXT[:, :].rearrange("p (a b c) -> p a b c", a=2, b=2), MA[:, 0:2, :, :], ALU.mult,
                                 [bXT], [XTARB])
                            S.tt(V, AKKARK[:, hb * 2:hb * 2 + 2, :, :],
                                 bAK[:, :].rearrange("p (a b c) -> p a b c", a=2, b=2), MB[:, 0:2, :, :], ALU.mult,
                                 [bAK], [AKKARK])
                        S.tt(V, XB[gi][:], b4(bX), MSLN4, ALU.mult, [bX], [XB[gi]])
                        S.cp(G, XTB[gi][:], XTARB[:, :, 0, :], [XTARB], [XTB[gi]])
                        S.cut()
                        L_dbl.append(S.begin())
                        if gi == 0:
                            doubling(XB[gi], XTB[gi], RT[gi], PB[4], PB[2], PB[3])
                        else:
                            doubling(XB[gi], XTB[gi], RT[gi], PB[6], PB[0], PB[1])
                        L_chain.append(S.begin())
                        for pl in range(2):
                            g = gi * 2 + pl
                            for hl in range(2):
                                ps = slice(hl * 64, hl * 64 + 64)
                                S.ts(V, H2Z[ps, g, hl, :], H2F[ps, g, :], BM[ps, 8 + g:9 + g], ALU.mult, [H2F, BM], [H2Z])
                        for hh in range(4):
                            g = gi * 2 + hh // 2
                            hl = hh % 2
                            ps = slice(hl * 64, hl * 64 + 64)
                            S.mm(PB[0][:, hh * 64:(hh + 1) * 64], FMQ[:, g, 0, :], H2Z[:, g, hl, :], True, False,
                                 [FMQ, H2Z], [PB[0]])
                            S.mm(PB[0][:, hh * 64:(hh + 1) * 64], AKKARK[:, hh, 0, :], VT[:, g, hl * 64:(hl + 1) * 64],
                                 False, True, [AKKARK, VT], [PB[0]])
                        S.cut()
                        S.cp(A, ZB[:], PB[0][:, 0:256].rearrange("p (a b) -> p a b", a=4), [PB[0]], [ZB])
                        for hh in range(4):
                            S.mm(PB[0][:, 256 + hh * 64:256 + (hh + 1) * 64], RT[gi][:, hh, :], ZB[:, hh, :], True, True,
                                 [RT[gi], ZB], [PB[0]])
                        S.cut()
                        S.ts(V, NSA[:], PB[0][:, 256:512].rearrange("p (a b) -> p a b", a=4), -1.0, ALU.mult,
                             [PB[0]], [NSA])
                        for hh in range(4):
                            g = gi * 2 + hh // 2
                            hl = hh % 2
                            ps = slice(hl * 64, hl * 64 + 64)
                            oy = PB[5][:, (gi * 4 + hh) * 64:(gi * 4 + hh + 1) * 64]
                            S.mm(oy, FMQ[:, g, 1, :], H2Z[:, g, hl, :], True, False, [FMQ, H2Z], [PB[5]])
                            S.mm(oy, AKKARK[:, hh, 1, :], VT[:, g, hl * 64:(hl + 1) * 64], False, False,
                                 [AKKARK, VT], [PB[5]])
                            S.mm(oy, XTARB[:, hh, 1, :], NSA[:, hh, :], False, True, [XTARB, NSA], [PB[5]])
                        for pl in range(2):
                            g = gi * 2 + pl
                            oh = PB[1][:, pl * 128:(pl + 1) * 128]
                            S.mm(oh, KET[:, g, :], VT[:, g, :], True, False, [KET, VT], [PB[1]])
                            S.mm(oh, BET[:, g, :], NSA[:, pl * 2:pl * 2 + 2, :].rearrange("p a b -> p (a b)"), False, True,
                                 [BET, NSA], [PB[1]])
                        S.cut()
                        for pl in range(2):
                            g = gi * 2 + pl
                            for hl in range(2):
                                ps = slice(hl * 64, hl * 64 + 64)
                                S.stt(V, H2F[ps, g, :], H2F[ps, g, :], BM[ps, 12 + g:13 + g],
                                      PB[1][ps, pl * 128 + hl * 64:pl * 128 + (hl + 1) * 64], ALU.mult, ALU.add,
                                      [H2F, BM, PB[1], H2Z], [H2F])
                        S.cut()
                    L_rwpost = S.begin()
                    S.cp(A, YSB[:], PB[5][:, :].rearrange("p (a b) -> p a b", a=8), [PB[5]], [YSB])
                    S.tt(G, YSQ[:], YSB[:], YSB[:], ALU.mult, [YSB], [YSQ])
                    S.op(V, lambda: nc.vector.tensor_reduce(out=GST[:, 0:8], in_=YSB[:], axis=AX.X, op=ALU.add), [YSB], [GST])
                    S.op(V, lambda: nc.vector.tensor_reduce(out=GST[:, 8:16], in_=YSQ[:], axis=AX.X, op=ALU.add), [YSQ], [GST])
                    S.ts(V, GST[:, 16:24], GST[:, 0:8], 1.0 / 64, ALU.mult, [GST], [GST])
                    S.tt(V, GST[:, 24:32], GST[:, 16:24], GST[:, 16:24], ALU.mult, [GST], [GST])
                    S.stt(V, GST[:, 32:40], GST[:, 8:16], 1.0 / 64, GST[:, 24:32], ALU.mult, ALU.subtract, [GST], [GST])
                    S.cut()
                    S.act(GST[:, 40:48], GST[:, 32:40], AF.Sqrt, [GST], [GST], bias=64e-5)
                    S.op(V, lambda: nc.vector.reciprocal(out=GST[:, 40:48], in_=GST[:, 40:48]), [GST], [GST])
                    for h in range(8):
                        S.ts(V, YN[:, h, :], YSB[:, h, :], GST[:, 16 + h:17 + h], ALU.subtract,
                             [YSB, GST], [YN], GST[:, 40 + h:41 + h], ALU.mult)
                    S.cut()
                    for g in range(4):
                        S.tr(PB[6][:, g * 128:(g + 1) * 128], YN[:, 2 * g:2 * g + 2, :].rearrange("p a b -> p (a b)"), IDF,
                             [YN], [PB[6]])
                    for g in range(4):
                        S.ts(V, Y2[:, g, :], PB[6][:, g * 128:(g + 1) * 128], GNW(g), ALU.mult, [PB[6], PV], [Y2],
                             GNB(g), ALU.add)
                    S.tt(G, Y2[:], Y2[:], BON[:], ALU.add, [Y2, BON], [Y2])
                    S.tt(V, YOB[:, 0:4, :], Y2[:], SGT[:], ALU.mult, [Y2, SGT], [YOB])

                    L_dnpre = S.begin()
                    for cj in range(12):
                        e = G if cj == 11 else V
                        cc = 17 + cj
                        S.ts(e, CV[:, cj, :], stg[:, cc, 0:128], CWT(cj, 0), ALU.mult, [stg, PV], [(CV, cj)])
                        for tap in range(1, 4):
                            if e is V:
                                S.stt(e, CV[:, cj, :], stg[:, cc, tap:tap + 128], CWT(cj, tap), CV[:, cj, :], ALU.mult, ALU.add,
                                      [stg, PV, (CV, cj)], [(CV, cj)])
                            else:
                                S.ts(G, TMPG[:], stg[:, cc, tap:tap + 128], CWT(cj, tap), ALU.mult, [stg, PV], [TMPG])
                                S.tt(G, CV[:, cj, :], CV[:, cj, :], TMPG[:], ALU.add, [TMPG, (CV, cj)], [(CV, cj)])
                        if cj % 2 == 1:
                            S.cut()
                    for j in range(3):
                        S.act(QKV[:, 4 * j:4 * j + 4, :], CV[:, 4 * j:4 * j + 4, :], AF.Silu,
                              [(CV, 4 * j + i) for i in range(4)], [QKV])
                    S.cut()
                    S.tt(G, SQD[:], QKV[:, 0:8, :], QKV[:, 0:8, :], ALU.mult, [QKV], [SQD])
                    S.mm(PB[5][:, :], ONES, SQD[:, 0:4, :].rearrange("p a b -> p (a b)"), True, True, [SQD], [PB[5]])
                    S.act(RND[:, 0:4, :], b4(PB[5]), AF.Ln, [PB[5]], [RND], bias=1e-6)
                    S.cut()
                    S.mm(PB[5][:, :], ONES, SQD[:, 4:8, :].rearrange("p a b -> p (a b)"), True, True, [SQD], [PB[5]])
                    S.act(RND[:, 4:8, :], b4(PB[5]), AF.Ln, [PB[5]], [RND], bias=1e-6)
                    S.cut()
                    S.act(RND[:], RND[:], AF.Exp, [RND], [RND], scale=-0.5)
                    S.cut()
                    S.tt(V, QKV[:, 0:8, :], QKV[:, 0:8, :], RND[:], ALU.mult, [QKV, RND], [QKV])
                    S.cp(A, KTB[:], QKV[:, 4:8, :], [QKV], [KTB])
                    S.cp(A, VTB[:], QKV[:, 8:12, :], [QKV], [VTB])
                    S.cut()
                    S.act(GE[0:32, :], stg[0:32, 33, 3:131], AF.Exp, [stg, PV], [GE], bias=PV[0:32, 91:92])
                    S.act(GE[0:32, :], GE[0:32, :], AF.Ln, [GE], [GE], bias=1.0)
                    S.ts(V, GE[0:32, :], GE[0:32, :], NA[0:32, 0:1], ALU.mult, [GE, NA], [GE])
                    S.op(V, lambda: nc.vector.tensor_tensor_scan(out=GB[0:32, :], data0=ONE4[0:32, :], data1=GE[0:32, :],
                                                                initial=0.0, op0=ALU.mult, op1=ALU.add), [GE, ONE4], [GB])
                    S.act(GB[32:64, :], stg[32:64, 33, 3:131], AF.Sigmoid, [stg], [GB])
                    S.tr(PB[5][:, 0:128], GB[:, :], IDF, [GB], [PB[5]])
                    S.cp(V, TM[:], PB[5][:, 0:64], [PB[5]], [TM])
                    S.cut()
                    for i in range(4):
                        S.mm(PB[5][:, i * 128:(i + 1) * 128], CST[:, C_SEL + i * 128:C_SEL + (i + 1) * 128], GB[:, :],
                             True, True, [GB], [PB[5]])
                    S.cp(A, GCB[:], b4(PB[5]), [PB[5]], [GCB])
                    S.cut()
                    for i in range(4):
                        S.mm(PB[5][:, i * 128:(i + 1) * 128], CST[:, C_SEL + (4 + i) * 128:C_SEL + (5 + i) * 128], GB[:, :],
                             True, True, [GB], [PB[5]])
                    S.cp(A, BEBC[:], b4(PB[5]), [PB[5]], [BEBC])
                    S.cut()
                    S.cut()
                    S.act(EGB[:], GCB[:], AF.Exp, [GCB], [EGB])
                    S.act(TMS[:, 0:4], TM[:, 0:4], AF.Exp, [TM], [TMS])
                    S.tt(V, TMS[:, 4:8], TMS[:, 0:4], TM[:, 32:36], ALU.mult, [TMS, TM], [TMS])
                    for h in range(4):
                        S.act(TMS[:, 8 + h:9 + h], TM[:, h:h + 1], AF.Exp, [TM, GCB], [TMS], bias=GCB[:, h, 127:128], scale=-1.0)
                    for h in range(4):
                        S.ts(V, DD[:, h, :], GCB[:, h, :], TM[:, h:h + 1], ALU.subtract, [GCB, TM], [DD])
                    S.cut()
                    S.ts(V, D1[:], DD[:], 0.0, ALU.max, [DD], [D1])
                    S.ts(V, D2[:], DD[:], 0.0, ALU.min, [DD], [D2])
                    S.act(EE[:], D1[:], AF.Exp, [D1], [EE], scale=-1.0)
                    S.act(ET[:], D2[:], AF.Exp, [D2], [ET])
                    for h in range(4):
                        S.stt(V, XE[:, h, :], EE[:, h, :], TM[:, 32 + h:33 + h], MSLN4[:, h, :], ALU.mult, ALU.mult,
                              [EE, TM], [XE])
                    S.cut()
                    S.tt(G, XTE[:], ET[:], BEBC[:], ALU.mult, [ET, BEBC], [XTE])
                    S.tt(G, XTE[:], XTE[:], NMSU4, ALU.mult, [XTE], [XTE])
                    S.tt(G, AE[:], ET[:], MIU4, ALU.mult, [ET], [AE])
                    S.cut()
                    S.act(QTB[:], QKV[:, 0:4, :], AF.Copy, [QKV], [QTB], scale=128.0 ** -0.5)
                    S.stt(V, QGT[:], QKV[:, 0:4, :], 128.0 ** -0.5, EGB[:], ALU.mult, ALU.mult, [QKV, EGB], [QGT])
                    L_dnamat = S.begin()
                    for h in range(4):
                        S.mm(PB[5][:, h * 128:(h + 1) * 128], KTB[:, h, :], KTB[:, h, :], True, True, [KTB], [PB[5]])
                    for h in range(4):
                        S.tr(PSB[:, h * 128:(h + 1) * 128], KTB[:, h, :], IDB[:], [KTB], [PSB])
                    for h in range(4):
                        S.tr(PSB[:, 512 + h * 128:512 + (h + 1) * 128], VTB[:, h, :], IDB[:], [VTB], [PSB])
                    S.cut()
                    S.tt(V, XF[:], b4(PB[5]), XE[:], ALU.mult, [PB[5], XE], [XF])
                    S.tt(V, XTBD[:], b4(PB[5]), XTE[:], ALU.mult, [PB[5], XTE], [XTBD])
                    S.cut()
                    for h in range(4):
                        S.mm(PB[5][:, h * 128:(h + 1) * 128], KTB[:, h, :], QTB[:, h, :], True, True, [KTB, QTB], [PB[5]])
                    S.cp(A, XH[:], XF[:], [XF], [XH])
                    S.tt(G, XL[:], XF[:], XH[:], ALU.subtract, [XF, XH], [XL])
                    S.cp(A, XBD[:], XH[:], [XH], [XBD])
                    S.cut()
                    S.tt(V, ATB[:], b4(PB[5]), AE[:], ALU.mult, [PB[5], AE], [ATB])
                    for h in range(4):
                        S.ts(V, KBG[:, h, :], PSB[:, h * 128:(h + 1) * 128], TMS[:, 4 + h:5 + h], ALU.mult, [PSB, TMS], [KBG])
                        S.ts(V, KE[:, h, :], PSB[:, h * 128:(h + 1) * 128], TMS[:, 8 + h:9 + h], ALU.mult, [PSB, TMS], [KE])
                        S.ts(V, VBD[:, h, :], PSB[:, 512 + h * 128:512 + (h + 1) * 128], TM[:, 32 + h:33 + h], ALU.mult,
                             [PSB, TM], [VBD])
                        if h % 2 == 1:
                            S.cut()
                    L_dndbl = S.begin()
                    doubling(XBD, XTBD, RTD, PB[4], PB[2], PB[3], nlev=6)
                    for h in range(4):
                        S.mm(PB[2][:, h * 128:(h + 1) * 128], XH[:, h, :], RTD[:, h, :], True, False, [XH, RTD], [PB[2]])
                        S.mm(PB[2][:, h * 128:(h + 1) * 128], XL[:, h, :], RTD[:, h, :], False, True, [XL, RTD], [PB[2]])
                    S.cut()
                    S.tt(V, XF[:], b4(PB[2]), RTD[:], ALU.subtract, [PB[2], RTD], [XF])
                    S.tt(G, RRT[:], XF[:], ID4[:], ALU.add, [XF], [RRT])
                    for h in range(4):
                        S.tr(PSB[:, h * 128:(h + 1) * 128], RRT[:, h, :], IDB[:], [RRT], [(PSB, 0)])
                    S.cut()
                    S.cp(A, RRN[:], PSB[:, 0:512].rearrange("p (a b) -> p a b", a=4), [(PSB, 0)], [RRN])
                    for h in range(4):
                        S.mm(PB[3][:, h * 128:(h + 1) * 128], RRN[:, h, :], RTD[:, h, :], True, True, [RRN, RTD], [PB[3]])
                    S.cut()
                    S.tt(V, RTD[:], b4(PB[3]), RTD[:], ALU.add, [PB[3], RTD], [RTD])
                    L_dntail = S.begin()
                    for h in range(4):
                        S.mm(PB[0][:, h * 128:(h + 1) * 128], RTD[:, h, :], VBD[:, h, :], True, True, [RTD, VBD], [PB[0]])
                    for h in range(4):
                        S.mm(PB[1][:, h * 128:(h + 1) * 128], KBG[:, h, :], RTD[:, h, :], True, True, [RTD, KBG], [PB[1]])
                    S.cut()
                    S.cp(A, USB[:], b4(PB[0]), [PB[0]], [USB])
                    S.cp(V, WTB[:], b4(PB[1]), [PB[1]], [WTB])
                    for h in range(4):
                        S.mm(PB[0][:, h * 128:(h + 1) * 128], WTB[:, h, :], SBF[:, h, :], True, True, [WTB, SBF], [PB[0]])
                    S.cut()
                    S.tt(V, VN[:], USB[:], b4(PB[0]), ALU.subtract, [USB, PB[0]], [VN])
                    for h in range(4):
                        S.mm(PB[1][:, h * 128:(h + 1) * 128], QGT[:, h, :], SBF[:, h, :], True, False, [QGT, SBF], [PB[1]])
                        S.mm(PB[1][:, h * 128:(h + 1) * 128], ATB[:, h, :], VN[:, h, :], False, True, [ATB, VN], [PB[1]])
                    for h in range(4):
                        S.mm(PB[0][:, h * 128:(h + 1) * 128], KE[:, h, :], VN[:, h, :], True, True, [KE, VN], [PB[0]])
                    for h in range(4):
                        S.stt(V, SF[:, h, :], SF[:, h, :], EGB[:, h, 127:128], PB[0][:, h * 128:(h + 1) * 128], ALU.mult, ALU.add,
                              [SF, EGB, PB[0]], [SF])
                    S.cp(A, SBF[:], SF[:], [SF], [SBF])
                    S.cut()
                    S.act(OSQ[:], b4(PB[1]), AF.Square, [PB[1]], [OSQ])
                    S.op(V, lambda: nc.vector.tensor_reduce(out=TMS[:, 12:16], in_=OSQ[:], axis=AX.X, op=ALU.add), [OSQ], [TMS])
                    S.act(TMS[:, 16:20], TMS[:, 12:16], AF.Sqrt, [TMS], [TMS], bias=1e-6, scale=1.0 / 128)
                    S.op(V, lambda: nc.vector.reciprocal(out=TMS[:, 20:24], in_=TMS[:, 16:20]), [TMS], [TMS])
                    for h in range(4):
                        S.act(ON[:, h, :], PB[1][:, h * 128:(h + 1) * 128], AF.Copy, [PB[1], TMS], [ON], scale=TMS[:, 20 + h:21 + h])
                    for h in range(4):
                        S.tr(PB[2][:, h * 128:(h + 1) * 128], ON[:, h, :], IDF, [ON], [PB[2]])
                    S.act(SZ[:], stg[:, 29:33, 3:131], AF.Silu, [stg], [SZ])
                    S.cut()
                    S.stt(V, YOB[:, 4:8, :], b4(PB[2]), DNW, SZ[:], ALU.mult, ALU.mult, [PB[2], PV, SZ], [YOB])
                    if c + 1 < NT:
                        L_pre = S.begin()
                        rw_pre(c + 1)
                    else:
                        L_pre = []
                    S.end()
                    S.play(L_amat[0] + L_dbl[0], L_amat[1] + L_dbl[1], L_dnpre + L_dnamat)
                    S.play(L_chain[0] + L_chain[1], L_dndbl, L_pre)
                    S.play(L_rwpost, L_dntail)
                    if coll:
                        for k in range(8):
                            S.dma(yTk[k].ap()[:, c * 128:(c + 1) * 128], YOB[:, k, :], [YOB], ())
                    else:
                        S.dma(yT_d[hf * 8:(hf + 1) * 8, :, c * 128:(c + 1) * 128].rearrange("k p t -> p k t"), YOB[:], [YOB], ())
        S.barrier()
        CCT = Tok()

        NK = NKH * 8
        with contextlib.ExitStack() as ph:
            WOB = sb(ph, "wob", [128, NK, D], BF16)
            WOS = [sb(ph, "wos%d" % i, [128, D], F32) for i in range(2)]
            FNW = sb(ph, "fnw", [128, D], F32)
            YT = [sb(ph, "yt%d" % i, [128, NK, 512 if coll else 128], BF16) for i in range(2)]
            XR3 = [sb(ph, "xr3_%d" % i, [128, D], F32) for i in range(2)]
            HS = sb(ph, "hs", [128, D], F32)
            JK = sb(ph, "jk", [128, D], BF16)
            OT = [sb(ph, "ot%d" % i, [128, D], F32) for i in range(2)]
            ST3 = [sb(ph, "st3_%d" % i, [128, 4], F32) for i in range(2)]
            S.dma(FNW[:], fnw_d[:, :], (), [FNW])
            for kc in range(NK if stop >= 3 else 0):
                wos = WOS[kc % 2]
                S.dma(wos[:], wout_d[kc * 128:(kc + 1) * 128, :], (), [wos])
                S.cp(V if kc % 2 == 0 else A, WOB[:, kc, :], wos[:], [wos], [WOB])
            if coll and stop >= 3:
                ccsem = es.enter_context(nc.semaphore("s_cc"))
                with nc.Block() as block:
                    @block.gpsimd
                    def _(g):
                        for k in range(8):
                            g.collective_compute("AllGather", ALU.bypass, replica_groups=[[0, 1], [2, 3], [4, 5], [6, 7]],
                                                 ins=[yTk[k].ap().opt()], outs=[yAk[k].ap().opt()]).then_inc(ccsem)
                            g.wait_ge(ccsem, k + 1)
                CCD = sb(ph, "ccd", [128, 8], F32)
                S.ms(G, CCD[:], 0.0, [CCD, CCT])

            for i in range(1, NT if stop >= 3 else 0):
                yt, xr, ot, st3 = YT[i % 2], XR3[i % 2], OT[i % 2], ST3[i % 2]
                if coll:
                    st_i = (i - 1) // 4
                    yt = YT[st_i % 2]
                    if (i - 1) % 4 == 0:
                        n3 = min(512, T - i * 128)
                        for k in range(8):
                            for r in range(2):
                                S.dma(yt[:, r * 8 + k, 0:n3], yAk[k].ap()[r * 128:(r + 1) * 128, i * 128:i * 128 + n3], [CCT], [yt])
                    yoff = ((i - 1) % 4) * 128
                else:
                    yoff = 0
                    S.dma(yt[:], yA_d[:, :, i * 128:(i + 1) * 128].rearrange("k p t -> p k t"), [CCT], [yt])
                S.dma(xr[:], x_d[(i - 1) * 128:i * 128, :], (), [xr])
                for nb in range(4):
                    for kc in range(NK):
                        S.mm(PB[nb][:, :], yt[:, kc, yoff:yoff + 128], WOB[:, kc, nb * 512:(nb + 1) * 512], kc == 0, kc == NK - 1,
                             [yt, WOB], [PB[nb]])
                for nb in range(4):
                    S.tt(V, HS[:, nb * 512:(nb + 1) * 512], PB[nb][:, :], xr[:, nb * 512:(nb + 1) * 512], ALU.add,
                         [PB[nb], xr], [HS])
                S.ms(V, st3[:, 0:1], 0.0, [st3])
                S.act(JK[:], HS[:], AF.Square, [HS], [JK, st3], accum_out=st3[:, 0:1])
                S.act(st3[:, 1:2], st3[:, 0:1], AF.Sqrt, [st3], [st3], bias=1e-6, scale=1.0 / D)
                S.op(V, lambda: nc.vector.reciprocal(out=st3[:, 2:3], in_=st3[:, 1:2]), [st3], [st3])
                S.stt(V, ot[:], HS[:], st3[:, 2:3], FNW[:], ALU.mult, ALU.mult, [HS, st3, FNW], [ot])
                S.dma(out_d[(i - 1) * 128:i * 128, :], ot[:], [ot], ())
        S.finish()
        print('MARKS', S.marks[:40], 'total', S.nops, [e.n for e in S.engs])
    return nc


def _consts():
    c = np.zeros((128, C_END), np.float32)
    p = np.arange(128)[:, None]
    f = np.arange(128)[None, :]
    ident = (p == f).astype(np.float32)
    msl_neg = -(p > f).astype(np.float32)
    msu = (f > p).astype(np.float32)
    miu = (f >= p).astype(np.float32)
    c[:, C_ID:C_ID + 128] = ident
    c[:, C_MSLN4:C_MSLN4 + 512] = np.tile(msl_neg, (1, 4))
    c[:, C_MA:C_MA + 1024] = np.tile(np.concatenate([-msu, miu], 1), (1, 4))
    c[:, C_MB:C_MB + 1024] = np.tile(np.concatenate([msu, miu], 1), (1, 4))
    c[:, C_BD1:C_BD1 + 128] = ((p // 64) == (f // 64)).astype(np.float32)
    c[:, C_ONES:C_ONES + 128] = 1.0
    rows = [0, 1, 2, 3, 32, 33, 34, 35]
    for i, r in enumerate(rows):
        c[r, C_SEL + i * 128:C_SEL + (i + 1) * 128] = 1.0
    c[:, C_NMSU4:C_NMSU4 + 512] = np.tile(-msu, (1, 4))
    c[:, C_MIU4:C_MIU4 + 512] = np.tile(miu, (1, 4))
    return c


def _half_layout(hh, w_in, mu, w0, w2, a0, a2, k_k, k_a, r_k, gn_w, gn_b, conv_w, A_log, dt_bias, dn_norm_w, w_out):
    o = 512 * hh
    ar = np.arange(512)
    cols = np.concatenate([o + ar, 1024 + o + ar, 2048 + o + ar, 3072 + np.arange(128), 3200 + o + ar,
                           4224 + o + ar, 5248 + o + ar, 6272 + o + ar, 7312 + o + ar])
    wl = np.zeros((D, NCOL), np.float32)
    wl[:, :33 * 128] = w_in[:, cols]
    wl[:, 33 * 128:33 * 128 + 4] = w_in[:, 7304 + 4 * hh:7304 + 4 * hh + 4]
    wl[:, 33 * 128 + 32:33 * 128 + 36] = w_in[:, 7296 + 4 * hh:7296 + 4 * hh + 4]
    pv = np.zeros((128, NPV), np.float32)
    pv[:, 0:13] = mu[cols[:13 * 128]].reshape(13, 128).T
    loc = lambda v: v[o:o + 512].reshape(4, 128).T
    pv[:, 13:17] = loc(w0)
    pv[:, 17:21] = loc(a0)
    pv[:, 21:25] = loc(k_k)
    pv[:, 25:29] = loc(k_a)
    pv[:, 29:33] = loc(r_k)
    pv[:, 33:37] = loc(gn_w)
    pv[:, 37:41] = loc(gn_b)
    for cj in range(12):
        base = (cj // 4) * 1024 + o + (cj % 4) * 128
        pv[:, 41 + 4 * cj:45 + 4 * cj] = conv_w[:, base:base + 128].T
    pv[:, 89] = dn_norm_w
    pv[0:4, 90] = A_log[4 * hh:4 * hh + 4]
    pv[0:4, 91] = dt_bias[4 * hh:4 * hh + 4]
    w2a2 = np.zeros((128, 2, 512), np.float32)
    w2a2[0:64, 0] = w2[:, o:o + 512]
    w2a2[64:128, 1] = a2[:, o:o + 512]
    wo = np.concatenate([w_out[o:o + 512], w_out[1024 + o:1024 + o + 512]], 0)
    return wl, pv, np.ascontiguousarray(w2a2), np.ascontiguousarray(wo)


_NC_CACHE = {}


def kernel(x, meta_tokens, norm_w, w_in, rw_shift_mu, rw_w0, rw_w2, rw_a0, rw_a2, rw_k_k, rw_k_a, rw_r_k,
           rw_gn_w, rw_gn_b, dn_conv_w, dn_A_log, dn_dt_bias, dn_norm_w, w_out, final_norm_w, _nt=33, _cores=8, _stop=9, _nh=2, _maxops=10 ** 9, _dbg=False, _b0=0):
    f = lambda a: np.asarray(a, np.float32)
    x = f(x)
    NT = _nt
    coll = (_cores == 8) and not _dbg
    NH = 1 if coll else _nh
    halves = [_half_layout(hh, f(w_in)[0], f(rw_shift_mu)[0], f(rw_w0)[0], f(rw_w2)[0], f(rw_a0)[0], f(rw_a2)[0],
                           f(rw_k_k)[0], f(rw_k_a)[0], f(rw_r_k)[0], f(rw_gn_w)[0], f(rw_gn_b)[0], f(dn_conv_w)[0],
                           f(dn_A_log)[0], f(dn_dt_bias)[0], f(dn_norm_w)[0], f(w_out)[0]) for hh in range(2)]
    shared = {
        "meta": f(meta_tokens),
        "normw": np.ascontiguousarray(np.broadcast_to(f(norm_w)[0][None], (128, D))),
        "fnormw": np.ascontiguousarray(np.broadcast_to(f(final_norm_w)[None], (128, D))),
        "consts": _consts(),
    }
    sel = (lambda hh: [hh]) if coll else (lambda hh: list(range(NH)))
    key = (NT, NH, _stop, _maxops, _dbg, coll)
    if key not in _NC_CACHE:
        _NC_CACHE[key] = build(NT, NH, _stop, _maxops, _dbg, coll)
    nc = _NC_CACHE[key]
    NX = (NT - 1) * 128
    wo_halves = list(range(2)) if coll else list(range(NH))
    w_out_l = np.concatenate([halves[h][3] for h in wo_halves], 0)
    in_maps = []
    for c in range(_cores):
        b = (_b0 + c // 2) % x.shape[0]
        hs = sel(c % 2)
        m = dict(shared)
        m["x"] = np.ascontiguousarray(x[b, :NX])
        m["w_in_l"] = np.stack([halves[h][0] for h in hs])
        m["pv"] = np.stack([halves[h][1] for h in hs])
        m["w2a2"] = np.stack([halves[h][2] for h in hs])
        m["w_out_l"] = w_out_l
        in_maps.append(m)
    res = run_bass_kernel_spmd(nc, in_maps, core_ids=list(range(_cores)))
    if _dbg:
        return res.results[0]["out"][None], res.results[0]["yT_s"]
    outs = [res.results[2 * b]["out"] for b in range(x.shape[0])] if _cores == 8 else [res.results[0]["out"]]
    return np.stack(outs, 0).astype(np.float32)
```

```python
import contextlib
import numpy as np
import ml_dtypes
import concourse.bass as bass
import concourse.mybir as mybir
from concourse.bass_utils import run_bass_kernel_spmd

F32 = mybir.dt.float32
BF16 = mybir.dt.bfloat16
AF = mybir.ActivationFunctionType
ALU = mybir.AluOpType
AX = mybir.AxisListType

D = 2048
SEQ = 4096
NMETA = 16
NCH = 34
NCOL = 33 * 128 + 36
LASTM = 36
NPV = 92
CW = 0.6065306597126334
C_ID, C_MSLN4, C_MA, C_MB, C_BD1, C_ONES, C_SEL, C_NMSU4, C_MIU4, C_END = (
    0, 128, 640, 1664, 2688, 2816, 2944, 3968, 4480, 4992)


class Tile:
    def __init__(self, t, ntok=1):
        self.t = t
        self.k = [Tok() for _ in range(ntok)]

    def __getitem__(self, idx):
        return self.t[idx]


class Tok:
    __slots__ = ("w", "rs")

    def __init__(self):
        self.w = {}
        self.rs = {}


class Eng:
    def __init__(self, h, sem, is_pe=False):
        self.h = h
        self.sem = sem
        self.n = 0
        self.known = {}
        self.is_pe = is_pe


def _toks(lst):
    out = []
    for x in lst:
        if isinstance(x, Tile):
            out.extend(x.k)
        elif isinstance(x, Tok):
            out.append(x)
        else:
            t, i = x
            out.append(t.k[i])
    return out


class Sch:
    NS = 16

    def __init__(self, nc, es):
        self.nc = nc
        sem = lambda n: es.enter_context(nc.semaphore(n))
        self.P = Eng(nc.tensor, sem("s_pe"), True)
        self.V = Eng(nc.vector, sem("s_dve"))
        self.A = Eng(nc.scalar, sem("s_act"))
        self.G = Eng(nc.gpsimd, sem("s_pool"))
        self.Q = Eng(nc.sync, sem("s_sp"))
        self.engs = [self.P, self.V, self.A, self.G, self.Q]
        self.dsems = [sem("s_dma%d" % i) for i in range(self.NS)]
        self.dcnt = [0] * self.NS
        self.dn = 0
        self.nops = 0
        self.maxops = 10 ** 9
        self.marks = []
        self.rec = None

    def _wait(self, eng, sem, val):
        k = id(sem)
        if eng.known.get(k, 0) >= val:
            return
        eng.h.wait_ge(sem, val)
        eng.known[k] = val

    def _deps(self, eng, reads, writes, is_dma):
        need = {}

        def add(rec, raw):
            sem, val, src = rec
            if (not is_dma) and src is eng:
                if eng.is_pe or not raw:
                    return
            k = id(sem)
            if k not in need or need[k][1] < val:
                need[k] = (sem, val)

        for t in reads:
            for rec in t.w.values():
                add(rec, True)
        for t in writes:
            for rec in t.w.values():
                add(rec, False)
            for rec in t.rs.values():
                add(rec, False)
        for sem, val in need.values():
            self._wait(eng, sem, val)

    def _record(self, rec, reads, writes):
        k = id(rec[0])
        for t in reads:
            t.rs[k] = rec
        for t in writes:
            t.w[k] = rec
            t.rs = {}

    def mark(self, name):
        self.marks.append((name, self.nops))

    def begin(self):
        self.rec = []
        return self.rec

    def end(self):
        self.rec = None

    def cut(self):
        if self.rec is not None:
            self.rec.append(None)

    def play(self, *lists):
        its = [list(l) for l in lists]
        pos = [0] * len(its)
        live = True
        while live:
            live = False
            for i, l in enumerate(its):
                while pos[i] < len(l):
                    item = l[pos[i]]
                    pos[i] += 1
                    if item is None:
                        break
                    if item[0] == "op":
                        self.op(*item[1:])
                    else:
                        self.dma(*item[1:])
                if pos[i] < len(l):
                    live = True

    def op(self, eng, fn, R=(), W=()):
        if self.rec is not None:
            self.rec.append(("op", eng, fn, R, W))
            return
        self.nops += 1
        if self.nops > self.maxops:
            return
        reads = _toks(R)
        writes = _toks(W)
        self._deps(eng, reads, writes, False)
        inst = fn()
        eng.n += 1
        inst.then_inc(eng.sem, 1)
        self._record((eng.sem, eng.n, eng), reads, writes)

    def dma(self, out, in_, R=(), W=()):
        if self.rec is not None:
            self.rec.append(("dma", out, in_, R, W))
            return
        self.nops += 1
        if self.nops > self.maxops:
            return
        reads = _toks(R)
        writes = _toks(W)
        q = self.Q
        self._deps(q, reads, writes, True)
        i = self.dn % self.NS
        self.dn += 1
        sem = self.dsems[i]
        if self.dcnt[i] > 0:
            self._wait(q, sem, 16 * self.dcnt[i])
        self.dcnt[i] += 1
        q.h.dma_start(out=out, in_=in_).then_inc(sem, 16)
        self._record((sem, 16 * self.dcnt[i], None), reads, writes)

    def barrier(self):
        for e in self.engs:
            for f in self.engs:
                if f is not e and f.n > 0:
                    self._wait(e, f.sem, f.n)
            for i in range(self.NS):
                if self.dcnt[i] > 0:
                    self._wait(e, self.dsems[i], 16 * self.dcnt[i])

    def finish(self):
        q = self.Q
        for f in self.engs:
            if f is not q and f.n > 0:
                self._wait(q, f.sem, f.n)
        for i in range(self.NS):
            if self.dcnt[i] > 0:
                self._wait(q, self.dsems[i], 16 * self.dcnt[i])

    def mm(self, out, lhsT, rhs, start, stop, R, W):
        self.op(self.P, lambda: self.nc.tensor.matmul(out, lhsT=lhsT, rhs=rhs, start=start, stop=stop), R, W)

    def tr(self, out, in_, ident, R, W):
        self.op(self.P, lambda: self.nc.tensor.transpose(out, in_, ident), R, W)

    def act(self, out, in_, func, R, W, bias=0.0, scale=1.0, accum_out=None):
        if accum_out is None:
            self.op(self.A, lambda: self.nc.scalar.activation(out=out, in_=in_, func=func, bias=bias, scale=scale), R, W)
        else:
            self.op(self.A, lambda: self.nc.scalar.activation(out=out, in_=in_, func=func, bias=bias, scale=scale,
                                                              accum_out=accum_out), R, W)

    def tt(self, e, out, in0, in1, op, R, W):
        self.op(e, lambda: e.h.tensor_tensor(out=out, in0=in0, in1=in1, op=op), R, W)

    def ts(self, e, out, in0, s1, op0, R, W, s2=None, op1=None):
        if op1 is None:
            self.op(e, lambda: e.h.tensor_scalar(out=out, in0=in0, scalar1=s1, scalar2=None, op0=op0), R, W)
        else:
            self.op(e, lambda: e.h.tensor_scalar(out=out, in0=in0, scalar1=s1, scalar2=s2, op0=op0, op1=op1), R, W)

    def stt(self, e, out, in0, scalar, in1, op0, op1, R, W):
        self.op(e, lambda: e.h.scalar_tensor_tensor(out=out, in0=in0, scalar=scalar, in1=in1, op0=op0, op1=op1), R, W)

    def cp(self, e, out, in_, R, W):
        if e is self.A:
            self.op(e, lambda: self.nc.scalar.activation(out=out, in_=in_, func=AF.Copy), R, W)
        else:
            self.op(e, lambda: e.h.tensor_copy(out=out, in_=in_), R, W)

    def ms(self, e, ap, val, W):
        self.op(e, lambda: e.h.memset(ap, val), (), W)


def build(NT, NH, stop=9, maxops=10 ** 9, dbg=False, coll=False):
    T = NT * 128
    NX = (NT - 1) * 128
    nc = bass.Bass("TRN2", target_bir_lowering=False)
    dt = lambda name, shape, dtype, kind: nc.dram_tensor(name, shape, dtype, kind=kind).ap()
    x_d = dt("x", [NX, D], F32, "ExternalInput")
    meta_d = dt("meta", [NMETA, D], F32, "ExternalInput")
    nw_d = dt("normw", [128, D], F32, "ExternalInput")
    fnw_d = dt("fnormw", [128, D], F32, "ExternalInput")
    win_d = dt("w_in_l", [NH, D, NCOL], F32, "ExternalInput")
    pv_d = dt("pv", [NH, 128, NPV], F32, "ExternalInput")
    w2a2_d = dt("w2a2", [NH, 128, 2, 512], F32, "ExternalInput")
    NKH = 2 if coll else NH
    wout_d = dt("w_out_l", [NKH * 1024, D], F32, "ExternalInput")
    cst_d = dt("consts", [128, C_END], F32, "ExternalInput")
    out_d = dt("out", [NX, D], F32, "ExternalOutput")
    uT_d = dt("uT_s", [16, 128, T], BF16, "Internal")
    pT_d = dt("pT_s", [NH, NCH * 128, T], F32, "Internal")
    yT_t = nc.dram_tensor("yT_s", [NH * 8 * 128, T], BF16, kind="ExternalOutput" if dbg else "Internal")
    yT_d = yT_t.ap().rearrange("(k p) t -> k p t", p=128)
    if coll:
        yTk = [nc.dram_tensor("yT_k%d" % k, [128, T], BF16, kind="Internal") for k in range(8)]
        yAk = [nc.dram_tensor("yA_k%d" % k, [2 * 128, T], BF16, kind="Internal") for k in range(8)]
    else:
        yA_d = yT_d

    with contextlib.ExitStack() as es:
        S = Sch(nc, es)
        S.maxops = maxops
        P, V, A, G = S.P, S.V, S.A, S.G

        def sb(stack, name, shape, dtype, ntok=1):
            return Tile(stack.enter_context(nc.sbuf_tensor("sb_" + name, shape, dtype)), ntok)

        PB = [Tile(es.enter_context(nc.psum_tensor("pb%d" % i, [128, 512], F32))) for i in range(7)]
        PSB = Tile(es.enter_context(nc.psum_tensor("psb", [128, 1024], BF16)), 2)

        CST = sb(es, "cst", [128, C_END], F32)
        IDB = sb(es, "idb", [128, 128], BF16)
        S.dma(CST[:], cst_d[:, :], (), [CST])
        S.cp(V, IDB[:], CST[:, C_ID:C_ID + 128], [CST], [IDB])
        IDF = CST[:, C_ID:C_ID + 128]
        S.barrier()
        KC = ()

        with contextlib.ExitStack() as ph:
            NWR = sb(ph, "nwr", [128, D], F32)
            S.dma(NWR[:], nw_d[:, :], (), [NWR])
            XT = [sb(ph, "xt%d" % i, [128, D], F32) for i in range(2)]
            UNB = [sb(ph, "unb%d" % i, [128, D], BF16) for i in range(2)]
            UTS = [sb(ph, "uts%d" % i, [128, 16, 128], BF16) for i in range(2)]
            ST1 = [sb(ph, "st1_%d" % i, [128, 4], F32) for i in range(2)]
            for i in range(NT):
                xt, unb, uts, st1 = XT[i % 2], UNB[i % 2], UTS[i % 2], ST1[i % 2]
                if i == 0:
                    S.ms(G, xt[:], 0.0, [xt])
                    S.dma(xt[112:128, :], meta_d[:, :], (), [xt])
                else:
                    S.dma(xt[:], x_d[(i - 1) * 128:i * 128, :], (), [xt])
                S.ms(V, st1[:, 0:1], 0.0, [st1])
                S.act(unb[:], xt[:], AF.Square, [xt], [unb, st1], accum_out=st1[:, 0:1])
                S.act(st1[:, 1:2], st1[:, 0:1], AF.Sqrt, [st1], [st1], bias=1e-6, scale=1.0 / D)
                S.op(V, lambda: nc.vector.reciprocal(out=st1[:, 2:3], in_=st1[:, 1:2]), [st1], [st1])
                S.stt(V, unb[:], xt[:], st1[:, 2:3], NWR[:], ALU.mult, ALU.mult, [xt, st1, NWR], [unb])
                for r in range(2):
                    for j in range(8):
                        kc = r * 8 + j
                        S.tr(PSB[:, j * 128:(j + 1) * 128], unb[:, kc * 128:(kc + 1) * 128], IDB[:], [unb], [PSB])
                    S.cp(A if r == 0 else V, uts[:, r * 8:(r + 1) * 8, :],
                         PSB[:, :].rearrange("p (a b) -> p a b", a=8), [PSB], [uts])
                S.dma(uT_d[:, :, i * 128:(i + 1) * 128].rearrange("k p t -> p k t"), uts[:], [uts], ())
        S.barrier()

        passes = [(0, 17), (17, NCH)] if stop >= 1 else []
        with contextlib.ExitStack() as ph:
            WBF = sb(ph, "wbf", [128, 16, 17 * 128], BF16)
            WST = [sb(ph, "wst%d" % i, [128, 17 * 128], F32) for i in range(2)]
            UTT = [sb(ph, "utt%d" % i, [128, 16, 512], BF16) for i in range(2)]
            OST = [sb(ph, "ost%d" % i, [128, 512], F32) for i in range(4)]
            cnt = 0
            ocnt = 0
            for hf in range(NH):
                for (ca, cb) in passes:
                    c0 = ca * 128
                    ncols = min(cb * 128, NCOL) - c0
                    for kc in range(16):
                        wst = WST[kc % 2]
                        S.dma(wst[:, 0:ncols], win_d[hf, kc * 128:(kc + 1) * 128, c0:c0 + ncols], (), [wst])
                        S.cp(V if kc % 2 == 0 else A, WBF[:, kc, 0:ncols], wst[:, 0:ncols], [wst], [WBF])
                    for t0 in range(0, T, 512):
                        n = min(512, T - t0)
                        utt = UTT[cnt % 2]
                        cnt += 1
                        S.dma(utt[:, :, 0:n], uT_d[:, :, t0:t0 + n].rearrange("k p t -> p k t"), (), [utt])
                        for cc in range(ca, cb):
                            m = 128 if cc < NCH - 1 else LASTM
                            pb = PB[ocnt % 4]
                            ost = OST[ocnt % 4]
                            for kc in range(16):
                                S.mm(pb[0:m, 0:n], WBF[:, kc, (cc - ca) * 128:(cc - ca) * 128 + m], utt[:, kc, 0:n],
                                     kc == 0, kc == 15, [WBF, utt], [pb])
                            S.cp(A if ocnt % 2 == 0 else V, ost[0:m, 0:n], pb[0:m, 0:n], [pb], [ost])
                            S.dma(pT_d[hf, cc * 128:cc * 128 + m, t0:t0 + n], ost[0:m, 0:n], [ost], ())
                            ocnt += 1
        S.barrier()

        with contextlib.ExitStack() as ph:
            def t_(name, shape, dtype=F32, ntok=1):
                return sb(ph, name, shape, dtype, ntok)

            PV = t_("pv", [128, NPV])
            OMM = t_("omm", [128, 13])
            MUF = t_("muf", [128, 13, 128])
            KKF = t_("kkf", [128, 4, 128])
            KAF = t_("kaf", [128, 4, 128])
            OMKAF = t_("omkaf", [128, 4, 128])
            OMKA = t_("omka", [128, 4])
            NA = t_("na", [64, 1])
            W2b = t_("w2b", [128, 2, 512], BF16)
            ONE4 = t_("one4", [128, 128])
            STG = [t_("stg%d" % i, [128, NCH, 131]) for i in range(2)]
            TMP = t_("tmp", [128, 13, 128], F32, 13)
            W2f = TMP
            XS = t_("xs", [128, 13, 128], F32, 13)
            TL = t_("tl", [128, 128], BF16)
            SG = t_("sg", [128, 4, 128])
            AA = t_("aa", [128, 4, 128])
            CUM = t_("cum", [128, 4, 128])
            CME = t_("cme", [128, 4, 128])
            BMs = [t_("bm%d" % i, [128, 16]) for i in range(2)]
            PIN = t_("pin", [128, 4, 128])
            PEX = t_("pex", [128, 4, 128])
            IP = t_("ip", [128, 4, 128])
            KRAW = t_("kraw", [128, 4, 128])
            SQ = t_("sq", [128, 4, 128])
            RN = t_("rn", [128, 4, 128])
            KK = t_("kk", [128, 4, 128])
            KF = t_("kf", [128, 4, 128])
            KP = t_("kp", [128, 4, 128])
            BB = t_("bb", [128, 4, 128])
            FMQs = [t_("fmq%d" % i, [128, 4, 2, 128], BF16) for i in range(2)]
            KDF = t_("kdf", [128, 4, 128])
            BDF = t_("bdf", [128, 4, 128])
            KDZ = t_("kdz", [128, 4, 2, 128], BF16)
            BDZ = t_("bdz", [128, 4, 2, 128], BF16)
            H2Z = t_("h2z", [128, 4, 2, 64], BF16)
            KEB = t_("keb", [128, 4, 128], BF16)
            BEB_ = t_("beb", [128, 4, 128], BF16)
            VB_ = t_("vb", [128, 4, 128], BF16)
            KETs = [t_("ket%d" % i, [128, 4, 128], BF16) for i in range(2)]
            BETs = [t_("bet%d" % i, [128, 4, 128], BF16) for i in range(2)]
            VTs = [t_("vt%d" % i, [128, 4, 128], BF16) for i in range(2)]
            RKR = KRAW
            BONs = [t_("bon%d" % i, [128, 4, 128]) for i in range(2)]
            SGTs = [t_("sgt%d" % i, [128, 4, 128]) for i in range(2)]
            XB = [t_("xb%d" % i, [128, 4, 128], BF16) for i in range(2)]
            XTB = [t_("xtb%d" % i, [128, 4, 128], BF16) for i in range(2)]
            RT = [t_("rt%d" % i, [128, 4, 128], BF16) for i in range(2)]
            XTARBs = [t_("xtarb%d" % i, [128, 4, 2, 128], BF16) for i in range(2)]
            AKKARKs = [t_("akkark%d" % i, [128, 4, 2, 128], BF16) for i in range(2)]
            XBD = t_("xbd", [128, 4, 128], BF16)
            XTBD = t_("xtbd", [128, 4, 128], BF16)
            RTD = t_("rtd", [128, 4, 128], BF16)
            H2F = t_("h2f", [128, 4, 64])
            ZBs = [t_("zb%d" % i, [128, 4, 64], BF16) for i in range(2)]
            NSAs = [t_("nsa%d" % i, [128, 4, 64], BF16) for i in range(2)]
            YSB = t_("ysb", [128, 8, 64])
            YSQ = t_("ysq", [128, 8, 64])
            GST = t_("gst", [128, 48])
            YN = YSQ
            Y2 = t_("y2", [128, 4, 128])
            YOB = t_("yob", [128, 8, 128], BF16)
            CV = TMP
            TMPG = t_("tmpg", [128, 128])
            QKV = XS
            SQD = t_("sqd", [128, 8, 128])
            RND = SQD
            QTB = t_("qtb", [128, 4, 128], BF16)
            KTB = t_("ktb", [128, 4, 128], BF16)
            VTB = t_("vtb", [128, 4, 128], BF16)
            GB = t_("gb", [128, 128])
            GE = t_("ge", [64, 128])
            TM = t_("tm", [128, 64])
            TMS = t_("tms", [128, 24])
            GCB = SG
            BEBC = AA
            EGB = t_("egb", [128, 4, 128])
            DD = PEX
            D1 = IP
            D2 = KRAW
            EE = D1
            ET = D2
            XE = D1
            XTE = SQ
            AE = D2
            ATB = t_("atb", [128, 4, 128], BF16)
            XH = t_("xh", [128, 4, 128], BF16)
            XL = t_("xl", [128, 4, 128], BF16)
            RRT = t_("rrt", [128, 4, 128], BF16)
            RRN = t_("rrn", [128, 4, 128], BF16)
            XF = t_("xf", [128, 4, 128])
            VBD = t_("vbd", [128, 4, 128], BF16)
            KBG = t_("kbg", [128, 4, 128], BF16)
            KE = t_("ke", [128, 4, 128], BF16)
            QGT = t_("qgt", [128, 4, 128], BF16)
            USB = t_("usb", [128, 4, 128])
            WTB = t_("wtb", [128, 4, 128], BF16)
            VN = t_("vn", [128, 4, 128], BF16)
            SF = t_("sf", [128, 4, 128])
            SBF = t_("sbf", [128, 4, 128], BF16)
            OSQ = t_("osq", [128, 4, 128])
            ON = t_("on", [128, 4, 128])
            SZ = t_("sz", [128, 4, 128])

            MSLN4 = CST[:, C_MSLN4:C_MSLN4 + 512].rearrange("p (a b) -> p a b", a=4)
            MA = CST[:, C_MA:C_MA + 1024].rearrange("p (a b c) -> p a b c", a=4, b=2)
            MB = CST[:, C_MB:C_MB + 1024].rearrange("p (a b c) -> p a b c", a=4, b=2)
            BD1 = CST[:, C_BD1:C_BD1 + 128]
            ONES = CST[:, C_ONES:C_ONES + 128]
            NMSU4 = CST[:, C_NMSU4:C_NMSU4 + 512].rearrange("p (a b) -> p a b", a=4)
            MIU4 = CST[:, C_MIU4:C_MIU4 + 512].rearrange("p (a b) -> p a b", a=4)
            ID4 = t_("id4", [128, 4, 128], BF16)
            for a in range(4):
                S.cp(V, ID4[:, a, :], IDF, KC, [ID4])
            S.ms(V, ONE4[:], 1.0, [ONE4])
            S.ms(G, KDZ[:], 0.0, [KDZ])
            S.ms(G, BDZ[:], 0.0, [BDZ])
            S.ms(G, H2Z[:], 0.0, [H2Z])
            S.ms(G, GB[:], 0.0, [GB])
            for i in range(2):
                S.ms(G, STG[i][:, 33, :], 0.0, [STG[i]])

            def b4(pb):
                return pb[:, :].rearrange("p (a b) -> p a b", a=4)

            def doubling(xb, xtb, rt, bS, bY, bYT, nlev=7):
                S.tt(V, rt[:], xtb[:], ID4[:], ALU.add, [xtb, ID4], [rt])
                for lvl in range(nlev):
                    last = lvl == nlev - 1
                    if lvl >= 1:
                        for h in range(4):
                            S.mm(bS[:, h * 128:(h + 1) * 128], IDB[:], rt[:, h, :], True, False, [rt], [bS])
                            S.mm(bS[:, h * 128:(h + 1) * 128], xb[:, h, :], rt[:, h, :], False, True, [xb, rt], [bS])
                    if not last:
                        for h in range(4):
                            S.mm(bY[:, h * 128:(h + 1) * 128], xtb[:, h, :], xb[:, h, :], True, True, [xtb, xb], [bY])
                        if lvl < nlev - 2:
                            for h in range(4):
                                S.mm(bYT[:, h * 128:(h + 1) * 128], xb[:, h, :], xtb[:, h, :], True, True, [xtb, xb], [bYT])
                    S.cut()
                    if lvl >= 1:
                        S.cp(A if lvl % 2 == 0 else V, rt[:], b4(bS), [bS], [rt])
                    if not last:
                        S.cp(V if lvl % 2 == 0 else A, xb[:], b4(bY), [bY], [xb])
                        if lvl < nlev - 2:
                            S.cp(A, xtb[:], b4(bYT), [bYT], [xtb])
                    S.cut()

            for hf in range(NH if stop >= 2 else 0):
                S.dma(PV[:], pv_d[hf, :, :], (), [PV])
                S.dma(W2f[:, 0:8, :].rearrange("p (a b) c -> p a (b c)", a=2), w2a2_d[hf, :, :, :], (), [W2f])
                S.cp(V, W2b[:], W2f[:, 0:8, :].rearrange("p (a b) c -> p a (b c)", a=2), [W2f], [W2b])
                S.ts(V, OMM[:], PV[:, 0:13], -1.0, ALU.mult, [PV], [OMM], 1.0, ALU.add)
                S.ts(V, OMKA[:], PV[:, 25:29], -1.0, ALU.mult, [PV], [OMKA], 1.0, ALU.add)
                S.act(NA[:], PV[0:64, 90:91], AF.Exp, [PV], [NA])
                S.ts(V, NA[:], NA[:], -1.0, ALU.mult, [NA], [NA])
                for cc in range(13):
                    S.ts(V, MUF[:, cc, :], ONE4[:], PV[:, cc:cc + 1], ALU.mult, [ONE4, PV], [MUF])
                for g in range(4):
                    S.ts(V, KKF[:, g, :], ONE4[:], PV[:, 21 + g:22 + g], ALU.mult, [ONE4, PV], [KKF])
                    S.ts(V, KAF[:, g, :], ONE4[:], PV[:, 25 + g:26 + g], ALU.mult, [ONE4, PV], [KAF])
                    S.ts(V, OMKAF[:, g, :], ONE4[:], OMKA[:, g:g + 1], ALU.mult, [ONE4, OMKA], [OMKAF])
                S.ms(V, H2F[:], 0.0, [H2F])
                S.ms(V, SF[:], 0.0, [SF])
                S.ms(G, SBF[:], 0.0, [SBF])
                MU = lambda cc: PV[:, cc:cc + 1]
                W0 = lambda g: PV[:, 13 + g:14 + g]
                A0 = lambda g: PV[:, 17 + g:18 + g]
                K_K = lambda g: PV[:, 21 + g:22 + g]
                K_A = lambda g: PV[:, 25 + g:26 + g]
                R_K = lambda g: PV[:, 29 + g:30 + g]
                GNW = lambda g: PV[:, 33 + g:34 + g]
                GNB = lambda g: PV[:, 37 + g:38 + g]
                CWT = lambda cj, tap: PV[:, 41 + 4 * cj + tap:42 + 4 * cj + tap]
                DNW = PV[:, 89:90]

                def load_chunk(c):
                    stg = STG[c % 2]
                    t0 = c * 128
                    src = pT_d[hf].rearrange("(cc p) t -> p cc t", p=128)
                    if c == 0:
                        S.ms(G, stg[:, :, 0:3], 0.0, [stg])
                        lo, dst0 = 0, 3
                    else:
                        lo, dst0 = t0 - 3, 0
                    for (a, b) in ((0, 9), (9, 17), (17, 25), (25, 33)):
                        S.dma(stg[:, a:b, dst0:131], src[:, a:b, lo:t0 + 128], (), [stg])
                    S.dma(stg[0:LASTM, 33, dst0:131], pT_d[hf, 33 * 128:33 * 128 + LASTM, lo:t0 + 128], (), [stg])

                def rw_pre(ci):
                    stg = STG[ci % 2]
                    BON = BONs[ci % 2]
                    SGT = SGTs[ci % 2]
                    BM, FMQ, KET, BET, VT = BMs[ci % 2], FMQs[ci % 2], KETs[ci % 2], BETs[ci % 2], VTs[ci % 2]
                    S.tt(G, TMP[:], stg[:, 0:13, 2:130], stg[:, 0:13, 3:131], ALU.subtract, [stg], [TMP])
                    S.tt(G, TMP[:], TMP[:], MUF[:], ALU.mult, [TMP, MUF], [TMP])
                    S.tt(V, XS[:], stg[:, 0:13, 3:131], TMP[:], ALU.add, [stg, TMP], [XS])
                    XR = XS[:, 0:4, :]
                    XK = XS[:, 4:8, :]
                    XV = XS[:, 8:12, :]
                    xr_t = [(XS, i) for i in range(0, 4)]
                    xk_t = [(XS, i) for i in range(4, 8)]
                    xv_t = [(XS, i) for i in range(8, 12)]
                    S.cut()
                    S.act(TL[0:64, :], XS[0:64, 12, :], AF.Tanh, [(XS, 12)], [TL])
                    S.cp(A, TL[64:128, :], XS[64:128, 12, :], [(XS, 12)], [TL])
                    for g in range(4):
                        S.mm(PB[6][:, g * 128:(g + 1) * 128], W2b[:, 0, g * 128:(g + 1) * 128], TL[:, :], True, True,
                             [W2b, TL], [PB[6]])
                    S.cut()
                    for g in range(4):
                        S.act(SG[:, g, :], PB[6][:, g * 128:(g + 1) * 128], AF.Sigmoid, [PB[6], PV], [SG], bias=W0(g))
                    for g in range(4):
                        S.mm(PB[6][:, g * 128:(g + 1) * 128], W2b[:, 1, g * 128:(g + 1) * 128], TL[:, :], True, True,
                             [W2b, TL], [PB[6]])
                    for g in range(4):
                        S.act(AA[:, g, :], PB[6][:, g * 128:(g + 1) * 128], AF.Sigmoid, [PB[6], PV], [AA], bias=A0(g))
                    S.cut()
                    for g in range(4):
                        S.op(V, lambda g=g: nc.vector.tensor_tensor_scan(out=CUM[:, g, :], data0=ONE4[:], data1=SG[:, g, :],
                                                                        initial=0.0, op0=ALU.mult, op1=ALU.add),
                             [SG, ONE4], [CUM])
                    S.ts(V, BM[:, 0:4], CUM[:, :, 63], CW, ALU.mult, [CUM], [BM])
                    S.ts(V, BM[:, 4:8], CUM[:, :, 63], -CW, ALU.mult, [CUM], [BM])
                    S.tt(V, CME[:], CUM[:], SG[:], ALU.subtract, [CUM, SG], [CME])
                    S.cut()
                    for g in range(4):
                        S.act(PIN[:, g, :], CUM[:, g, :], AF.Exp, [CUM, BM], [PIN], bias=BM[:, g:g + 1], scale=-CW)
                    for g in range(4):
                        S.act(IP[:, g, :], CUM[:, g, :], AF.Exp, [CUM, BM], [IP], bias=BM[:, 4 + g:5 + g], scale=CW)
                    S.cut()
                    for g in range(4):
                        S.act(PEX[:, g, :], CME[:, g, :], AF.Exp, [CME, BM], [PEX], bias=BM[:, g:g + 1], scale=-CW)
                    S.act(BM[:, 8:12], BM[:, 0:4], AF.Exp, [BM], [BM], scale=-1.0)
                    S.tt(V, BM[:, 12:16], BM[:, 8:12], PIN[:, :, 127], ALU.mult, [BM, PIN], [BM])
                    S.cut()
                    S.tt(G, KRAW[:], XK, KKF[:], ALU.mult, xk_t + [KKF], [KRAW])
                    S.tt(G, SQ[:], KRAW[:], KRAW[:], ALU.mult, [KRAW], [SQ])
                    S.mm(PB[6][:, :], BD1, SQ[:, :, :].rearrange("p a b -> p (a b)"), True, True, [SQ], [PB[6]])
                    S.act(RN[:], b4(PB[6]), AF.Ln, [PB[6]], [RN], bias=1e-6)
                    S.act(RN[:], RN[:], AF.Exp, [RN], [RN], scale=-0.5)
                    S.cut()
                    S.tt(V, KK[:], KRAW[:], RN[:], ALU.mult, [KRAW, RN], [KK])
                    S.tt(G, KF[:], AA[:], KAF[:], ALU.mult, [AA, KAF], [KF])
                    S.tt(G, KF[:], KF[:], OMKAF[:], ALU.add, [KF, OMKAF], [KF])
                    S.tt(V, KP[:], XK, KF[:], ALU.mult, xk_t + [KF], [KP])
                    S.tt(G, BB[:], KK[:], AA[:], ALU.mult, [KK, AA], [BB])
                    S.cut()
                    S.tt(V, FMQ[:, :, 0, :], KK[:], PEX[:], ALU.mult, [KK, PEX], [FMQ])
                    S.tt(V, FMQ[:, :, 1, :], XR, PIN[:], ALU.mult, xr_t + [PIN], [FMQ])
                    S.tt(V, KDF[:], KP[:], IP[:], ALU.mult, [KP, IP], [KDF])
                    S.tt(G, BDF[:], BB[:], IP[:], ALU.mult, [BB, IP], [BDF])
                    S.cut()
                    for hl in range(2):
                        ps = slice(hl * 64, hl * 64 + 64)
                        S.cp(A, KDZ[ps, :, hl, :], KDF[ps, :, :], [KDF], [KDZ])
                        S.cp(A, BDZ[ps, :, hl, :], BDF[ps, :, :], [BDF], [BDZ])
                    for g in range(4):
                        S.act(KEB[:, g, :], KDF[:, g, :], AF.Copy, [KDF, PIN], [KEB], scale=PIN[:, g, 127:128])
                    for g in range(4):
                        S.act(BEB_[:, g, :], BDF[:, g, :], AF.Copy, [BDF, PIN], [BEB_], scale=PIN[:, g, 127:128])
                    S.cut()
                    S.cp(A, VB_[:], XV, xv_t, [VB_])
                    for (src, dst) in ((KEB, KET), (BEB_, BET), (VB_, VT)):
                        for g in range(4):
                            S.tr(PSB[:, 512 + g * 128:512 + (g + 1) * 128], src[:, g, :], IDB[:], [src], [(PSB, 1)])
                        S.cp(A, dst[:], PSB[:, 512:1024].rearrange("p (a b) -> p a b", a=4), [(PSB, 1)], [dst])
                        S.cut()
                    S.cut()
                    for g in range(4):
                        S.stt(V, RKR[:, g, :], XR[:, g, :], R_K(g), KP[:, g, :], ALU.mult, ALU.mult, xr_t + [PV, KP], [RKR])
                    S.mm(PB[6][:, :], BD1, RKR[:, :, :].rearrange("p a b -> p (a b)"), True, True, [RKR], [PB[6]])
                    S.tt(V, BON[:], b4(PB[6]), XV, ALU.mult, [PB[6]] + xv_t, [BON])
                    S.act(SGT[:], stg[:, 13:17, 3:131], AF.Silu, [stg], [SGT])


                load_chunk(0)
                rw_pre(0)
                for c in range(NT):
                    stg = STG[c % 2]
                    BON = BONs[c % 2]
                    SGT = SGTs[c % 2]
                    BM, FMQ, KET, BET, VT = BMs[c % 2], FMQs[c % 2], KETs[c % 2], BETs[c % 2], VTs[c % 2]
                    if c + 1 < NT:
                        load_chunk(c + 1)
                    cur = lambda cc: stg[:, cc, 3:131]
                    prv = lambda cc: stg[:, cc, 2:130]
                    L_amat, L_dbl, L_chain = [], [], []
                    for gi in range(2):
                        XTARB, AKKARK, ZB, NSA = XTARBs[gi], AKKARKs[gi], ZBs[gi], NSAs[gi]
                        bX, bXT, bAK = (PB[2], PB[3], PB[4]) if gi == 0 else (PB[6], PB[0], PB[1])
                        L_amat.append(S.begin())
                        for pl in range(2):
                            g = gi * 2 + pl
                            for hl in range(2):
                                ps = slice(hl * 64, hl * 64 + 64)
                                hh = pl * 2 + hl
                                S.mm(bX[:, hh * 128:(hh + 1) * 128], FMQ[:, g, 0, :], BDZ[:, g, hl, :], True, True,
                                     [FMQ, BDZ], [bX])
                        for hb in range(2):
                            for q in range(2):
                                hh = hb * 2 + q
                                g = gi * 2 + hh // 2
                                ps = slice((hh % 2) * 64, (hh % 2) * 64 + 64)
                                S.mm(bXT[:, q * 256:(q + 1) * 256], BDZ[:, g, hh % 2, :],
                                     FMQ[:, g, :, :].rearrange("p a b -> p (a b)"), True, True, [FMQ, BDZ], [bXT])
                                S.mm(bAK[:, q * 256:(q + 1) * 256], KDZ[:, g, hh % 2, :],
                                     FMQ[:, g, :, :].rearrange("p a b -> p (a b)"), True, True, [FMQ, KDZ], [bAK])
                            S.tt(V, XTARB[:, hb * 2:hb * 2 + 2, :, :],
                                 bXT[:, :].rearrange("p (a b c) -> p a b c", a=2, b=2), MA[:, 0:2, :, :], ALU.mult,
                                 [bXT], [XTARB])
                            S.tt(V, AKKARK[:, hb * 2:hb * 2 + 2, :, :],
                                 bAK[:, :].rearrange("p (a b c) -> p a b c", a=2, b=2), MB[:, 0:2, :, :], ALU.mult,
                                 [bAK], [AKKARK])
                        S.tt(V, XB[gi][:], b4(bX), MSLN4, ALU.mult, [bX], [XB[gi]])
                        S.cp(V, XTB[gi][:], XTARB[:, :, 0, :], [XTARB], [XTB[gi]])
                        S.cut()
                        L_dbl.append(S.begin())
                        if gi == 0:
                            doubling(XB[gi], XTB[gi], RT[gi], PB[4], PB[2], PB[3])
                        else:
                            doubling(XB[gi], XTB[gi], RT[gi], PB[6], PB[0], PB[1])
                        L_chain.append(S.begin())
                        for pl in range(2):
                            g = gi * 2 + pl
                            for hl in range(2):
                                ps = slice(hl * 64, hl * 64 + 64)
                                S.ts(V, H2Z[ps, g, hl, :], H2F[ps, g, :], BM[ps, 8 + g:9 + g], ALU.mult, [H2F, BM], [H2Z])
                        for hh in range(4):
                            g = gi * 2 + hh // 2
                            hl = hh % 2
                            ps = slice(hl * 64, hl * 64 + 64)
                            S.mm(PB[0][:, hh * 64:(hh + 1) * 64], FMQ[:, g, 0, :], H2Z[:, g, hl, :], True, False,
                                 [FMQ, H2Z], [PB[0]])
                            S.mm(PB[0][:, hh * 64:(hh + 1) * 64], AKKARK[:, hh, 0, :], VT[:, g, hl * 64:(hl + 1) * 64],
                                 False, True, [AKKARK, VT], [PB[0]])
                        S.cut()
                        S.cp(A, ZB[:], PB[0][:, 0:256].rearrange("p (a b) -> p a b", a=4), [PB[0]], [ZB])
                        for hh in range(4):
                            S.mm(PB[0][:, 256 + hh * 64:256 + (hh + 1) * 64], RT[gi][:, hh, :], ZB[:, hh, :], True, True,
                                 [RT[gi], ZB], [PB[0]])
                        S.cut()
                        S.ts(V, NSA[:], PB[0][:, 256:512].rearrange("p (a b) -> p a b", a=4), -1.0, ALU.mult,
                             [PB[0]], [NSA])
                        for hh in range(4):
                            g = gi * 2 + hh // 2
                            hl = hh % 2
                            ps = slice(hl * 64, hl * 64 + 64)
                            oy = PB[5][:, (gi * 4 + hh) * 64:(gi * 4 + hh + 1) * 64]
                            S.mm(oy, FMQ[:, g, 1, :], H2Z[:, g, hl, :], True, False, [FMQ, H2Z], [PB[5]])
                            S.mm(oy, AKKARK[:, hh, 1, :], VT[:, g, hl * 64:(hl + 1) * 64], False, False,
                                 [AKKARK, VT], [PB[5]])
                            S.mm(oy, XTARB[:, hh, 1, :], NSA[:, hh, :], False, True, [XTARB, NSA], [PB[5]])
                        for pl in range(2):
                            g = gi * 2 + pl
                            oh = PB[1][:, pl * 128:(pl + 1) * 128]
                            S.mm(oh, KET[:, g, :], VT[:, g, :], True, False, [KET, VT], [PB[1]])
                            S.mm(oh, BET[:, g, :], NSA[:, pl * 2:pl * 2 + 2, :].rearrange("p a b -> p (a b)"), False, True,
                                 [BET, NSA], [PB[1]])
                        S.cut()
                        for pl in range(2):
                            g = gi * 2 + pl
                            for hl in range(2):
                                ps = slice(hl * 64, hl * 64 + 64)
                                S.stt(V, H2F[ps, g, :], H2F[ps, g, :], BM[ps, 12 + g:13 + g],
                                      PB[1][ps, pl * 128 + hl * 64:pl * 128 + (hl + 1) * 64], ALU.mult, ALU.add,
                                      [H2F, BM, PB[1], H2Z], [H2F])
                        S.cut()
                    L_rwpost = S.begin()
                    S.cp(A, YSB[:], PB[5][:, :].rearrange("p (a b) -> p a b", a=8), [PB[5]], [YSB])
                    S.tt(G, YSQ[:], YSB[:], YSB[:], ALU.mult, [YSB], [YSQ])
                    S.op(V, lambda: nc.vector.tensor_reduce(out=GST[:, 0:8], in_=YSB[:], axis=AX.X, op=ALU.add), [YSB], [GST])
                    S.op(V, lambda: nc.vector.tensor_reduce(out=GST[:, 8:16], in_=YSQ[:], axis=AX.X, op=ALU.add), [YSQ], [GST])
                    S.ts(V, GST[:, 16:24], GST[:, 0:8], 1.0 / 64, ALU.mult, [GST], [GST])
                    S.tt(V, GST[:, 24:32], GST[:, 16:24], GST[:, 16:24], ALU.mult, [GST], [GST])
                    S.stt(V, GST[:, 32:40], GST[:, 8:16], 1.0 / 64, GST[:, 24:32], ALU.mult, ALU.subtract, [GST], [GST])
                    S.cut()
                    S.act(GST[:, 40:48], GST[:, 32:40], AF.Sqrt, [GST], [GST], bias=64e-5)
                    S.op(V, lambda: nc.vector.reciprocal(out=GST[:, 40:48], in_=GST[:, 40:48]), [GST], [GST])
                    for h in range(8):
                        S.ts(V, YN[:, h, :], YSB[:, h, :], GST[:, 16 + h:17 + h], ALU.subtract,
                             [YSB, GST], [YN], GST[:, 40 + h:41 + h], ALU.mult)
                    S.cut()
                    for g in range(4):
                        S.tr(PB[6][:, g * 128:(g + 1) * 128], YN[:, 2 * g:2 * g + 2, :].rearrange("p a b -> p (a b)"), IDF,
                             [YN], [PB[6]])
                    for g in range(4):
                        S.ts(V, Y2[:, g, :], PB[6][:, g * 128:(g + 1) * 128], GNW(g), ALU.mult, [PB[6], PV], [Y2],
                             GNB(g), ALU.add)
                    S.tt(V, Y2[:], Y2[:], BON[:], ALU.add, [Y2, BON], [Y2])
                    S.tt(V, YOB[:, 0:4, :], Y2[:], SGT[:], ALU.mult, [Y2, SGT], [YOB])

                    L_dnpre = S.begin()
                    for cj in range(12):
                        e = G if cj == 11 else V
                        cc = 17 + cj
                        S.ts(e, CV[:, cj, :], stg[:, cc, 0:128], CWT(cj, 0), ALU.mult, [stg, PV], [(CV, cj)])
                        for tap in range(1, 4):
                            if e is V:
                                S.stt(e, CV[:, cj, :], stg[:, cc, tap:tap + 128], CWT(cj, tap), CV[:, cj, :], ALU.mult, ALU.add,
                                      [stg, PV, (CV, cj)], [(CV, cj)])
                            else:
                                S.ts(G, TMPG[:], stg[:, cc, tap:tap + 128], CWT(cj, tap), ALU.mult, [stg, PV], [TMPG])
                                S.tt(G, CV[:, cj, :], CV[:, cj, :], TMPG[:], ALU.add, [TMPG, (CV, cj)], [(CV, cj)])
                        if cj % 2 == 1:
                            S.cut()
                    for j in range(3):
                        S.act(QKV[:, 4 * j:4 * j + 4, :], CV[:, 4 * j:4 * j + 4, :], AF.Silu,
                              [(CV, 4 * j + i) for i in range(4)], [QKV])
                    S.cut()
                    S.tt(G, SQD[:], QKV[:, 0:8, :], QKV[:, 0:8, :], ALU.mult, [QKV], [SQD])
                    S.mm(PB[5][:, :], ONES, SQD[:, 0:4, :].rearrange("p a b -> p (a b)"), True, True, [SQD], [PB[5]])
                    S.act(RND[:, 0:4, :], b4(PB[5]), AF.Ln, [PB[5]], [RND], bias=1e-6)
                    S.cut()
                    S.mm(PB[5][:, :], ONES, SQD[:, 4:8, :].rearrange("p a b -> p (a b)"), True, True, [SQD], [PB[5]])
                    S.act(RND[:, 4:8, :], b4(PB[5]), AF.Ln, [PB[5]], [RND], bias=1e-6)
                    S.cut()
                    S.act(RND[:], RND[:], AF.Exp, [RND], [RND], scale=-0.5)
                    S.cut()
                    S.tt(V, QKV[:, 0:8, :], QKV[:, 0:8, :], RND[:], ALU.mult, [QKV, RND], [QKV])
                    S.cp(A, KTB[:], QKV[:, 4:8, :], [QKV], [KTB])
                    S.cp(A, VTB[:], QKV[:, 8:12, :], [QKV], [VTB])
                    S.cut()
                    S.act(GE[0:32, :], stg[0:32, 33, 3:131], AF.Exp, [stg, PV], [GE], bias=PV[0:32, 91:92])
                    S.act(GE[0:32, :], GE[0:32, :], AF.Ln, [GE], [GE], bias=1.0)
                    S.ts(V, GE[0:32, :], GE[0:32, :], NA[0:32, 0:1], ALU.mult, [GE, NA], [GE])
                    S.op(V, lambda: nc.vector.tensor_tensor_scan(out=GB[0:32, :], data0=ONE4[0:32, :], data1=GE[0:32, :],
                                                                initial=0.0, op0=ALU.mult, op1=ALU.add), [GE, ONE4], [GB])
                    S.act(GB[32:64, :], stg[32:64, 33, 3:131], AF.Sigmoid, [stg], [GB])
                    S.tr(PB[5][:, 0:128], GB[:, :], IDF, [GB], [PB[5]])
                    S.cp(V, TM[:], PB[5][:, 0:64], [PB[5]], [TM])
                    S.cut()
                    for i in range(4):
                        S.mm(PB[5][:, i * 128:(i + 1) * 128], CST[:, C_SEL + i * 128:C_SEL + (i + 1) * 128], GB[:, :],
                             True, True, [GB], [PB[5]])
                    S.cp(A, GCB[:], b4(PB[5]), [PB[5]], [GCB])
                    S.cut()
                    for i in range(4):
                        S.mm(PB[5][:, i * 128:(i + 1) * 128], CST[:, C_SEL + (4 + i) * 128:C_SEL + (5 + i) * 128], GB[:, :],
                             True, True, [GB], [PB[5]])
                    S.cp(A, BEBC[:], b4(PB[5]), [PB[5]], [BEBC])
                    S.cut()
                    S.cut()
                    S.act(EGB[:], GCB[:], AF.Exp, [GCB], [EGB])
                    S.act(TMS[:, 0:4], TM[:, 0:4], AF.Exp, [TM], [TMS])
                    S.tt(V, TMS[:, 4:8], TMS[:, 0:4], TM[:, 32:36], ALU.mult, [TMS, TM], [TMS])
                    for h in range(4):
                        S.act(TMS[:, 8 + h:9 + h], TM[:, h:h + 1], AF.Exp, [TM, GCB], [TMS], bias=GCB[:, h, 127:128], scale=-1.0)
                    for h in range(4):
                        S.ts(V, DD[:, h, :], GCB[:, h, :], TM[:, h:h + 1], ALU.subtract, [GCB, TM], [DD])
                    S.cut()
                    S.ts(V, D1[:], DD[:], 0.0, ALU.max, [DD], [D1])
                    S.ts(V, D2[:], DD[:], 0.0, ALU.min, [DD], [D2])
                    S.act(EE[:], D1[:], AF.Exp, [D1], [EE], scale=-1.0)
                    S.act(ET[:], D2[:], AF.Exp, [D2], [ET])
                    for h in range(4):
                        S.stt(V, XE[:, h, :], EE[:, h, :], TM[:, 32 + h:33 + h], MSLN4[:, h, :], ALU.mult, ALU.mult,
                              [EE, TM], [XE])
                    S.cut()
                    S.tt(G, XTE[:], ET[:], BEBC[:], ALU.mult, [ET, BEBC], [XTE])
                    S.tt(G, XTE[:], XTE[:], NMSU4, ALU.mult, [XTE], [XTE])
                    S.tt(G, AE[:], ET[:], MIU4, ALU.mult, [ET], [AE])
                    S.cut()
                    S.act(QTB[:], QKV[:, 0:4, :], AF.Copy, [QKV], [QTB], scale=128.0 ** -0.5)
                    S.stt(V, QGT[:], QKV[:, 0:4, :], 128.0 ** -0.5, EGB[:], ALU.mult, ALU.mult, [QKV, EGB], [QGT])
                    L_dnamat = S.begin()
                    for h in range(4):
                        S.mm(PB[5][:, h * 128:(h + 1) * 128], KTB[:, h, :], KTB[:, h, :], True, True, [KTB], [PB[5]])
                    for h in range(4):
                        S.tr(PSB[:, h * 128:(h + 1) * 128], KTB[:, h, :], IDB[:], [KTB], [PSB])
                    for h in range(4):
                        S.tr(PSB[:, 512 + h * 128:512 + (h + 1) * 128], VTB[:, h, :], IDB[:], [VTB], [PSB])
                    S.cut()
                    S.tt(V, XF[:], b4(PB[5]), XE[:], ALU.mult, [PB[5], XE], [XF])
                    S.tt(V, XTBD[:], b4(PB[5]), XTE[:], ALU.mult, [PB[5], XTE], [XTBD])
                    S.cut()
                    for h in range(4):
                        S.mm(PB[5][:, h * 128:(h + 1) * 128], KTB[:, h, :], QTB[:, h, :], True, True, [KTB, QTB], [PB[5]])
                    S.cp(A, XH[:], XF[:], [XF], [XH])
                    S.tt(G, XL[:], XF[:], XH[:], ALU.subtract, [XF, XH], [XL])
                    S.cp(A, XBD[:], XH[:], [XH], [XBD])
                    S.cut()
                    S.tt(V, ATB[:], b4(PB[5]), AE[:], ALU.mult, [PB[5], AE], [ATB])
                    for h in range(4):
                        S.ts(V, KBG[:, h, :], PSB[:, h * 128:(h + 1) * 128], TMS[:, 4 + h:5 + h], ALU.mult, [PSB, TMS], [KBG])
                        S.ts(V, KE[:, h, :], PSB[:, h * 128:(h + 1) * 128], TMS[:, 8 + h:9 + h], ALU.mult, [PSB, TMS], [KE])
                        S.ts(V, VBD[:, h, :], PSB[:, 512 + h * 128:512 + (h + 1) * 128], TM[:, 32 + h:33 + h], ALU.mult,
                             [PSB, TM], [VBD])
                        if h % 2 == 1:
                            S.cut()
                    L_dndbl = S.begin()
                    doubling(XBD, XTBD, RTD, PB[4], PB[2], PB[3], nlev=6)
                    for h in range(4):
                        S.mm(PB[2][:, h * 128:(h + 1) * 128], XH[:, h, :], RTD[:, h, :], True, False, [XH, RTD], [PB[2]])
                        S.mm(PB[2][:, h * 128:(h + 1) * 128], XL[:, h, :], RTD[:, h, :], False, True, [XL, RTD], [PB[2]])
                    S.cut()
                    S.tt(V, XF[:], b4(PB[2]), RTD[:], ALU.subtract, [PB[2], RTD], [XF])
                    S.tt(V, RRT[:], XF[:], ID4[:], ALU.add, [XF], [RRT])
                    for h in range(4):
                        S.tr(PSB[:, h * 128:(h + 1) * 128], RRT[:, h, :], IDB[:], [RRT], [(PSB, 0)])
                    S.cut()
                    S.cp(A, RRN[:], PSB[:, 0:512].rearrange("p (a b) -> p a b", a=4), [(PSB, 0)], [RRN])
                    for h in range(4):
                        S.mm(PB[3][:, h * 128:(h + 1) * 128], RRN[:, h, :], RTD[:, h, :], True, True, [RRN, RTD], [PB[3]])
                    S.cut()
                    S.tt(V, RTD[:], b4(PB[3]), RTD[:], ALU.add, [PB[3], RTD], [RTD])
                    L_dntail = S.begin()
                    for h in range(4):
                        S.mm(PB[0][:, h * 128:(h + 1) * 128], RTD[:, h, :], VBD[:, h, :], True, True, [RTD, VBD], [PB[0]])
                    for h in range(4):
                        S.mm(PB[1][:, h * 128:(h + 1) * 128], KBG[:, h, :], RTD[:, h, :], True, True, [RTD, KBG], [PB[1]])
                    S.cut()
                    S.cp(A, USB[:], b4(PB[0]), [PB[0]], [USB])
                    S.cp(V, WTB[:], b4(PB[1]), [PB[1]], [WTB])
                    for h in range(4):
                        S.mm(PB[0][:, h * 128:(h + 1) * 128], WTB[:, h, :], SBF[:, h, :], True, True, [WTB, SBF], [PB[0]])
                    S.cut()
                    S.tt(V, VN[:], USB[:], b4(PB[0]), ALU.subtract, [USB, PB[0]], [VN])
                    for h in range(4):
                        S.mm(PB[1][:, h * 128:(h + 1) * 128], QGT[:, h, :], SBF[:, h, :], True, False, [QGT, SBF], [PB[1]])
                        S.mm(PB[1][:, h * 128:(h + 1) * 128], ATB[:, h, :], VN[:, h, :], False, True, [ATB, VN], [PB[1]])
                    for h in range(4):
                        S.mm(PB[0][:, h * 128:(h + 1) * 128], KE[:, h, :], VN[:, h, :], True, True, [KE, VN], [PB[0]])
                    for h in range(4):
                        S.stt(V, SF[:, h, :], SF[:, h, :], EGB[:, h, 127:128], PB[0][:, h * 128:(h + 1) * 128], ALU.mult, ALU.add,
                              [SF, EGB, PB[0]], [SF])
                    S.cp(A, SBF[:], SF[:], [SF], [SBF])
                    S.cut()
                    S.act(OSQ[:], b4(PB[1]), AF.Square, [PB[1]], [OSQ])
                    S.op(V, lambda: nc.vector.tensor_reduce(out=TMS[:, 12:16], in_=OSQ[:], axis=AX.X, op=ALU.add), [OSQ], [TMS])
                    S.act(TMS[:, 16:20], TMS[:, 12:16], AF.Sqrt, [TMS], [TMS], bias=1e-6, scale=1.0 / 128)
                    S.op(V, lambda: nc.vector.reciprocal(out=TMS[:, 20:24], in_=TMS[:, 16:20]), [TMS], [TMS])
                    for h in range(4):
                        S.act(ON[:, h, :], PB[1][:, h * 128:(h + 1) * 128], AF.Copy, [PB[1], TMS], [ON], scale=TMS[:, 20 + h:21 + h])
                    for h in range(4):
                        S.tr(PB[2][:, h * 128:(h + 1) * 128], ON[:, h, :], IDF, [ON], [PB[2]])
                    S.act(SZ[:], stg[:, 29:33, 3:131], AF.Silu, [stg], [SZ])
                    S.cut()
                    S.stt(V, YOB[:, 4:8, :], b4(PB[2]), DNW, SZ[:], ALU.mult, ALU.mult, [PB[2], PV, SZ], [YOB])
                    if c + 1 < NT:
                        L_pre = S.begin()
                        rw_pre(c + 1)
                    else:
                        L_pre = []
                    S.end()
                    S.play(L_amat[0] + L_dbl[0], L_amat[1] + L_dbl[1], L_dnpre + L_dnamat)
                    S.play(L_chain[0] + L_chain[1], L_dndbl, L_pre)
                    S.play(L_rwpost, L_dntail)
                    if coll:
                        for k in range(8):
                            S.dma(yTk[k].ap()[:, c * 128:(c + 1) * 128], YOB[:, k, :], [YOB], ())
                    else:
                        S.dma(yT_d[hf * 8:(hf + 1) * 8, :, c * 128:(c + 1) * 128].rearrange("k p t -> p k t"), YOB[:], [YOB], ())
        S.barrier()
        CCT = Tok()

        NK = NKH * 8
        with contextlib.ExitStack() as ph:
            WOB = sb(ph, "wob", [128, NK, D], BF16)
            WOS = [sb(ph, "wos%d" % i, [128, D], F32) for i in range(2)]
            FNW = sb(ph, "fnw", [128, D], F32)
            YT = [sb(ph, "yt%d" % i, [128, NK, 512 if coll else 128], BF16) for i in range(2)]
            XR3 = [sb(ph, "xr3_%d" % i, [128, D], F32) for i in range(2)]
            HS = sb(ph, "hs", [128, D], F32)
            JK = sb(ph, "jk", [128, D], BF16)
            OT = [sb(ph, "ot%d" % i, [128, D], F32) for i in range(2)]
            ST3 = [sb(ph, "st3_%d" % i, [128, 4], F32) for i in range(2)]
            S.dma(FNW[:], fnw_d[:, :], (), [FNW])
            for kc in range(NK if stop >= 3 else 0):
                wos = WOS[kc % 2]
                S.dma(wos[:], wout_d[kc * 128:(kc + 1) * 128, :], (), [wos])
                S.cp(V if kc % 2 == 0 else A, WOB[:, kc, :], wos[:], [wos], [WOB])
            if coll and stop >= 3:
                ccsem = es.enter_context(nc.semaphore("s_cc"))
                with nc.Block() as block:
                    @block.gpsimd
                    def _(g):
                        for k in range(8):
                            g.collective_compute("AllGather", ALU.bypass, replica_groups=[[0, 1], [2, 3], [4, 5], [6, 7]],
                                                 ins=[yTk[k].ap().opt()], outs=[yAk[k].ap().opt()]).then_inc(ccsem)
                            g.wait_ge(ccsem, k + 1)
                CCD = sb(ph, "ccd", [128, 8], F32)
                S.ms(G, CCD[:], 0.0, [CCD, CCT])

            for i in range(1, NT if stop >= 3 else 0):
                yt, xr, ot, st3 = YT[i % 2], XR3[i % 2], OT[i % 2], ST3[i % 2]
                if coll:
                    st_i = (i - 1) // 4
                    yt = YT[st_i % 2]
                    if (i - 1) % 4 == 0:
                        n3 = min(512, T - i * 128)
                        for k in range(8):
                            for r in range(2):
                                S.dma(yt[:, r * 8 + k, 0:n3], yAk[k].ap()[r * 128:(r + 1) * 128, i * 128:i * 128 + n3], [CCT], [yt])
                    yoff = ((i - 1) % 4) * 128
                else:
                    yoff = 0
                    S.dma(yt[:], yA_d[:, :, i * 128:(i + 1) * 128].rearrange("k p t -> p k t"), [CCT], [yt])
                S.dma(xr[:], x_d[(i - 1) * 128:i * 128, :], (), [xr])
                for nb in range(4):
                    for kc in range(NK):
                        S.mm(PB[nb][:, :], yt[:, kc, yoff:yoff + 128], WOB[:, kc, nb * 512:(nb + 1) * 512], kc == 0, kc == NK - 1,
                             [yt, WOB], [PB[nb]])
                for nb in range(4):
                    S.tt(V, HS[:, nb * 512:(nb + 1) * 512], PB[nb][:, :], xr[:, nb * 512:(nb + 1) * 512], ALU.add,
                         [PB[nb], xr], [HS])
                S.ms(V, st3[:, 0:1], 0.0, [st3])
                S.act(JK[:], HS[:], AF.Square, [HS], [JK, st3], accum_out=st3[:, 0:1])
                S.act(st3[:, 1:2], st3[:, 0:1], AF.Sqrt, [st3], [st3], bias=1e-6, scale=1.0 / D)
                S.op(V, lambda: nc.vector.reciprocal(out=st3[:, 2:3], in_=st3[:, 1:2]), [st3], [st3])
                S.stt(V, ot[:], HS[:], st3[:, 2:3], FNW[:], ALU.mult, ALU.mult, [HS, st3, FNW], [ot])
                S.dma(out_d[(i - 1) * 128:i * 128, :], ot[:], [ot], ())
        S.finish()
        print('MARKS', S.marks[:40], 'total', S.nops, [e.n for e in S.engs])
    return nc


def _consts():
    c = np.zeros((128, C_END), np.float32)
    p = np.arange(128)[:, None]
    f = np.arange(128)[None, :]
    ident = (p == f).astype(np.float32)
    msl_neg = -(p > f).astype(np.float32)
    msu = (f > p).astype(np.float32)
    miu = (f >= p).astype(np.float32)
    c[:, C_ID:C_ID + 128] = ident
    c[:, C_MSLN4:C_MSLN4 + 512] = np.tile(msl_neg, (1, 4))
    c[:, C_MA:C_MA + 1024] = np.tile(np.concatenate([-msu, miu], 1), (1, 4))
    c[:, C_MB:C_MB + 1024] = np.tile(np.concatenate([msu, miu], 1), (1, 4))
    c[:, C_BD1:C_BD1 + 128] = ((p // 64) == (f // 64)).astype(np.float32)
    c[:, C_ONES:C_ONES + 128] = 1.0
    rows = [0, 1, 2, 3, 32, 33, 34, 35]
    for i, r in enumerate(rows):
        c[r, C_SEL + i * 128:C_SEL + (i + 1) * 128] = 1.0
    c[:, C_NMSU4:C_NMSU4 + 512] = np.tile(-msu, (1, 4))
    c[:, C_MIU4:C_MIU4 + 512] = np.tile(miu, (1, 4))
    return c


def _half_layout(hh, w_in, mu, w0, w2, a0, a2, k_k, k_a, r_k, gn_w, gn_b, conv_w, A_log, dt_bias, dn_norm_w, w_out):
    o = 512 * hh
    ar = np.arange(512)
    cols = np.concatenate([o + ar, 1024 + o + ar, 2048 + o + ar, 3072 + np.arange(128), 3200 + o + ar,
                           4224 + o + ar, 5248 + o + ar, 6272 + o + ar, 7312 + o + ar])
    wl = np.zeros((D, NCOL), np.float32)
    wl[:, :33 * 128] = w_in[:, cols]
    wl[:, 33 * 128:33 * 128 + 4] = w_in[:, 7304 + 4 * hh:7304 + 4 * hh + 4]
    wl[:, 33 * 128 + 32:33 * 128 + 36] = w_in[:, 7296 + 4 * hh:7296 + 4 * hh + 4]
    pv = np.zeros((128, NPV), np.float32)
    pv[:, 0:13] = mu[cols[:13 * 128]].reshape(13, 128).T
    loc = lambda v: v[o:o + 512].reshape(4, 128).T
    pv[:, 13:17] = loc(w0)
    pv[:, 17:21] = loc(a0)
    pv[:, 21:25] = loc(k_k)
    pv[:, 25:29] = loc(k_a)
    pv[:, 29:33] = loc(r_k)
    pv[:, 33:37] = loc(gn_w)
    pv[:, 37:41] = loc(gn_b)
    for cj in range(12):
        base = (cj // 4) * 1024 + o + (cj % 4) * 128
        pv[:, 41 + 4 * cj:45 + 4 * cj] = conv_w[:, base:base + 128].T
    pv[:, 89] = dn_norm_w
    pv[0:4, 90] = A_log[4 * hh:4 * hh + 4]
    pv[0:4, 91] = dt_bias[4 * hh:4 * hh + 4]
    w2a2 = np.zeros((128, 2, 512), np.float32)
    w2a2[0:64, 0] = w2[:, o:o + 512]
    w2a2[64:128, 1] = a2[:, o:o + 512]
    wo = np.concatenate([w_out[o:o + 512], w_out[1024 + o:1024 + o + 512]], 0)
    return wl, pv, np.ascontiguousarray(w2a2), np.ascontiguousarray(wo)


_NC_CACHE = {}


def kernel(x, meta_tokens, norm_w, w_in, rw_shift_mu, rw_w0, rw_w2, rw_a0, rw_a2, rw_k_k, rw_k_a, rw_r_k,
           rw_gn_w, rw_gn_b, dn_conv_w, dn_A_log, dn_dt_bias, dn_norm_w, w_out, final_norm_w, _nt=33, _cores=8, _stop=9, _nh=2, _maxops=10 ** 9, _dbg=False, _b0=0):
    f = lambda a: np.asarray(a, np.float32)
    x = f(x)
    NT = _nt
    coll = (_cores == 8) and not _dbg
    NH = 1 if coll else _nh
    halves = [_half_layout(hh, f(w_in)[0], f(rw_shift_mu)[0], f(rw_w0)[0], f(rw_w2)[0], f(rw_a0)[0], f(rw_a2)[0],
                           f(rw_k_k)[0], f(rw_k_a)[0], f(rw_r_k)[0], f(rw_gn_w)[0], f(rw_gn_b)[0], f(dn_conv_w)[0],
                           f(dn_A_log)[0], f(dn_dt_bias)[0], f(dn_norm_w)[0], f(w_out)[0]) for hh in range(2)]
    shared = {
        "meta": f(meta_tokens),
        "normw": np.ascontiguousarray(np.broadcast_to(f(norm_w)[0][None], (128, D))),
        "fnormw": np.ascontiguousarray(np.broadcast_to(f(final_norm_w)[None], (128, D))),
        "consts": _consts(),
    }
    sel = (lambda hh: [hh]) if coll else (lambda hh: list(range(NH)))
    key = (NT, NH, _stop, _maxops, _dbg, coll)
    if key not in _NC_CACHE:
        _NC_CACHE[key] = build(NT, NH, _stop, _maxops, _dbg, coll)
    nc = _NC_CACHE[key]
    NX = (NT - 1) * 128
    wo_halves = list(range(2)) if coll else list(range(NH))
    w_out_l = np.concatenate([halves[h][3] for h in wo_halves], 0)
    in_maps = []
    for c in range(_cores):
        b = (_b0 + c // 2) % x.shape[0]
        hs = sel(c % 2)
        m = dict(shared)
        m["x"] = np.ascontiguousarray(x[b, :NX])
        m["w_in_l"] = np.stack([halves[h][0] for h in hs])
        m["pv"] = np.stack([halves[h][1] for h in hs])
        m["w2a2"] = np.stack([halves[h][2] for h in hs])
        m["w_out_l"] = w_out_l
        in_maps.append(m)
    res = run_bass_kernel_spmd(nc, in_maps, core_ids=list(range(_cores)))
    if _dbg:
        return res.results[0]["out"][None], res.results[0]["yT_s"]
    outs = [res.results[2 * b]["out"] for b in range(x.shape[0])] if _cores == 8 else [res.results[0]["out"]]
    return np.stack(outs, 0).astype(np.float32)
```
